# Optimizing a Trainium2 kernel written in Bass

```python
import math
import jax, jax.numpy as jnp
from jax import lax
import numpy as np

D_MODEL = 1024
BATCH = 32
SEQ = 2048
DEPTH = 2
DEC_BATCH = 8
DEC_SEQ = 4096
PAST_LEN = 128

N_META = 16
NORM_EPS = 1e-6
N_BRANCH = 4
BRANCH_W = 512
FNET_GROUPS = 4
FNET_GW = BRANCH_W // FNET_GROUPS
SSM_HEAD_DIM = 64
SSM_HEADS = BRANCH_W // SSM_HEAD_DIM
SSM_GROUPS = 2
SSM_HPG = SSM_HEADS // SSM_GROUPS
SSM_STATE = 128
SSM_CHUNK = 128
SSM_CONV = 3
SSM_XBC = BRANCH_W + 2 * SSM_GROUPS * SSM_STATE
HYENA_ORDER = 2
HYENA_EMB = 33
HYENA_FFN = 64
HYENA_CONV = 3
HYENA_TARGET = 1e-2
HYENA_FAST = 0.3
HYENA_SLOW = 1.5
SC_CONV = 3
D_FF = 2816
FFN_CONV = 3

COL_FNET = BRANCH_W
COL_SSM = BRANCH_W + SSM_XBC + 2 * SSM_HEADS
COL_HYENA = (HYENA_ORDER + 1) * BRANCH_W
COL_SC = 3 * BRANCH_W
COL_GATE = N_BRANCH * D_MODEL
D_IN_PROJ = COL_FNET + COL_SSM + COL_HYENA + COL_SC + COL_GATE
SPLIT_IN = (COL_FNET, COL_FNET + COL_SSM, COL_FNET + COL_SSM + COL_HYENA, COL_FNET + COL_SSM + COL_HYENA + COL_SC)

kernel_name = 'hybrid_bidir_gated_encoder'


def rms_norm(x, w):
    xf = x.astype(jnp.float32)
    y = xf * lax.rsqrt(jnp.mean(xf * xf, axis=-1, keepdims=True) + NORM_EPS)
    return (y * w.astype(jnp.float32)).astype(x.dtype)


def dwconv_centred(u, w):
    k = w.shape[0]
    p = k // 2
    t = u.shape[1]
    up = jnp.pad(u, ((0, 0), (p, p), (0, 0)))
    out = up[:, 0:t] * w[0]
    for j in range(1, k):
        out = out + up[:, j:j + t] * w[j]
    return out


def _pad_time(a, front, back):
    return jnp.pad(a, [(0, 0), (front, back)] + [(0, 0)] * (a.ndim - 2))


def fnet_branch(u):
    b, t, _ = u.shape
    g = u.astype(jnp.float32).reshape(b, t, FNET_GROUPS, FNET_GW)
    y = jnp.fft.fft2(g, axes=(1, 3), norm='ortho').real
    return y.reshape(b, t, BRANCH_W).astype(u.dtype)


def ssd_scan(x, dt, a, bm, cm):
    b, t, g, r, p = x.shape
    n = bm.shape[-1]
    c = t // SSM_CHUNK
    xc = (x * dt[..., None]).reshape(b, c, SSM_CHUNK, g, r, p)
    a_cs = jnp.cumsum((dt * a).reshape(b, c, SSM_CHUNK, g, r), axis=2)
    bc = bm.reshape(b, c, SSM_CHUNK, g, n)
    cc = cm.reshape(b, c, SSM_CHUNK, g, n)
    seg = a_cs[:, :, :, None] - a_cs[:, :, None, :]
    lower = jnp.tril(jnp.ones((SSM_CHUNK, SSM_CHUNK), dtype=bool))[None, None, :, :, None, None]
    decay = jnp.exp(jnp.where(lower, seg, -jnp.inf))
    cb = jnp.einsum('bclgn,bcsgn->bclsg', cc, bc)
    y_diag = jnp.einsum('bclsgr,bcsgrp->bclgrp', cb[..., None] * decay, xc)
    decay_states = jnp.exp(a_cs[:, :, -1:] - a_cs)
    states = jnp.einsum('bcsgn,bcsgr,bcsgrp->bcgrpn', bc, decay_states, xc)
    chunk_decay = jnp.exp(a_cs[:, :, -1])

    def step(h, inp):
        s, d = inp
        return h * d[..., None, None] + s, h

    h0 = jnp.zeros((b, g, r, p, n), jnp.float32)
    _, prev = lax.scan(step, h0, (jnp.moveaxis(states, 1, 0), jnp.moveaxis(chunk_decay, 1, 0)))
    prev = jnp.moveaxis(prev, 0, 1)
    y_off = jnp.einsum('bclgn,bcgrpn,bclgr->bclgrp', cc, prev, jnp.exp(a_cs))
    return (y_diag + y_off).reshape(b, t, g, r, p)


def ssd_branch(u, conv_w, conv_b, dt_bias, a_log, d_skip, norm_w):
    b, t, _ = u.shape
    f32 = jnp.float32
    z, xbc, dt_raw = jnp.split(u, (BRANCH_W, BRANCH_W + SSM_XBC), axis=-1)
    xbc = jax.nn.silu(dwconv_centred(xbc, conv_w) + conv_b).astype(f32)
    xs, bm, cm = jnp.split(xbc, (BRANCH_W, BRANCH_W + SSM_GROUPS * SSM_STATE), axis=-1)
    dt = jax.nn.softplus(dt_raw.astype(f32).reshape(b, t, 2, SSM_GROUPS, SSM_HPG)
                         + dt_bias.astype(f32).reshape(2, SSM_GROUPS, SSM_HPG))
    a = -jnp.exp(a_log.astype(f32)).reshape(2, SSM_GROUPS, SSM_HPG)
    front = SSM_CHUNK - N_META
    back = (-(t - N_META)) % SSM_CHUNK
    xs5 = xs.reshape(b, t, SSM_GROUPS, SSM_HPG, SSM_HEAD_DIM)
    xh = _pad_time(xs5, front, back)
    bp = _pad_time(bm.reshape(b, t, SSM_GROUPS, SSM_STATE), front, back)
    cp = _pad_time(cm.reshape(b, t, SSM_GROUPS, SSM_STATE), front, back)
    dtp = _pad_time(dt, front, back)
    y_fwd = ssd_scan(xh, dtp[:, :, 0], a[0], bp, cp)
    y_bwd = ssd_scan(xh[:, ::-1], dtp[:, ::-1, 1], a[1], bp[:, ::-1], cp[:, ::-1])[:, ::-1]
    y = (y_fwd + y_bwd)[:, front:front + t] + xs5 * d_skip.astype(f32).reshape(SSM_GROUPS, SSM_HPG, 1)
    y = y.reshape(b, t, BRANCH_W) * jax.nn.silu(z.astype(f32))
    yg = y.reshape(b, t, SSM_GROUPS, BRANCH_W // SSM_GROUPS)
    yg = yg * lax.rsqrt(jnp.mean(yg * yg, axis=-1, keepdims=True) + NORM_EPS)
    return (yg.reshape(b, t, BRANCH_W) * norm_w.astype(f32)).astype(u.dtype)


def hyena_filter_spectrum(t, w1, b1, w2, b2, w3, freq):
    f32 = jnp.float32
    tt = jnp.linspace(0.0, 1.0, t, dtype=f32)[:, None]
    bands = (HYENA_EMB - 1) // 2
    w = (2.0 * math.pi / t) * jnp.arange(t, dtype=f32)[:, None]
    fr = jnp.linspace(1e-4, bands - 1, bands, dtype=f32)[None, :]
    z = jnp.concatenate([tt, jnp.cos(fr * w), -jnp.sin(fr * w)], axis=-1)
    fq = freq.astype(f32)
    hid = jnp.sin(fq * (z @ w1.astype(f32) + b1.astype(f32)))
    hid = jnp.sin(fq * (hid @ w2.astype(f32) + b2.astype(f32)))
    h = (hid @ w3.astype(f32)).reshape(t, HYENA_ORDER, 2, BRANCH_W)
    max_decay = math.log(HYENA_TARGET) / HYENA_FAST
    min_decay = math.log(HYENA_TARGET) / HYENA_SLOW
    deltas = jnp.abs(jnp.linspace(min_decay, max_decay, BRANCH_W, dtype=f32))
    h = h * jnp.exp(-tt[:, :, None, None] * deltas)
    h_f = h[:, :, 0]
    h_b = h[1:, :, 1]
    l1 = jnp.sum(jnp.abs(h_f), axis=0) + jnp.sum(jnp.abs(h_b), axis=0)
    k = jnp.concatenate([h_f, jnp.zeros((1, HYENA_ORDER, BRANCH_W), f32), h_b[::-1]], axis=0) / l1
    return jnp.fft.rfft(k, axis=0)


def long_conv(u, kf, bias):
    t = u.shape[1]
    uf = jnp.fft.rfft(u, n=2 * t, axis=1)
    return jnp.fft.irfft(uf * kf, n=2 * t, axis=1)[:, :t] + u * bias


def hyena_branch(u, conv_w, w1, b1, w2, b2, w3, freq, bias):
    t = u.shape[1]
    uc = dwconv_centred(u, conv_w).astype(jnp.float32)
    v, x1, x2 = jnp.split(uc, 3, axis=-1)
    kf = hyena_filter_spectrum(t, w1, b1, w2, b2, w3, freq)
    bias = bias.astype(jnp.float32)
    z = x1 * long_conv(v, kf[:, 0], bias[0])
    z = x2 * long_conv(z, kf[:, 1], bias[1])
    return z.astype(u.dtype)


def shortconv_branch(u, conv_w):
    bg, cg, xin = jnp.split(u, 3, axis=-1)
    return bg * dwconv_centred(cg * xin, conv_w)


def mixer_block(x, norm_mix, w_in, ssm_conv_w, ssm_conv_b, ssm_dt_bias, ssm_a_log, ssm_d, ssm_norm,
                hyena_conv_w, hyena_w1, hyena_b1, hyena_w2, hyena_b2, hyena_w3, hyena_freq, hyena_bias,
                sc_conv_w, w_branch, w_out):
    b, t, _ = x.shape
    h = rms_norm(x, norm_mix)
    proj = h @ w_in
    u_fn, u_ssm, u_hy, u_sc, g_raw = jnp.split(proj, SPLIT_IN, axis=-1)
    branches = (
        fnet_branch(u_fn),
        ssd_branch(u_ssm, ssm_conv_w, ssm_conv_b, ssm_dt_bias, ssm_a_log, ssm_d, ssm_norm),
        hyena_branch(u_hy, hyena_conv_w, hyena_w1, hyena_b1, hyena_w2, hyena_b2, hyena_w3, hyena_freq, hyena_bias),
        shortconv_branch(u_sc, sc_conv_w),
    )
    gates = jax.nn.sigmoid(g_raw.astype(jnp.float32)).astype(x.dtype).reshape(b, t, N_BRANCH, D_MODEL)
    merged = gates[:, :, 0] * (branches[0] @ w_branch[0])
    for k in range(1, N_BRANCH):
        merged = merged + gates[:, :, k] * (branches[k] @ w_branch[k])
    return merged @ w_out


def conv_ffn(x, norm_ffn, ffn_conv_w, w_up, w_down):
    h = rms_norm(x, norm_ffn)
    up = dwconv_centred(h @ w_up, ffn_conv_w)
    a, v = jnp.split(up, 2, axis=-1)
    return (jax.nn.silu(a) * v) @ w_down


def run_trunk(x, meta_tokens, norm_final, mixer_params, ffn_params):
    b = x.shape[0]
    meta = jnp.broadcast_to(meta_tokens[None].astype(x.dtype), (b, N_META, D_MODEL))
    h = jnp.concatenate([meta, x], axis=1)
    for l in range(DEPTH):
        h = h + mixer_block(h, *(p[l] for p in mixer_params))
        h = h + conv_ffn(h, *(p[l] for p in ffn_params))
    return rms_norm(h, norm_final)[:, N_META:]


def setup_inputs(seed: int = 0) -> dict:
    key = jax.random.key(seed)
    ks = jax.random.split(key, 32)
    f32 = jnp.float32

    def nrm(k, shape, scale):
        return jax.random.normal(k, shape, f32) * scale

    dt0 = jnp.exp(jax.random.uniform(ks[8], (DEPTH, 2, SSM_HEADS), f32, math.log(1e-3), math.log(1e-1)))
    return {
        'x_prompt': nrm(ks[0], (BATCH, SEQ, D_MODEL), 1.0),
        'x_sample': nrm(ks[1], (DEC_BATCH, DEC_SEQ, D_MODEL), 1.0),
        'meta_tokens': nrm(ks[2], (N_META, D_MODEL), 1.0),
        'norm_mix': 1.0 + nrm(ks[3], (DEPTH, D_MODEL), 0.02),
        'w_in': nrm(ks[4], (DEPTH, D_MODEL, D_IN_PROJ), D_MODEL ** -0.5),
        'ssm_conv_w': nrm(ks[5], (DEPTH, SSM_CONV, SSM_XBC), SSM_CONV ** -0.5),
        'ssm_conv_b': nrm(ks[6], (DEPTH, SSM_XBC), 0.02),
        'ssm_dt_bias': dt0 + jnp.log(-jnp.expm1(-dt0)),
        'ssm_a_log': jnp.log(jax.random.uniform(ks[9], (DEPTH, 2, SSM_HEADS), f32, 1.0, 16.0)),
        'ssm_d': 1.0 + nrm(ks[10], (DEPTH, SSM_HEADS), 0.02),
        'ssm_norm': 1.0 + nrm(ks[11], (DEPTH, BRANCH_W), 0.02),
        'hyena_conv_w': nrm(ks[12], (DEPTH, HYENA_CONV, COL_HYENA), HYENA_CONV ** -0.5),
        'hyena_w1': nrm(ks[13], (DEPTH, HYENA_EMB, HYENA_FFN), HYENA_EMB ** -0.5),
        'hyena_b1': nrm(ks[14], (DEPTH, HYENA_FFN), 0.02),
        'hyena_w2': nrm(ks[15], (DEPTH, HYENA_FFN, HYENA_FFN), HYENA_FFN ** -0.5),
        'hyena_b2': nrm(ks[16], (DEPTH, HYENA_FFN), 0.02),
        'hyena_w3': nrm(ks[17], (DEPTH, HYENA_FFN, HYENA_ORDER * 2 * BRANCH_W), HYENA_FFN ** -0.5),
        'hyena_freq': 1.0 + nrm(ks[18], (DEPTH, HYENA_FFN), 0.02),
        'hyena_bias': nrm(ks[19], (DEPTH, HYENA_ORDER, BRANCH_W), 1.0),
        'sc_conv_w': nrm(ks[20], (DEPTH, SC_CONV, BRANCH_W), SC_CONV ** -0.5),
        'w_branch': nrm(ks[21], (DEPTH, N_BRANCH, BRANCH_W, D_MODEL), BRANCH_W ** -0.5),
        'w_out': nrm(ks[22], (DEPTH, D_MODEL, D_MODEL), D_MODEL ** -0.5),
        'norm_ffn': 1.0 + nrm(ks[23], (DEPTH, D_MODEL), 0.02),
        'ffn_conv_w': nrm(ks[24], (DEPTH, FFN_CONV, 2 * D_FF), FFN_CONV ** -0.5),
        'w_up': nrm(ks[25], (DEPTH, D_MODEL, 2 * D_FF), D_MODEL ** -0.5),
        'w_down': nrm(ks[26], (DEPTH, D_FF, D_MODEL), D_FF ** -0.5),
        'norm_final': 1.0 + nrm(ks[27], (D_MODEL,), 0.02),
    }


def reference(x_prompt, x_sample, meta_tokens, norm_mix, w_in, ssm_conv_w, ssm_conv_b, ssm_dt_bias,
              ssm_a_log, ssm_d, ssm_norm, hyena_conv_w, hyena_w1, hyena_b1, hyena_w2, hyena_b2, hyena_w3,
              hyena_freq, hyena_bias, sc_conv_w, w_branch, w_out, norm_ffn, ffn_conv_w, w_up, w_down,
              norm_final):
    mixer_params = (norm_mix, w_in, ssm_conv_w, ssm_conv_b, ssm_dt_bias, ssm_a_log, ssm_d, ssm_norm,
                    hyena_conv_w, hyena_w1, hyena_b1, hyena_w2, hyena_b2, hyena_w3, hyena_freq, hyena_bias,
                    sc_conv_w, w_branch, w_out)
    ffn_params = (norm_ffn, ffn_conv_w, w_up, w_down)
    y_prompt = run_trunk(x_prompt, meta_tokens, norm_final, mixer_params, ffn_params)
    y_sample = run_trunk(x_sample, meta_tokens, norm_final, mixer_params, ffn_params)
    return (y_prompt, y_sample)
```

```python
import math
from contextlib import ExitStack

import numpy as np
import ml_dtypes

import concourse.bass as bass
import concourse.mybir as mybir
from concourse.bass_utils import run_bass_kernel_spmd

F32 = mybir.dt.float32
BF16 = mybir.dt.bfloat16
AF = mybir.ActivationFunctionType
ALU = mybir.AluOpType

D = 1024
NMETA = 16
EPS = 1e-6
BW = 512
DIN = 9232
DFF = 2816
TMAX = 4112
O_FN, O_Z, O_XBC, O_DT, O_HY, O_SC, O_G = 0, 512, 1024, 2048, 2064, 3600, 5136
R_FN, R_SZ, R_XS, R_B, R_C, R_V, R_X1, R_X2, R_G = 0, 4, 8, 12, 14, 16, 20, 24, 28
NPR = 60
P_NMIX, P_NFFN, P_NFIN, P_SCW, P_SCB, P_SD, P_SNW, P_HCW, P_HB, P_CCW, P_FCW, P_HB1, P_HB2, P_HFQ = (
    0, 8, 16, 24, 48, 56, 60, 64, 100, 108, 120, 252, 253, 254)
NPP = 256
CM_ID, CM_LE, CM_GE, CM_GT, CM_LT, CM_ONE = 0, 128, 256, 384, 512, 640
NCM = 768


def chunks_of(L):
    return [(0, 16)] + [(16 + 128 * i, 128) for i in range(L // 128)]


def tiles_of(L):
    return [(0, 16)] + [(16 + 512 * i, 512) for i in range(L // 512)]


def fchunks_of(L):
    return [(0, 17)] + [(17 + 128 * i, 128) for i in range(L // 128)]


class Obj:
    __slots__ = ("w", "r", "name")

    def __init__(self, name=""):
        self.w = None
        self.r = {}
        self.name = name


class Emit:
    NDS = 40

    def __init__(self, nc, es):
        self.nc = nc
        self.eng = {"pe": nc.tensor, "act": nc.scalar, "dve": nc.vector, "pool": nc.gpsimd, "sp": nc.sync}
        self.sem = {}
        self.cnt = {}
        for e in ("pe", "act", "dve", "pool"):
            self.sem[e] = es.enter_context(nc.semaphore("s_" + e))
            self.cnt[e] = 0
        self.dsem = [es.enter_context(nc.semaphore("d%d" % i)) for i in range(self.NDS)]
        self.dcnt = [0] * self.NDS
        self.dnext = 0
        self.seen = {e: {} for e in self.eng}
        self.nins = 0
        self.marks = []

    def _wait(self, e, deps):
        best = {}
        for (s, v) in deps:
            if e == "pe" and s is self.sem["pe"]:
                continue
            k = id(s)
            if k not in best or best[k][1] < v:
                best[k] = (s, v)
        sn = self.seen[e]
        for k, (s, v) in best.items():
            if sn.get(k, 0) >= v:
                continue
            self.eng[e].wait_ge(s, v)
            sn[k] = v

    def _deps(self, reads, writes):
        deps = []
        for o in reads:
            if o.w is not None:
                deps.append(o.w)
        for o in writes:
            if o.w is not None:
                deps.append(o.w)
            deps.extend(o.r.values())
        return deps

    def _mark(self, e, tok, reads, writes):
        for o in reads:
            o.r[e] = tok
        for o in writes:
            o.w = tok
            o.r = {}

    def op(self, e, fn, reads=(), writes=()):
        self._wait(e, self._deps(reads, writes))
        ins = fn(self.eng[e])
        self.cnt[e] += 1
        ins.then_inc(self.sem[e], 1)
        self._mark(e, (self.sem[e], self.cnt[e]), reads, writes)
        self.nins += 1
        return ins

    def dma(self, q, out, in_, reads=(), writes=()):
        deps = self._deps(reads, writes)
        i = self.dnext
        self.dnext = (i + 1) % self.NDS
        if self.dcnt[i]:
            deps.append((self.dsem[i], self.dcnt[i]))
        self._wait(q, deps)
        ins = self.eng[q].dma_start(out=out, in_=in_)
        self.dcnt[i] += 16
        ins.then_inc(self.dsem[i], 16)
        self._mark("dma%d" % i, (self.dsem[i], self.dcnt[i]), reads, writes)
        self.nins += 1
        return ins

    def mark(self, name):
        self.marks.append((name, self.cnt["pe"]))

    def barrier(self):
        toks = [(self.sem[e], self.cnt[e]) for e in self.sem if self.cnt[e]]
        toks += [(self.dsem[i], self.dcnt[i]) for i in range(self.NDS) if self.dcnt[i]]
        for e in self.eng:
            self._wait(e, toks)


class Ring:
    def __init__(self, items):
        self.items = items
        self.i = 0

    def next(self):
        it = self.items[self.i]
        self.i = (self.i + 1) % len(self.items)
        return it


def build_program(nP, nS, n_layers=2, stop_after=None, debug=False):
    nc = bass.Bass("TRN2", target_bir_lowering=False)
    es = ExitStack()
    K = Emit(nc, es)

    def din(name, shape, dt=F32):
        return nc.dram_tensor(name, list(shape), dt, kind="ExternalInput").ap()

    def dscr(name, shape, dt, out=False):
        kind = "ExternalOutput" if (out or debug) else "Internal"
        return nc.dram_tensor(name, list(shape), dt, kind=kind).ap()

    xp = din("xp", [max(nP, 1), 2048, D])
    xs = din("xs", [max(nS, 1), 4096, D])
    meta_d = din("meta_tokens", [NMETA, D])
    w_in = din("w_in", [2, D, DIN])
    w_branch = din("w_branch", [2, 4, BW, D])
    w_out = din("w_out", [2, D, D])
    w_up = din("w_up", [2, D, 2 * DFF])
    w_down = din("w_down", [2, DFF, D])
    hw1 = din("hyena_w1", [2, 33, 64])
    hw2 = din("hyena_w2", [2, 64, 64])
    hw3 = din("hyena_w3", [2, 64, 2048])
    pp_d = din("pp", [2, 128, NPP])
    pb_d = din("pb", [2, 128, 32])
    cm_d = din("cm", [128, NCM])
    cmb_d = din("cmb", [128, 512], BF16)
    dl_d = din("dl", [128, 512])
    tabs = {}
    for L in sorted(set(([2048] if nP else []) + ([4096] if nS else []))):
        T = L + 16
        tabs[L] = dict(
            fc=din("fc%d" % L, [T, T], BF16), fs=din("fs%d" % L, [T, T], BF16),
            hc=din("hc%d" % L, [T + 1, T + 1], BF16), hs=din("hs%d" % L, [T + 1, T + 1], BF16),
            hcb=din("hcb%d" % L, [1 + L // 128, 128, 1 + L // 128, 128], BF16),
            hsb=din("hsb%d" % L, [1 + L // 128, 128, 1 + L // 128, 128], BF16),
            zt=din("zt%d" % L, [33, T]), ntt=din("ntt%d" % L, [128, 1 + L // 128]),
            wf=din("wf%d" % L, [128, 1 + L // 128]),
            kf=[dscr("kf%d_%d" % (L, l), [1 + L // 128, 128, 2, 2, 512], F32) for l in range(n_layers)],
        )
    yp = nc.dram_tensor("yp", [max(nP, 1), 2048, D], F32, kind="ExternalOutput").ap()
    ys = nc.dram_tensor("ys", [max(nS, 1), 4096, D], F32, kind="ExternalOutput").ap()
    XT = dscr("XT", [8, 128, TMAX], F32)
    PR = dscr("PR", [NPR, 128, TMAX], BF16)
    BR = dscr("BR", [16, 128, TMAX], BF16)
    FA = dscr("FA", [22, 128, TMAX], BF16)
    o_XTt = [Obj("XT%d" % i) for i in range(10)]

    def xto(t0):
        return [o_XTt[0 if t0 < 16 else 1 + (t0 - 16) // 512]]
    o_PR = [Obj("PR%d" % i) for i in range(NPR)]
    o_BR = [Obj("BR%d" % i) for i in range(16)]
    o_FA = [Obj("FA%d" % i) for i in range(22)]

    uid = [0]

    def sb(name, shape, dt=F32, stack=es):
        uid[0] += 1
        return stack.enter_context(nc.sbuf_tensor("%s_%d" % (name, uid[0]), list(shape), dt))

    cm = sb("cm_sb", [128, NCM]); o_cm = Obj()
    cmb = sb("cmb_sb", [128, 512], BF16); o_cmb = Obj()
    pp = sb("pp_sb", [128, 2, NPP]); o_pp = Obj()
    pb = sb("pb_sb", [128, 2, 32]); o_pb = Obj()
    ps = [es.enter_context(nc.psum_tensor("ps%d" % i, [128, 512], F32)) for i in range(8)]
    o_ps = [Obj("ps%d" % i) for i in range(8)]
    psr = Ring(list(range(7)))

    K.dma("sp", cm[:], cm_d, writes=[o_cm])
    K.dma("sp", cmb[:], cmb_d, writes=[o_cmb])
    K.dma("sp", pp[:], pp_d.rearrange("l p c -> p l c"), writes=[o_pp])
    K.dma("sp", pb[:], pb_d.rearrange("l p c -> p l c"), writes=[o_pb])
    epsc = sb("epsc", [128, 2]);
    K.op("pool", lambda e: e.memset(epsc[:, 0:1], EPS), writes=[o_cm])
    K.op("pool", lambda e: e.memset(epsc[:, 1:2], -math.pi), writes=[o_cm])
    identf = cm[:, CM_ID:CM_ID + 128]
    identb = cmb[:, 0:128]
    onesb = cmb[:, 128:256]
    cs128 = cmb[:, 256:512]

    def ppc(l, col, n=1, rows=128):
        return pp[0:rows, l, col:col + n]

    def rstd_from(r_ap, o_r, ps_ap, o_p, scale):
        K.op("act", lambda e: e.activation(out=r_ap, in_=ps_ap, func=AF.Sqrt, bias=epsc[:, 0:1], scale=scale),
             reads=[o_p, o_cm], writes=[o_r])
        K.op("dve", lambda e: e.reciprocal(out=r_ap, in_=r_ap), reads=[o_r], writes=[o_r])

    def load_w(stack_bufs, src_ap, kc, ncols):
        stg, o_stg = stack_bufs["stg"].next()
        wb, o_wb = stack_bufs["wb"].next()
        K.dma("sp", stg[:, 0:kc, 0:ncols], src_ap.rearrange("(k p) m -> p k m", p=128),
              writes=[o_stg])
        K.op("pool", lambda e: e.tensor_copy(out=wb[:, 0:kc, 0:ncols], in_=stg[:, 0:kc, 0:ncols]),
             reads=[o_stg], writes=[o_wb])
        return wb, o_wb

    class WStream:
        def __init__(self, bufs, plan, lookahead):
            self.bufs, self.plan, self.la = bufs, plan, lookahead
            self.issued = 0
            self.got = {}

        def prime(self):
            self._fill(0)

        def get(self, i):
            self._fill(i)
            return self.got.pop(i)

        def _fill(self, i):
            while self.issued < min(len(self.plan), i + self.la + 1):
                src, kc, n = self.plan[self.issued]
                self.got[self.issued] = load_w(self.bufs, src, kc, n)
                self.issued += 1

    def rmsnorm_to(HT, o_HTt, l, pcol, tl, keep=None):
        with ExitStack() as own:
            st = keep if keep is not None else own
            xr = Ring([(sb("rn_x%d" % i, [128, 8, 512], F32, st), Obj()) for i in range(2)])
            sq = Ring([(sb("rn_s%d" % i, [128, 8, 512], BF16, st), Obj()) for i in range(2)])
            rs = Ring([(sb("rn_r%d" % i, [128, 512], F32, st), Obj()) for i in range(2)])
            for ti, (t0, tw) in enumerate(tl):
                x, o_x = xr.next()
                s, o_s = sq.next()
                r, o_r = rs.next()
                K.dma("sp", x[:, :, 0:tw], XT[:, :, t0:t0 + tw].rearrange("c p t -> p c t"),
                      reads=xto(t0), writes=[o_x])
                K.op("act", lambda e: e.activation(out=s[:, :, 0:tw], in_=x[:, :, 0:tw], func=AF.Square),
                     reads=[o_x], writes=[o_s])
                b = psr.next()
                for c in range(8):
                    K.op("pe", lambda e: e.matmul(ps[b][:, 0:tw], lhsT=onesb, rhs=s[:, c, 0:tw],
                                                  start=(c == 0), stop=(c == 7)),
                         reads=[o_s, o_cmb], writes=[o_ps[b]])
                rstd_from(r[:, 0:tw], o_r, ps[b][:, 0:tw], o_ps[b], 1.0 / D)
                for c in range(8):
                    K.op("dve", lambda e: e.scalar_tensor_tensor(
                        out=HT[:, c, t0:t0 + tw], in0=x[:, c, 0:tw], scalar=ppc(l, pcol + c),
                        in1=r[:, 0:tw], op0=ALU.mult, op1=ALU.mult),
                        reads=[o_x, o_r, o_pp], writes=[o_HTt[ti]])
            if keep is None:
                K.barrier()

    def phase_input(x_d, si, L):
        with ExitStack() as st:
            xin = Ring([(sb("pi_x%d" % i, [128, D], F32, st), Obj()) for i in range(4)])
            xo = Ring([(sb("pi_o%d" % i, [128, 8, 128], F32, st), Obj()) for i in range(4)])
            for (t0, q) in chunks_of(L):
                x, o_x = xin.next()
                o, o_o = xo.next()
                src = meta_d if t0 == 0 else x_d[si, t0 - 16:t0 - 16 + q, :]
                K.dma("sp", x[0:q, :], src, writes=[o_x])
                b0 = psr.next()
                b1 = psr.next()
                for c in range(8):
                    b = b0 if c < 4 else b1
                    K.op("pe", lambda e: e.matmul(ps[b][:, (c % 4) * 128:(c % 4) * 128 + q],
                                                  lhsT=x[0:q, c * 128:(c + 1) * 128], rhs=identf[0:q, 0:q],
                                                  start=True, stop=True),
                         reads=[o_x, o_cm], writes=[o_ps[b]])
                K.op("act", lambda e: e.activation(
                    out=o[:, 0:4, 0:q], in_=ps[b0][:].rearrange("p (c t) -> p c t", c=4)[:, :, 0:q], func=AF.Copy),
                    reads=[o_ps[b0]], writes=[o_o])
                K.op("dve", lambda e: e.tensor_copy(
                    out=o[:, 4:8, 0:q], in_=ps[b1][:].rearrange("p (c t) -> p c t", c=4)[:, :, 0:q]),
                    reads=[o_ps[b1]], writes=[o_o])
                K.dma("pool", XT[:, :, t0:t0 + q].rearrange("c p t -> p c t"), o[:, :, 0:q],
                      reads=[o_o], writes=xto(t0))

    def phase_final(y_d, si, L):
        with ExitStack() as st:
            HT = sb("pf_h", [128, 8, 512], F32, st)
            o_HT = Obj()
            yo = Ring([(sb("pf_y%d" % i, [128, D], F32, st), Obj()) for i in range(4)])
            xr_r = Ring([(sb("pf_x", [128, 8, 512], F32, st), Obj()) for _ in range(2)])
            s_r = Ring([(sb("pf_s", [128, 8, 512], BF16, st), Obj()) for _ in range(2)])
            r_r = Ring([(sb("pf_r", [128, 512], F32, st), Obj()) for _ in range(2)])
            for (t0, tw) in tiles_of(L)[1:]:
                if True:
                    xr, o_x = xr_r.next()
                    s, o_s = s_r.next()
                    r, o_r = r_r.next()
                    K.dma("sp", xr[:], XT[:, :, t0:t0 + tw].rearrange("c p t -> p c t"), reads=xto(t0), writes=[o_x])
                    K.op("act", lambda e: e.activation(out=s[:], in_=xr[:], func=AF.Square), reads=[o_x], writes=[o_s])
                    b = psr.next()
                    for c in range(8):
                        K.op("pe", lambda e: e.matmul(ps[b][:], lhsT=onesb, rhs=s[:, c, :], start=(c == 0), stop=(c == 7)),
                             reads=[o_s, o_cmb], writes=[o_ps[b]])
                    rstd_from(r[:], o_r, ps[b][:], o_ps[b], 1.0 / D)
                    for c in range(8):
                        K.op("dve", lambda e: e.scalar_tensor_tensor(
                            out=HT[:, c, :], in0=xr[:, c, :], scalar=ppc(0, P_NFIN + c), in1=r[:],
                            op0=ALU.mult, op1=ALU.mult), reads=[o_x, o_r, o_pp], writes=[o_HT])
                    for j in range(4):
                        y, o_y = yo.next()
                        b0 = psr.next()
                        b1 = psr.next()
                        for c in range(8):
                            b = b0 if c < 4 else b1
                            K.op("pe", lambda e: e.matmul(ps[b][:, (c % 4) * 128:(c % 4 + 1) * 128],
                                                          lhsT=HT[:, c, j * 128:(j + 1) * 128], rhs=identf,
                                                          start=True, stop=True),
                                 reads=[o_HT, o_cm], writes=[o_ps[b]])
                        K.op("act", lambda e: e.activation(out=y[:, 0:512], in_=ps[b0][:], func=AF.Copy),
                             reads=[o_ps[b0]], writes=[o_y])
                        K.op("dve", lambda e: e.tensor_copy(out=y[:, 512:1024], in_=ps[b1][:]),
                             reads=[o_ps[b1]], writes=[o_y])
                        tt = t0 - 16 + j * 128
                        K.dma("pool", y_d[si, tt:tt + 128, :], y[:], reads=[o_y], writes=[])

    def conv3(dst, o_dst, row, o_row, l, wcol, T, acc=None, o_acc=None):
        if acc is None:
            acc, o_acc = dst, o_dst
        K.op("act", lambda e: e.activation(out=acc, in_=row[:, 0:T], func=AF.Copy, scale=ppc(l, wcol)),
             reads=[o_row, o_pp], writes=[o_acc])
        K.op("dve", lambda e: e.scalar_tensor_tensor(out=acc, in0=row[:, 1:T + 1], scalar=ppc(l, wcol + 1), in1=acc,
                                                     op0=ALU.mult, op1=ALU.add), reads=[o_row, o_pp, o_acc], writes=[o_acc])
        K.op("dve", lambda e: e.scalar_tensor_tensor(out=dst, in0=row[:, 2:T + 2], scalar=ppc(l, wcol + 2), in1=acc,
                                                     op0=ALU.mult, op1=ALU.add), reads=[o_row, o_pp, o_acc], writes=[o_dst])

    def phase1(l, L, dtS, dtA, o_dt):
        T = L + 16
        tl = tiles_of(L)
        with ExitStack() as st:
            HT = sb("p1_ht", [128, 8, T], BF16, st); o_HTt = [Obj() for _ in tl]
            wbufs = dict(stg=Ring([(sb("p1_stg", [128, 8, 256], F32, st), Obj()) for _ in range(3)]),
                         wb=Ring([(sb("p1_wb", [128, 8, 256], BF16, st), Obj()) for _ in range(3)]))
            flip = [0]
            plan = []
            for c0_ in (O_FN, O_FN + 256, O_Z, O_Z + 256):
                plan.append((w_in[l, :, c0_:c0_ + 256], 8, 256))
            for g_ in range(0, 8, 2):
                plan.append((w_in[l, :, O_XBC + g_ * 128:O_XBC + g_ * 128 + 256], 8, 256))
            plan.append((w_in[l, :, O_DT:O_DT + 16], 8, 16))
            for g_ in range(0, 12, 2):
                plan.append((w_in[l, :, O_HY + g_ * 128:O_HY + g_ * 128 + 256], 8, 256))
            for j_ in range(4):
                for cc_ in (O_SC + j_ * 128, O_SC + 512 + j_ * 128, O_SC + 1024 + j_ * 128):
                    plan.append((w_in[l, :, cc_:cc_ + 128], 8, 128))
            for g_ in range(0, 32, 2):
                plan.append((w_in[l, :, O_G + g_ * 128:O_G + g_ * 128 + 256], 8, 256))
            ws = WStream(wbufs, plan, 2)
            ws.prime()
            rmsnorm_to(HT, o_HTt, l, P_NMIX, tl, keep=(st if L == 2048 else None))
            K.mark('p1_norm')
            rows = Ring([(sb("p1_row", [128, T + 2], F32, st), Obj()) for _ in range(4)])
            outs = Ring([(sb("p1_out", [128, T], BF16, st), Obj()) for _ in range(2)])
            for (rw, o_rw) in rows.items:
                K.op("pool", lambda e: e.memset(rw[:, 0:1], 0.0), writes=[o_rw])
                K.op("pool", lambda e: e.memset(rw[:, T + 1:T + 2], 0.0), writes=[o_rw])
            wi = [0]

            def next_w():
                r_ = ws.get(wi[0])
                wi[0] += 1
                return r_

            def gemm_row(wb, o_wb, c0, dst, o_dst, off, func=None):
                for ti, (t0, tw) in enumerate(tl):
                    b = psr.next()
                    for kc in range(8):
                        K.op("pe", lambda e: e.matmul(ps[b][:, 0:tw], lhsT=wb[:, kc, c0:c0 + 128],
                                                      rhs=HT[:, kc, t0:t0 + tw], start=(kc == 0), stop=(kc == 7)),
                             reads=[o_wb, o_HTt[ti]], writes=[o_ps[b]])
                    d = dst[:, off + t0:off + t0 + tw]
                    flip[0] ^= 1
                    if func is not None:
                        K.op("act", lambda e: e.activation(out=d, in_=ps[b][:, 0:tw], func=func),
                             reads=[o_ps[b]], writes=[o_dst])
                    else:
                        K.op("act", lambda e: e.activation(out=d, in_=ps[b][:, 0:tw], func=AF.Copy),
                             reads=[o_ps[b]], writes=[o_dst])

            def store(dst_d, o_d, ob, o_ob):
                K.dma("pool", dst_d[:, 0:T], ob[:, 0:T], reads=[o_ob], writes=[o_d])

            def wsrc(c0, n):
                return w_in[l, :, c0:c0 + n]

            def simple_rows(col0, nrows, pr0, func):
                for g in range(0, nrows, 2):
                    wb, o_wb = next_w()
                    for j in range(2):
                        ob, o_ob = outs.next()
                        gemm_row(wb, o_wb, j * 128, ob, o_ob, 0, func)
                        store(PR[pr0 + g + j], o_PR[pr0 + g + j], ob, o_ob)

            simple_rows(O_FN, 4, R_FN, None)
            simple_rows(O_Z, 4, R_SZ, AF.Silu)
            K.mark('p1_fnz')
            for g in range(0, 8, 2):
                wb, o_wb = next_w()
                for j in range(2):
                    c = g + j
                    rw, o_rw = rows.next()
                    tmp, o_tmp = rows.next()
                    gemm_row(wb, o_wb, j * 128, rw, o_rw, 1)
                    conv3(tmp[:, 1:T + 1], o_tmp, rw, o_rw, l, P_SCW + 3 * c, T)
                    ob, o_ob = outs.next()
                    K.op("act", lambda e: e.activation(out=ob[:, 0:T], in_=tmp[:, 1:T + 1], func=AF.Silu,
                                                       bias=ppc(l, P_SCB + c), scale=1.0),
                         reads=[o_tmp, o_pp], writes=[o_ob])
                    store(PR[R_XS + c], o_PR[R_XS + c], ob, o_ob)
            K.mark('p1_xbc')
            wb, o_wb = next_w()
            with ExitStack() as st2:
                av = sb("p1_a", [128, 16], F32, st2); o_av = Obj()
                tmpd = Ring([(sb("p1_td", [128, 16], F32, st2), Obj()) for _ in range(2)])
                K.op("act", lambda e: e.activation(out=av[:], in_=pb[:, l, 16:32], func=AF.Exp), reads=[o_pb], writes=[o_av])
                K.op("dve", lambda e: e.tensor_scalar(out=av[:], in0=av[:], scalar1=-1.0, scalar2=None, op0=ALU.mult),
                     reads=[o_av], writes=[o_av])
                for ci, (q0, q) in enumerate(chunks_of(L)):
                    b = psr.next()
                    td, o_td = tmpd.next()
                    for kc in range(8):
                        K.op("pe", lambda e: e.matmul(ps[b][0:q, 0:16], lhsT=HT[:, kc, q0:q0 + q], rhs=wb[:, kc, 0:16],
                                                      start=(kc == 0), stop=(kc == 7)),
                             reads=[o_wb, o_HTt[0 if q0 < 16 else 1 + (q0 - 16) // 512]], writes=[o_ps[b]])
                    K.op("dve", lambda e: e.tensor_tensor(out=td[0:q, :], in0=ps[b][0:q, 0:16], in1=pb[0:q, l, 0:16],
                                                          op=ALU.add), reads=[o_ps[b], o_pb], writes=[o_td])
                    K.op("act", lambda e: e.activation(out=td[0:q, :], in_=td[0:q, :], func=AF.Exp),
                         reads=[o_td], writes=[o_td])
                    K.op("act", lambda e: e.activation(out=dtS[0:q, ci, :], in_=td[0:q, :], func=AF.Ln, bias=1.0),
                         reads=[o_td], writes=[o_dt])
                    K.op("dve", lambda e: e.tensor_tensor(out=dtA[0:q, ci, :], in0=dtS[0:q, ci, :], in1=av[0:q, :],
                                                          op=ALU.mult), reads=[o_dt, o_av], writes=[o_dt])
                K.barrier()
            K.mark('p1_dt')
            for g in range(0, 12, 2):
                wb, o_wb = next_w()
                for j in range(2):
                    c = g + j
                    rw, o_rw = rows.next()
                    tmp, o_tmp = rows.next()
                    gemm_row(wb, o_wb, j * 128, rw, o_rw, 1)
                    ob, o_ob = outs.next()
                    conv3(ob[:, 0:T], o_ob, rw, o_rw, l, P_HCW + 3 * c, T, acc=tmp[:, 1:T + 1], o_acc=o_tmp)
                    store(PR[R_V + c], o_PR[R_V + c], ob, o_ob)
            K.mark('p1_hy')
            for j in range(4):
                rb, o_rb = rows.next()
                rc, o_rc = rows.next()
                rx, o_rx = rows.next()
                rt, o_rt = rows.next()
                for (r_, o_r_, cc) in ((rb, o_rb, O_SC + j * 128), (rc, o_rc, O_SC + 512 + j * 128),
                                       (rx, o_rx, O_SC + 1024 + j * 128)):
                    wb, o_wb = next_w()
                    gemm_row(wb, o_wb, 0, r_, o_r_, 1)
                K.op("dve", lambda e: e.tensor_tensor(out=rc[:, 1:T + 1], in0=rc[:, 1:T + 1], in1=rx[:, 1:T + 1],
                                                      op=ALU.mult), reads=[o_rc, o_rx], writes=[o_rc])
                conv3(rt[:, 1:T + 1], o_rt, rc, o_rc, l, P_CCW + 3 * j, T)
                ob, o_ob = outs.next()
                K.op("dve", lambda e: e.tensor_tensor(out=ob[:, 0:T], in0=rb[:, 1:T + 1], in1=rt[:, 1:T + 1],
                                                      op=ALU.mult), reads=[o_rb, o_rt], writes=[o_ob])
                store(BR[12 + j], o_BR[12 + j], ob, o_ob)
            K.mark('p1_sc')
            simple_rows(O_G, 32, R_G, AF.Sigmoid)
            K.barrier()

    def phase_ssd(l, L, dtS, dtA, o_dt):
        T = L + 16
        ch = chunks_of(L)
        nch = len(ch)
        with ExitStack() as st:
            xsT = sb("ss_xs", [128, 4, T], BF16, st); o_xs = Obj()
            bcT = sb("ss_bc", [128, 4, T], BF16, st); o_bc = Obj()
            Yp = sb("ss_y", [128, nch, 512], F32, st); o_Y = [Obj() for _ in ch]
            hst2 = [(sb("ss_h", [128, 512], F32, st), Obj()) for _ in range(2)]
            hbf2 = [(sb("ss_hb", [128, 512], BF16, st), Obj()) for _ in range(2)]
            K.dma("sp", xsT[:], PR[R_XS:R_XS + 4, :, 0:T].rearrange("c p t -> p c t"), reads=o_PR[R_XS:R_XS + 4], writes=[o_xs])
            K.dma("sp", bcT[:], PR[R_B:R_B + 4, :, 0:T].rearrange("c p t -> p c t"), reads=o_PR[R_B:R_B + 4], writes=[o_bc])

            def ring(name, shape, dt, n=2):
                return Ring([(sb(name, shape, dt, st), Obj()) for _ in range(n)])
            r_xdt = ring("ss_xdt", [128, 512], BF16, 3)
            r_xw = ring("ss_xw", [128, 512], BF16, 3)
            r_bt = ring("ss_bt", [128, 256], BF16, 3)
            r_cbm = ring("ss_cbm", [128, 2, 128], F32)
            r_E = ring("ss_E", [128, 8], F32, 8)
            r_wd = ring("ss_wd", [128, 8], F32)
            r_cd = ring("ss_cd", [128, 8], F32, 8)
            r_lm = ring("ss_lm", [128, 8, 128], F32)
            r_LT = ring("ss_LT", [128, 8, 128], F32)
            r_MT = ring("ss_MT", [128, 8, 128], BF16, 3)
            r_yd = ring("ss_yd", [128, 512], F32, 3)
            r_sts = ring("ss_sts", [128, 512], F32, 3)
            r_yo = ring("ss_yo", [128, 512], F32)
            r_yz = ring("ss_yz", [128, 4, 128], F32)
            r_sq = ring("ss_sq", [128, 4, 128], BF16)
            r_rs = ring("ss_rs", [128, 2, 128], F32)
            r_ob = ring("ss_ob", [128, 4, 128], BF16)
            r_sz = ring("ss_szc", [128, 4, 128], BF16)

            def mm(b, out, lhsT, rhs, reads, start=True, stop=True):
                K.op("pe", lambda e: e.matmul(out, lhsT=lhsT, rhs=rhs, start=start, stop=stop),
                     reads=reads, writes=[o_ps[b]])

            def v8(ap):
                return ap.rearrange("p (h d) -> p h d", h=8)

            for d in (0, 1):
                K.op("pool", lambda e: e.memset(hst2[d][0][:], 0.0), writes=[hst2[d][1]])
                K.op("pool", lambda e: e.memset(hbf2[d][0][:], 0.0), writes=[hbf2[d][1]])

            def stage_a(d, ci):
                tri = cm[:, CM_LE:CM_LE + 128] if d == 0 else cm[:, CM_GE:CM_GE + 128]
                strict = cm[:, CM_GT:CM_GT + 128] if d == 0 else cm[:, CM_LT:CM_LT + 128]
                q0, q = ch[ci]
                dtv = dtS[0:q, ci, d * 8:(d + 1) * 8]
                dav = dtA[0:q, ci, d * 8:(d + 1) * 8]
                xdt, o_xdt = r_xdt.next(); xw, o_xw = r_xw.next(); bt, o_bt = r_bt.next()
                cbm, o_cbm = r_cbm.next(); E, o_E = r_E.next(); wd, o_wd = r_wd.next(); cd, o_cd = r_cd.next()
                lm, o_lm = r_lm.next(); LT, o_LT = r_LT.next(); MT, o_MT = r_MT.next()
                for g in range(2):
                    K.op("pool", lambda e: e.tensor_tensor(
                        out=lm[0:q, g * 4:(g + 1) * 4, 0:q],
                        in0=strict[0:q, 0:q].unsqueeze(1).to_broadcast([q, 4, q]),
                        in1=dav[:, g * 4:(g + 1) * 4].unsqueeze(2).to_broadcast([q, 4, q]), op=ALU.mult),
                        reads=[o_cm, o_dt], writes=[o_lm])
                bx = psr.next()
                for j in range(4):
                    mm(bx, ps[bx][0:q, j * 128:(j + 1) * 128], xsT[:, j, q0:q0 + q], identb, [o_xs, o_cmb])
                ba = psr.next()
                mm(ba, ps[ba][0:q, 0:8], tri[0:q, 0:q], dav, [o_cm, o_dt])
                mm(ba, ps[ba][0:q, 8:16], strict[0:q, 0:q], dav, [o_cm, o_dt])
                mm(ba, ps[ba][:, 16:24], cm[0:q, CM_ONE:CM_ONE + 128], dav, [o_cm, o_dt])
                K.op("act", lambda e: e.activation(out=E[0:q, :], in_=ps[ba][0:q, 0:8], func=AF.Exp), reads=[o_ps[ba]], writes=[o_E])
                K.op("act", lambda e: e.activation(out=wd[0:q, :], in_=ps[ba][0:q, 8:16], func=AF.Exp), reads=[o_ps[ba]], writes=[o_wd])
                K.op("act", lambda e: e.activation(out=cd[:, :], in_=ps[ba][:, 16:24], func=AF.Exp), reads=[o_ps[ba]], writes=[o_cd])
                K.op("dve", lambda e: e.tensor_tensor(out=wd[0:q, :], in0=wd[0:q, :], in1=dtv, op=ALU.mult), reads=[o_wd, o_dt], writes=[o_wd])
                K.op("dve", lambda e: e.tensor_tensor(out=v8(xdt[0:q, :]), in0=v8(ps[bx][0:q, :]),
                                                      in1=dtv.unsqueeze(2).to_broadcast([q, 8, 64]), op=ALU.mult),
                     reads=[o_ps[bx], o_dt], writes=[o_xdt])
                K.op("dve", lambda e: e.tensor_tensor(out=v8(xw[0:q, :]), in0=v8(ps[bx][0:q, :]),
                                                      in1=wd[0:q, :].unsqueeze(2).to_broadcast([q, 8, 64]), op=ALU.mult),
                     reads=[o_ps[bx], o_wd], writes=[o_xw])
                bb = psr.next()
                for g in range(2):
                    mm(bb, ps[bb][0:q, g * 128:(g + 1) * 128], bcT[:, g, q0:q0 + q], identb, [o_bc, o_cmb])
                K.op("act", lambda e: e.activation(out=bt[0:q, :], in_=ps[bb][0:q, 0:256], func=AF.Copy), reads=[o_ps[bb]], writes=[o_bt])
                bc = psr.next()
                for g in range(2):
                    mm(bc, ps[bc][0:q, g * 128:g * 128 + q], bcT[:, g, q0:q0 + q], bcT[:, 2 + g, q0:q0 + q], [o_bc])
                for g in range(2):
                    K.op("dve", lambda e: e.tensor_tensor(out=cbm[0:q, g, 0:q], in0=ps[bc][0:q, g * 128:g * 128 + q],
                                                          in1=tri[0:q, 0:q], op=ALU.mult), reads=[o_ps[bc], o_cm], writes=[o_cbm])
                for g in range(2):
                    bD = psr.next()
                    for hh in range(4):
                        mm(bD, ps[bD][0:q, hh * 128:hh * 128 + q], lm[0:q, g * 4 + hh, 0:q], tri[0:q, 0:q], [o_lm, o_cm])
                    K.op("act", lambda e: e.activation(out=LT[0:q, g * 4:(g + 1) * 4, 0:q],
                                                       in_=ps[bD][0:q, :].rearrange("p (h s) -> p h s", h=4)[:, :, 0:q],
                                                       func=AF.Exp), reads=[o_ps[bD]], writes=[o_LT])
                    K.op("dve", lambda e: e.tensor_tensor(
                        out=MT[0:q, g * 4:(g + 1) * 4, 0:q], in0=LT[0:q, g * 4:(g + 1) * 4, 0:q],
                        in1=cbm[0:q, g, 0:q].unsqueeze(1).to_broadcast([q, 4, q]), op=ALU.mult),
                        reads=[o_LT, o_cbm], writes=[o_MT])
                return dict(d=d, ci=ci, E=E, o_E=o_E, cd=cd, o_cd=o_cd, MT=MT, o_MT=o_MT, xdt=xdt, o_xdt=o_xdt,
                            xw=xw, o_xw=o_xw, bt=bt, o_bt=o_bt)

            def stage_a2(c):
                q0, q = ch[c["ci"]]
                MT, o_MT, xdt, o_xdt, xw, o_xw, bt, o_bt = (c["MT"], c["o_MT"], c["xdt"], c["o_xdt"], c["xw"], c["o_xw"],
                                                            c["bt"], c["o_bt"])
                yd, o_yd = r_yd.next(); sts, o_sts = r_sts.next()
                by = psr.next()
                for h in range(8):
                    mm(by, ps[by][0:q, h * 64:(h + 1) * 64], MT[0:q, h, 0:q], xdt[0:q, h * 64:(h + 1) * 64], [o_MT, o_xdt])
                K.op("act", lambda e: e.activation(out=yd[0:q, :], in_=ps[by][0:q, :], func=AF.Copy), reads=[o_ps[by]], writes=[o_yd])
                bs = psr.next()
                for g in range(2):
                    mm(bs, ps[bs][:, g * 256:(g + 1) * 256], bt[0:q, g * 128:(g + 1) * 128], xw[0:q, g * 256:(g + 1) * 256], [o_bt, o_xw])
                K.op("act", lambda e: e.activation(out=sts[:, :], in_=ps[bs][:, :], func=AF.Copy), reads=[o_ps[bs]], writes=[o_sts])
                c.update(yd=yd, o_yd=o_yd, sts=sts, o_sts=o_sts)
                return c

            ywritten = [False] * nch

            def stage_b(c):
                d, ci = c["d"], c["ci"]
                q0, q = ch[ci]
                hst, o_h = hst2[d]
                hbf, o_hb = hbf2[d]
                E, o_E, cd, o_cd, yd, o_yd, sts, o_sts = c["E"], c["o_E"], c["cd"], c["o_cd"], c["yd"], c["o_yd"], c["sts"], c["o_sts"]
                first = not ywritten[ci]
                ywritten[ci] = True
                yo, o_yo = r_yo.next()
                bg = psr.next()
                for g in range(2):
                    mm(bg, ps[bg][0:q, g * 256:(g + 1) * 256], bcT[:, 2 + g, q0:q0 + q], hbf[:, g * 256:(g + 1) * 256], [o_bc, o_hb])
                K.op("dve", lambda e: e.tensor_tensor(out=v8(yo[0:q, :]), in0=v8(ps[bg][0:q, :]),
                                                      in1=E[0:q, :].unsqueeze(2).to_broadcast([q, 8, 64]), op=ALU.mult),
                     reads=[o_ps[bg], o_E], writes=[o_yo])
                if first:
                    K.op("dve", lambda e: e.tensor_tensor(out=Yp[0:q, ci, :], in0=yd[0:q, :], in1=yo[0:q, :], op=ALU.add),
                         reads=[o_yd, o_yo], writes=[o_Y[ci]])
                else:
                    K.op("pool", lambda e: e.tensor_tensor(out=yo[0:q, :], in0=yd[0:q, :], in1=yo[0:q, :], op=ALU.add),
                         reads=[o_yd, o_yo], writes=[o_yo])
                    K.op("pool", lambda e: e.tensor_tensor(out=Yp[0:q, ci, :], in0=Yp[0:q, ci, :], in1=yo[0:q, :], op=ALU.add),
                         reads=[o_Y[ci], o_yo], writes=[o_Y[ci]])
                K.op("dve", lambda e: e.tensor_tensor(out=v8(hst[:, :]), in0=v8(hst[:, :]),
                                                      in1=cd[:, :].unsqueeze(2).to_broadcast([128, 8, 64]), op=ALU.mult),
                     reads=[o_h, o_cd], writes=[o_h])
                K.op("dve", lambda e: e.tensor_tensor(out=hst[:, :], in0=hst[:, :], in1=sts[:, :], op=ALU.add),
                     reads=[o_h, o_sts], writes=[o_h])
                K.op("act", lambda e: e.activation(out=hbf[:, :], in_=hst[:, :], func=AF.Copy), reads=[o_h], writes=[o_hb])
                return None if first else ci

            def stage_c(ci):
                q0, q = ch[ci]
                if True:
                    yz, o_yz = r_yz.next(); sq, o_sq = r_sq.next(); rs, o_rs = r_rs.next(); ob, o_ob = r_ob.next()
                    btp = psr.next()
                    for j in range(4):
                        mm(btp, ps[btp][:, j * 128:j * 128 + q], Yp[0:q, ci, j * 128:(j + 1) * 128], identf[0:q, 0:q], [o_Y[ci], o_cm])
                    for j in range(4):
                        K.op("dve", lambda e: e.scalar_tensor_tensor(
                            out=yz[:, j, 0:q], in0=xsT[:, j, q0:q0 + q], scalar=ppc(l, P_SD + j),
                            in1=ps[btp][:, j * 128:j * 128 + q], op0=ALU.mult, op1=ALU.add),
                            reads=[o_xs, o_pp, o_ps[btp]], writes=[o_yz])
                    szc, o_sz = r_sz.next()
                    K.dma("sp", szc[:, :, 0:q], PR[R_SZ:R_SZ + 4, :, q0:q0 + q].rearrange("c p t -> p c t"),
                          reads=o_PR[R_SZ:R_SZ + 4], writes=[o_sz])
                    K.op("pool", lambda e: e.tensor_tensor(out=yz[:, :, 0:q], in0=yz[:, :, 0:q], in1=szc[:, :, 0:q], op=ALU.mult),
                         reads=[o_yz, o_sz], writes=[o_yz])
                    K.op("act", lambda e: e.activation(out=sq[:, :, 0:q], in_=yz[:, :, 0:q], func=AF.Square), reads=[o_yz], writes=[o_sq])
                    bn = psr.next()
                    for g in range(2):
                        for jj in range(2):
                            mm(bn, ps[bn][:, g * 128:g * 128 + q], onesb, sq[:, 2 * g + jj, 0:q], [o_cmb, o_sq],
                               start=(jj == 0), stop=(jj == 1))
                    rstd_from(rs[:, :, 0:q], o_rs, ps[bn][:, 0:256].rearrange("p (g s) -> p g s", g=2)[:, :, 0:q], o_ps[bn], 1.0 / 256)
                    for j in range(4):
                        K.op("dve", lambda e: e.scalar_tensor_tensor(
                            out=ob[:, j, 0:q], in0=yz[:, j, 0:q], scalar=ppc(l, P_SNW + j), in1=rs[:, j // 2, 0:q],
                            op0=ALU.mult, op1=ALU.mult), reads=[o_yz, o_pp, o_rs], writes=[o_ob])
                    K.dma("pool", BR[4:8, :, q0:q0 + q].rearrange("c p t -> p c t"), ob[:, :, 0:q],
                          reads=[o_ob], writes=o_BR[4:8])

            sched = []
            for i_ in range(nch):
                sched.append((0, i_))
                sched.append((1, nch - 1 - i_))
            pend = []
            pend_c = []

            def run_b():
                r_ = stage_b(pend.pop(0))
                if r_ is not None:
                    pend_c.append(r_)
                while len(pend_c) > 3:
                    stage_c(pend_c.pop(0))

            pend_a2 = []
            for (d, ci) in sched:
                pend_a2.append(stage_a(d, ci))
                if len(pend_a2) > 2:
                    pend.append(stage_a2(pend_a2.pop(0)))
                if len(pend) > 2:
                    run_b()
            while pend_a2:
                pend.append(stage_a2(pend_a2.pop(0)))
                if len(pend) > 2:
                    run_b()
            while pend:
                run_b()
            while pend_c:
                stage_c(pend_c.pop(0))
            K.barrier()

    def phase_fnet(l, L):
        T = L + 16
        ch = chunks_of(L)
        nch = len(ch)
        tb = tabs[L]
        with ExitStack() as st:
            ufT = sb("fn_u", [128, 4, T], BF16, st); o_u = Obj()
            Pcs = sb("fn_p", [128, nch, 2, 512], BF16, st); o_P = Obj()
            K.dma("sp", ufT[:], PR[R_FN:R_FN + 4, :, 0:T].rearrange("c p t -> p c t"), reads=o_PR[R_FN:R_FN + 4], writes=[o_u])
            for ci, (q0, q) in enumerate(ch):
                for gp in range(2):
                    b = psr.next()
                    for gg in range(2):
                        g = gp * 2 + gg
                        K.op("pe", lambda e: e.matmul(ps[b][0:q, gg * 256:(gg + 1) * 256], lhsT=ufT[:, g, q0:q0 + q], rhs=cs128,
                                                      start=True, stop=True), reads=[o_u, o_cmb], writes=[o_ps[b]])
                    eng = "act" if gp == 0 else "dve"
                    src = ps[b][0:q, :].rearrange("p (g s c) -> p s g c", g=2, s=2)
                    dst = Pcs[0:q, ci, :, gp * 256:(gp + 1) * 256].rearrange("p s (g c) -> p s g c", g=2)
                    if eng == "act":
                        K.op("act", lambda e: e.activation(out=dst, in_=src, func=AF.Copy), reads=[o_ps[b]], writes=[o_P])
                    else:
                        K.op("dve", lambda e: e.tensor_copy(out=dst, in_=src), reads=[o_ps[b]], writes=[o_P])
            tcr = Ring([(sb("fn_tc", [128, 512], BF16, st), Obj()) for _ in range(6)])
            tsr = Ring([(sb("fn_ts", [128, 512], BF16, st), Obj()) for _ in range(6)])
            obr = Ring([(sb("fn_ob", [128, 4, 512], BF16, st), Obj()) for _ in range(2)])
            for (f0, fw) in tiles_of(L):
                bs_ = [psr.next() for _ in range(4)]
                for ci, (q0, q) in enumerate(ch):
                    tc, o_tc = tcr.next()
                    ts, o_ts = tsr.next()
                    K.dma("sp", tc[0:q, 0:fw], tb["fc"][q0:q0 + q, f0:f0 + fw], writes=[o_tc])
                    K.dma("sp", ts[0:q, 0:fw], tb["fs"][q0:q0 + q, f0:f0 + fw], writes=[o_ts])
                    for g in range(4):
                        b = bs_[g]
                        K.op("pe", lambda e: e.matmul(ps[b][:, 0:fw], lhsT=Pcs[0:q, ci, 0, g * 128:(g + 1) * 128], rhs=tc[0:q, 0:fw],
                                                      start=(ci == 0), stop=False), reads=[o_P, o_tc], writes=[o_ps[b]])
                        K.op("pe", lambda e: e.matmul(ps[b][:, 0:fw], lhsT=Pcs[0:q, ci, 1, g * 128:(g + 1) * 128], rhs=ts[0:q, 0:fw],
                                                      start=False, stop=(ci == nch - 1)), reads=[o_P, o_ts], writes=[o_ps[b]])
                ob, o_ob = obr.next()
                for g in range(4):
                    b = bs_[g]
                    if g % 2 == 0:
                        K.op("act", lambda e: e.activation(out=ob[:, g, 0:fw], in_=ps[b][:, 0:fw], func=AF.Copy), reads=[o_ps[b]], writes=[o_ob])
                    else:
                        K.op("dve", lambda e: e.tensor_copy(out=ob[:, g, 0:fw], in_=ps[b][:, 0:fw]), reads=[o_ps[b]], writes=[o_ob])
                K.dma("pool", BR[0:4, :, f0:f0 + fw].rearrange("c p t -> p c t"), ob[:, :, 0:fw], reads=[o_ob], writes=o_BR[0:4])
            K.barrier()

    TWO_PI = 2.0 * math.pi

    def sin_act(dst, o_dst, a, o_a, ki, kf_, o_k, rows, q):
        K.op("dve", lambda e: e.tensor_scalar(out=ki[0:rows, 0:q], in0=a[0:rows, 0:q], scalar1=1.0 / TWO_PI, scalar2=None, op0=ALU.mult),
             reads=[o_a], writes=[o_k])
        K.op("dve", lambda e: e.tensor_copy(out=kf_[0:rows, 0:q], in_=ki[0:rows, 0:q]), reads=[o_k], writes=[o_k])
        K.op("dve", lambda e: e.scalar_tensor_tensor(out=a[0:rows, 0:q], in0=kf_[0:rows, 0:q], scalar=-TWO_PI, in1=a[0:rows, 0:q],
                                                     op0=ALU.mult, op1=ALU.add), reads=[o_k, o_a], writes=[o_a])
        K.op("dve", lambda e: e.tensor_scalar(out=a[0:rows, 0:q], in0=a[0:rows, 0:q], scalar1=-3.14159, scalar2=3.14159,
                                              op0=ALU.max, op1=ALU.min), reads=[o_a], writes=[o_a])
        K.op("act", lambda e: e.activation(out=dst[0:rows, 0:q], in_=a[0:rows, 0:q], func=AF.Sin), reads=[o_a], writes=[o_dst])

    def load_strip(strip, o_strip, tabb, fi):
        K.dma("sp", strip[:, :, :], tabb[fi], writes=[o_strip])

    o_KF = {}

    def phase_hfilter(l, L):
        T = L + 16
        ch = chunks_of(L)
        nch = len(ch)
        fch = fchunks_of(L)
        tb = tabs[L]
        o_KF[(L, l)] = Obj()
        with ExitStack() as st:
            w1s = sb("hf_w1", [33, 64], F32, st); w2s = sb("hf_w2", [64, 64], F32, st); w3s = sb("hf_w3", [64, 2048], F32, st)
            dls = sb("hf_dl", [128, 512], F32, st); ntt = sb("hf_ntt", [128, nch], F32, st); wfs = sb("hf_wf", [128, nch], F32, st)
            o_w = Obj()
            K.dma("sp", w1s[:], hw1[l], writes=[o_w]); K.dma("sp", w2s[:], hw2[l], writes=[o_w]); K.dma("sp", w3s[:], hw3[l], writes=[o_w])
            K.dma("sp", dls[:], dl_d, writes=[o_w]); K.dma("sp", ntt[:], tb["ntt"], writes=[o_w]); K.dma("sp", wfs[:], tb["wf"], writes=[o_w])
            S = sb("hf_S", [128, nch, 512], BF16, st); o_S = Obj()
            Dd = sb("hf_D", [128, nch, 512], BF16, st); o_D = Obj()
            rl1 = sb("hf_rl1", [128, 512], F32, st); o_rl = Obj()
            h2all = sb("hf_h2all", [64, T], F32, st); o_h2c = [Obj() for _ in ch]

            def ring(name, shape, dt, n=2):
                return Ring([(sb(name, shape, dt, st), Obj()) for _ in range(n)])
            r_z = ring("hf_z", [33, 128], F32); r_a = ring("hf_a", [64, 128], F32); r_h = ring("hf_h", [64, 128], F32)
            r_ki = ring("hf_ki", [64, 128], mybir.dt.int32); r_kf = ring("hf_kf", [64, 128], F32)
            r_wn = ring("hf_wn", [128, 512], F32); r_hf = ring("hf_hf", [128, 512], F32); r_hb = ring("hf_hb", [128, 512], F32)
            r_af = ring("hf_af", [128, 512], BF16); r_ab = ring("hf_ab", [128, 512], BF16)
            r_st = ring("hf_strip", [128, nch, 128], BF16, 4)
            r_kt = ring("hf_kt", [128, 2, 512], F32)
            for o in range(2):
                for ci, (q0, q) in enumerate(ch):
                    h2, o_h2 = h2all[:, q0:q0 + q], o_h2c[ci]
                    if o == 0:
                      z, o_z = r_z.next(); a, o_a = r_a.next(); h1, o_h1 = r_h.next()
                      ki, o_k = r_ki.next(); kf_, _ = r_kf.next()
                      K.dma("sp", z[0:33, 0:q], tb["zt"][:, q0:q0 + q], writes=[o_z])
                      b = psr.next()
                      K.op("pe", lambda e: e.matmul(ps[b][0:64, 0:q], lhsT=w1s[0:33, 0:64], rhs=z[0:33, 0:q], start=True, stop=True),
                           reads=[o_w, o_z], writes=[o_ps[b]])
                      K.op("dve", lambda e: e.tensor_scalar(out=a[0:64, 0:q], in0=ps[b][0:64, 0:q], scalar1=ppc(l, P_HB1, rows=64),
                                                            scalar2=ppc(l, P_HFQ, rows=64), op0=ALU.add, op1=ALU.mult),
                           reads=[o_ps[b], o_pp], writes=[o_a])
                      sin_act(h1, o_h1, a, o_a, ki, kf_, o_k, 64, q)
                      b = psr.next()
                      K.op("pe", lambda e: e.matmul(ps[b][0:64, 0:q], lhsT=w2s[0:64, 0:64], rhs=h1[0:64, 0:q], start=True, stop=True),
                           reads=[o_w, o_h1], writes=[o_ps[b]])
                      K.op("dve", lambda e: e.tensor_scalar(out=a[0:64, 0:q], in0=ps[b][0:64, 0:q], scalar1=ppc(l, P_HB2, rows=64),
                                                            scalar2=ppc(l, P_HFQ, rows=64), op0=ALU.add, op1=ALU.mult),
                           reads=[o_ps[b], o_pp], writes=[o_a])
                      sin_act(h2, o_h2, a, o_a, ki, kf_, o_k, 64, q)
                    bf_ = psr.next(); bb_ = psr.next()
                    K.op("pe", lambda e: e.matmul(ps[bf_][0:q, :], lhsT=h2[0:64, 0:q], rhs=w3s[0:64, o * 1024:o * 1024 + 512], start=True, stop=True),
                         reads=[o_w, o_h2], writes=[o_ps[bf_]])
                    K.op("pe", lambda e: e.matmul(ps[bb_][0:q, :], lhsT=h2[0:64, 0:q], rhs=w3s[0:64, o * 1024 + 512:o * 1024 + 1024], start=True, stop=True),
                         reads=[o_w, o_h2], writes=[o_ps[bb_]])
                    wn, o_wn = r_wn.next(); hf, o_hf = r_hf.next(); hb, o_hb = r_hb.next(); af, o_af = r_af.next(); ab, o_ab = r_ab.next()
                    K.op("act", lambda e: e.activation(out=wn[0:q, :], in_=dls[0:q, :], func=AF.Exp, scale=ntt[0:q, ci:ci + 1]),
                         reads=[o_w], writes=[o_wn])
                    K.op("dve", lambda e: e.tensor_tensor(out=hf[0:q, :], in0=ps[bf_][0:q, :], in1=wn[0:q, :], op=ALU.mult),
                         reads=[o_ps[bf_], o_wn], writes=[o_hf])
                    K.op("dve", lambda e: e.tensor_tensor(out=hb[0:q, :], in0=ps[bb_][0:q, :], in1=wn[0:q, :], op=ALU.mult),
                         reads=[o_ps[bb_], o_wn], writes=[o_hb])
                    if ci == 0:
                        K.op("dve", lambda e: e.memset(hb[0:1, :], 0.0), writes=[o_hb])
                    K.op("pool", lambda e: e.tensor_tensor(out=S[0:q, ci, :], in0=hf[0:q, :], in1=hb[0:q, :], op=ALU.add),
                         reads=[o_hf, o_hb], writes=[o_S])
                    K.op("pool", lambda e: e.tensor_tensor(out=Dd[0:q, ci, :], in0=hb[0:q, :], in1=hf[0:q, :], op=ALU.subtract),
                         reads=[o_hf, o_hb], writes=[o_D])
                    K.op("act", lambda e: e.activation(out=af[0:q, :], in_=hf[0:q, :], func=AF.Abs), reads=[o_hf], writes=[o_af])
                    K.op("act", lambda e: e.activation(out=ab[0:q, :], in_=hb[0:q, :], func=AF.Abs), reads=[o_hb], writes=[o_ab])
                    K.op("pe", lambda e: e.matmul(ps[7][:, :], lhsT=onesb[0:q, :], rhs=af[0:q, :], start=(ci == 0), stop=False),
                         reads=[o_cmb, o_af], writes=[o_ps[7]])
                    K.op("pe", lambda e: e.matmul(ps[7][:, :], lhsT=onesb[0:q, :], rhs=ab[0:q, :], start=False, stop=(ci == nch - 1)),
                         reads=[o_cmb, o_ab], writes=[o_ps[7]])
                K.mark('hf_mlp')
                K.op("dve", lambda e: e.reciprocal(out=rl1[:], in_=ps[7][:, :]), reads=[o_ps[7]], writes=[o_rl])
                for fi, (f0, mf) in enumerate(fch):
                    sc_, o_sc = r_st.next(); ss_, o_ss = r_st.next()
                    load_strip(sc_, o_sc, tb["hcb"], fi)
                    load_strip(ss_, o_ss, tb["hsb"], fi)
                    bre = psr.next(); bim = psr.next()
                    for ci, (q0, q) in enumerate(ch):
                        K.op("pe", lambda e: e.matmul(ps[bre][0:mf, :], lhsT=sc_[0:q, ci, 0:mf], rhs=S[0:q, ci, :], start=(ci == 0), stop=(ci == nch - 1)),
                             reads=[o_sc, o_S], writes=[o_ps[bre]])
                    for ci, (q0, q) in enumerate(ch):
                        K.op("pe", lambda e: e.matmul(ps[bim][0:mf, :], lhsT=ss_[0:q, ci, 0:mf], rhs=Dd[0:q, ci, :], start=(ci == 0), stop=(ci == nch - 1)),
                             reads=[o_ss, o_D], writes=[o_ps[bim]])
                    kt, o_kt = r_kt.next()
                    K.op("dve", lambda e: e.scalar_tensor_tensor(out=kt[0:mf, 0, :], in0=ps[bre][0:mf, :], scalar=wfs[0:mf, fi:fi + 1], in1=rl1[0:mf, :],
                                                                 op0=ALU.mult, op1=ALU.mult), reads=[o_ps[bre], o_w, o_rl], writes=[o_kt])
                    K.op("dve", lambda e: e.scalar_tensor_tensor(out=kt[0:mf, 1, :], in0=ps[bim][0:mf, :], scalar=wfs[0:mf, fi:fi + 1], in1=rl1[0:mf, :],
                                                                 op0=ALU.mult, op1=ALU.mult), reads=[o_ps[bim], o_w, o_rl], writes=[o_kt])
                    K.dma("pool", tb["kf"][l][fi, 0:mf, o, :, :], kt[0:mf, :, :], reads=[o_kt], writes=[o_KF[(L, l)]])
            K.barrier()

    def phase_hyena(l, L):
        T = L + 16
        ch = chunks_of(L)
        nch = len(ch)
        fch = fchunks_of(L)
        nfc = len(fch)
        tl = tiles_of(L)
        tb = tabs[L]
        okf = o_KF[(L, l)]
        with ExitStack() as st:
            inT = sb("hy_in", [128, 4, T], BF16, st); o_in = Obj()
            itok = sb("hy_tok", [128, nch, 512], BF16, st); o_tok = Obj()
            PQ = sb("hy_pq", [128, nfc, 2, 512], BF16, st); o_PQ = Obj()

            def ring(name, shape, dt, n=2):
                return Ring([(sb(name, shape, dt, st), Obj()) for _ in range(n)])
            r_st = ring("hy_strip", [128, nch, 128], BF16, 3)
            r_kt = ring("hy_kt", [128, 2, 512], F32)
            r_t = ring("hy_t", [128, 512], F32, 4)
            r_tc = ring("hy_tc", [128, 512], BF16, 4); r_ts = ring("hy_ts", [128, 512], BF16, 4)
            r_xt = ring("hy_xt", [128, 4, 512], BF16); r_ob = ring("hy_ob", [128, 4, 512], BF16)
            K.dma("sp", inT[:], PR[R_V:R_V + 4, :, 0:T].rearrange("c p t -> p c t"), reads=o_PR[R_V:R_V + 4], writes=[o_in])
            for o in range(2):
                for ci, (q0, q) in enumerate(ch):
                    b = psr.next()
                    for c in range(4):
                        K.op("pe", lambda e: e.matmul(ps[b][0:q, c * 128:(c + 1) * 128], lhsT=inT[:, c, q0:q0 + q], rhs=identb, start=True, stop=True),
                             reads=[o_in, o_cmb], writes=[o_ps[b]])
                    if ci % 2 == 0:
                        K.op("act", lambda e: e.activation(out=itok[0:q, ci, :], in_=ps[b][0:q, :], func=AF.Copy), reads=[o_ps[b]], writes=[o_tok])
                    else:
                        K.op("dve", lambda e: e.tensor_copy(out=itok[0:q, ci, :], in_=ps[b][0:q, :]), reads=[o_ps[b]], writes=[o_tok])
                K.mark('hy_tr')
                for fi, (f0, mf) in enumerate(fch):
                    sc_, o_sc = r_st.next(); ss_, o_ss = r_st.next()
                    load_strip(sc_, o_sc, tb["hcb"], fi)
                    load_strip(ss_, o_ss, tb["hsb"], fi)
                    kt, o_kt = r_kt.next()
                    K.dma("sp", kt[0:mf, :, :], tb["kf"][l][fi, 0:mf, o, :, :], reads=[okf], writes=[o_kt])
                    bA = psr.next(); bB = psr.next()
                    for ci, (q0, q) in enumerate(ch):
                        K.op("pe", lambda e: e.matmul(ps[bA][0:mf, :], lhsT=sc_[0:q, ci, 0:mf], rhs=itok[0:q, ci, :], start=(ci == 0), stop=(ci == nch - 1)),
                             reads=[o_sc, o_tok], writes=[o_ps[bA]])
                    for ci, (q0, q) in enumerate(ch):
                        K.op("pe", lambda e: e.matmul(ps[bB][0:mf, :], lhsT=ss_[0:q, ci, 0:mf], rhs=itok[0:q, ci, :], start=(ci == 0), stop=(ci == nch - 1)),
                             reads=[o_ss, o_tok], writes=[o_ps[bB]])
                    t1, o_t1 = r_t.next(); t2, o_t2 = r_t.next(); t3, o_t3 = r_t.next(); t4, o_t4 = r_t.next()
                    K.op("dve", lambda e: e.tensor_tensor(out=t1[0:mf, :], in0=ps[bA][0:mf, :], in1=kt[0:mf, 0, :], op=ALU.mult), reads=[o_ps[bA], o_kt], writes=[o_t1])
                    K.op("dve", lambda e: e.tensor_tensor(out=t2[0:mf, :], in0=ps[bB][0:mf, :], in1=kt[0:mf, 1, :], op=ALU.mult), reads=[o_ps[bB], o_kt], writes=[o_t2])
                    K.op("dve", lambda e: e.tensor_tensor(out=t3[0:mf, :], in0=ps[bB][0:mf, :], in1=kt[0:mf, 0, :], op=ALU.mult), reads=[o_ps[bB], o_kt], writes=[o_t3])
                    K.op("dve", lambda e: e.tensor_tensor(out=t4[0:mf, :], in0=ps[bA][0:mf, :], in1=kt[0:mf, 1, :], op=ALU.mult), reads=[o_ps[bA], o_kt], writes=[o_t4])
                    K.op("pool", lambda e: e.tensor_tensor(out=PQ[0:mf, fi, 0, :], in0=t1[0:mf, :], in1=t2[0:mf, :], op=ALU.add), reads=[o_t1, o_t2], writes=[o_PQ])
                    K.op("pool", lambda e: e.tensor_tensor(out=PQ[0:mf, fi, 1, :], in0=t3[0:mf, :], in1=t4[0:mf, :], op=ALU.subtract), reads=[o_t3, o_t4], writes=[o_PQ])
                K.mark('hy_fwd')
                for (t0, tw) in tl:
                    bs_ = [psr.next() for _ in range(4)]
                    for fi, (f0, mf) in enumerate(fch):
                        tc, o_tc = r_tc.next(); ts, o_ts = r_ts.next()
                        K.dma("sp", tc[0:mf, 0:tw], tb["hc"][f0:f0 + mf, t0:t0 + tw], writes=[o_tc])
                        K.dma("sp", ts[0:mf, 0:tw], tb["hs"][f0:f0 + mf, t0:t0 + tw], writes=[o_ts])
                        for c in range(4):
                            b = bs_[c]
                            K.op("pe", lambda e: e.matmul(ps[b][:, 0:tw], lhsT=PQ[0:mf, fi, 0, c * 128:(c + 1) * 128], rhs=tc[0:mf, 0:tw],
                                                          start=(fi == 0), stop=False), reads=[o_PQ, o_tc], writes=[o_ps[b]])
                            K.op("pe", lambda e: e.matmul(ps[b][:, 0:tw], lhsT=PQ[0:mf, fi, 1, c * 128:(c + 1) * 128], rhs=ts[0:mf, 0:tw],
                                                          start=False, stop=(fi == nfc - 1)), reads=[o_PQ, o_ts], writes=[o_ps[b]])
                    xt, o_xt = r_xt.next()
                    r0 = R_X1 if o == 0 else R_X2
                    K.dma("sp", xt[:, :, 0:tw], PR[r0:r0 + 4, :, t0:t0 + tw].rearrange("c p t -> p c t"), reads=o_PR[r0:r0 + 4], writes=[o_xt])
                    ob, o_ob = r_ob.next()
                    for c in range(4):
                        b = bs_[c]
                        tt_, o_tt = r_t.next()
                        K.op("dve", lambda e: e.scalar_tensor_tensor(out=tt_[:, 0:tw], in0=inT[:, c, t0:t0 + tw], scalar=ppc(l, P_HB + o * 4 + c),
                                                                     in1=ps[b][:, 0:tw], op0=ALU.mult, op1=ALU.add),
                             reads=[o_in, o_pp, o_ps[b]], writes=[o_tt])
                        if o == 0:
                            K.op("pool", lambda e: e.tensor_tensor(out=inT[:, c, t0:t0 + tw], in0=tt_[:, 0:tw], in1=xt[:, c, 0:tw], op=ALU.mult),
                                 reads=[o_tt, o_xt], writes=[o_in])
                        else:
                            K.op("pool", lambda e: e.tensor_tensor(out=ob[:, c, 0:tw], in0=tt_[:, 0:tw], in1=xt[:, c, 0:tw], op=ALU.mult),
                                 reads=[o_tt, o_xt], writes=[o_ob])
                    if o == 1:
                        K.dma("pool", BR[8:12, :, t0:t0 + tw].rearrange("c p t -> p c t"), ob[:, :, 0:tw], reads=[o_ob], writes=o_BR[8:12])
                K.mark('hy_inv')
            K.barrier()

    def load_w_into(dst3, o_dst, src_ap, kc, ncols, bufs):
        stg, o_stg = bufs.next()
        K.dma("sp", stg[:, 0:kc, 0:ncols], src_ap.rearrange("(k p) m -> p k m", p=128), writes=[o_stg])
        K.op("dve", lambda e: e.tensor_copy(out=dst3, in_=stg[:, 0:kc, 0:ncols]), reads=[o_stg], writes=[o_dst])

    def phase_merge(l, L):
        T = L + 16
        tl = tiles_of(L)
        with ExitStack() as st:
            wbr = sb("mg_wbr", [128, 16, 1024], BF16, st); o_wbr_g = [Obj() for _ in range(4)]
            wo = sb("mg_wo", [128, 8, 1024], BF16, st); o_wo_g = [Obj() for _ in range(4)]
            stgr = Ring([(sb("mg_stg", [128, 8, 256], F32, st), Obj()) for _ in range(2)])
            wbv = w_branch[l].rearrange("k r m -> (k r) m")
            for cg in range(4):
                for kg in range(2):
                    load_w_into(wbr[:, kg * 8:(kg + 1) * 8, cg * 256:(cg + 1) * 256], o_wbr_g[cg],
                                wbv[kg * 1024:(kg + 1) * 1024, cg * 256:(cg + 1) * 256], 8, 256, stgr)
            for cg in range(4):
                load_w_into(wo[:, :, cg * 256:(cg + 1) * 256], o_wo_g[cg], w_out[l, :, cg * 256:(cg + 1) * 256], 8, 256, stgr)
            brt2 = [[(sb("mg_br", [128, 4, 512], BF16, st), Obj()) for _ in range(4)] for _ in range(2)]
            gt = [(sb("mg_g", [128, 8, 512], BF16, st), Obj()) for _ in range(4)]
            r_acc = Ring([(sb("mg_acc", [128, 512], F32, st), Obj()) for _ in range(2)])
            r_tmp = Ring([(sb("mg_tmp", [128, 512], F32, st), Obj()) for _ in range(2)])
            r_mg = Ring([(sb("mg_m", [128, 8, 512], BF16, st), Obj()) for _ in range(2)])
            r_x = Ring([(sb("mg_x", [128, 8, 512], F32, st), Obj()) for _ in range(2)])
            for tix, (t0, tw) in enumerate(tl):
                brt = brt2[tix % 2]
                for k in range(4):
                    K.dma("sp", brt[k][0][:, :, 0:tw], BR[4 * k:4 * k + 4, :, t0:t0 + tw].rearrange("c p t -> p c t"),
                          reads=o_BR[4 * k:4 * k + 4], writes=[brt[k][1]])
                    K.dma("sp", gt[k][0][:, :, 0:tw], PR[R_G + 8 * k:R_G + 8 * k + 8, :, t0:t0 + tw].rearrange("c p t -> p c t"),
                          reads=o_PR[R_G + 8 * k:R_G + 8 * k + 8], writes=[gt[k][1]])
                xr, o_xr = r_x.next()
                K.dma("sp", xr[:, :, 0:tw], XT[:, :, t0:t0 + tw].rearrange("c p t -> p c t"), reads=xto(t0), writes=[o_xr])
                mg, o_mg = r_mg.next()
                for m in range(8):
                    acc, o_acc = r_acc.next()
                    for k in range(4):
                        b = psr.next()
                        for kc in range(4):
                            K.op("pe", lambda e: e.matmul(ps[b][:, 0:tw], lhsT=wbr[:, k * 4 + kc, m * 128:(m + 1) * 128],
                                                          rhs=brt[k][0][:, kc, 0:tw], start=(kc == 0), stop=(kc == 3)),
                                 reads=[o_wbr_g[m // 2], brt[k][1]], writes=[o_ps[b]])
                        if k == 0:
                            K.op("dve", lambda e: e.tensor_tensor(out=acc[:, 0:tw], in0=ps[b][:, 0:tw], in1=gt[k][0][:, m, 0:tw], op=ALU.mult),
                                 reads=[o_ps[b], gt[k][1]], writes=[o_acc])
                        else:
                            tmp, o_tmp = r_tmp.next()
                            K.op("dve", lambda e: e.tensor_tensor(out=tmp[:, 0:tw], in0=ps[b][:, 0:tw], in1=gt[k][0][:, m, 0:tw], op=ALU.mult),
                                 reads=[o_ps[b], gt[k][1]], writes=[o_tmp])
                            if k < 3:
                                K.op("dve", lambda e: e.tensor_tensor(out=acc[:, 0:tw], in0=acc[:, 0:tw], in1=tmp[:, 0:tw], op=ALU.add),
                                     reads=[o_acc, o_tmp], writes=[o_acc])
                            else:
                                K.op("dve", lambda e: e.tensor_tensor(out=mg[:, m, 0:tw], in0=acc[:, 0:tw], in1=tmp[:, 0:tw], op=ALU.add),
                                     reads=[o_acc, o_tmp], writes=[o_mg])
                for m2 in range(8):
                    b = psr.next()
                    for kc in range(8):
                        K.op("pe", lambda e: e.matmul(ps[b][:, 0:tw], lhsT=wo[:, kc, m2 * 128:(m2 + 1) * 128], rhs=mg[:, kc, 0:tw],
                                                      start=(kc == 0), stop=(kc == 7)), reads=[o_wo_g[m2 // 2], o_mg], writes=[o_ps[b]])
                    K.op("dve", lambda e: e.tensor_tensor(out=xr[:, m2, 0:tw], in0=xr[:, m2, 0:tw], in1=ps[b][:, 0:tw], op=ALU.add),
                         reads=[o_xr, o_ps[b]], writes=[o_xr])
                K.dma("pool", XT[:, :, t0:t0 + tw].rearrange("c p t -> p c t"), xr[:, :, 0:tw], reads=[o_xr], writes=xto(t0))
            K.barrier()

    def phase_ffn(l, L):
        T = L + 16
        tl = tiles_of(L)
        with ExitStack() as st:
            HT = sb("ff_ht", [128, 8, T], BF16, st); o_HTt = [Obj() for _ in tl]
            wbufs = dict(stg=Ring([(sb("ff_stg", [128, 8, 256], F32, st), Obj()) for _ in range(4)]),
                         wb=Ring([(sb("ff_wb", [128, 8, 256], BF16, st), Obj()) for _ in range(4)]))
            plan = []
            for jp_ in range(11):
                plan.append((w_up[l, :, jp_ * 256:(jp_ + 1) * 256], 8, 256))
                plan.append((w_up[l, :, DFF + jp_ * 256:DFF + (jp_ + 1) * 256], 8, 256))
            ws = WStream(wbufs, plan, 2)
            ws.prime()
            rmsnorm_to(HT, o_HTt, l, P_NFFN, tl, keep=(st if L == 2048 else None))
            K.mark('ff_norm')
            rows = Ring([(sb("ff_row", [128, T + 2], F32, st), Obj()) for _ in range(4)])
            outs = Ring([(sb("ff_out", [128, T], BF16, st), Obj()) for _ in range(2)])
            for (rw, o_rw) in rows.items:
                K.op("pool", lambda e: e.memset(rw[:, 0:1], 0.0), writes=[o_rw])
                K.op("pool", lambda e: e.memset(rw[:, T + 1:T + 2], 0.0), writes=[o_rw])
            flip = [0]

            def gemm_row(wb, o_wb, c0, dst, o_dst):
                for ti, (t0, tw) in enumerate(tl):
                    b = psr.next()
                    for kc in range(8):
                        K.op("pe", lambda e: e.matmul(ps[b][:, 0:tw], lhsT=wb[:, kc, c0:c0 + 128], rhs=HT[:, kc, t0:t0 + tw],
                                                      start=(kc == 0), stop=(kc == 7)), reads=[o_wb, o_HTt[ti]], writes=[o_ps[b]])
                    d = dst[:, 1 + t0:1 + t0 + tw]
                    flip[0] ^= 1
                    if flip[0]:
                        K.op("act", lambda e: e.activation(out=d, in_=ps[b][:, 0:tw], func=AF.Copy), reads=[o_ps[b]], writes=[o_dst])
                    else:
                        K.op("dve", lambda e: e.tensor_copy(out=d, in_=ps[b][:, 0:tw]), reads=[o_ps[b]], writes=[o_dst])

            for jp in range(11):
                wa, o_wa = ws.get(2 * jp)
                wv, o_wv = ws.get(2 * jp + 1)
                for jj in range(2):
                    j = jp * 2 + jj
                    ra, o_ra = rows.next(); ta, o_ta = rows.next(); rv, o_rv = rows.next(); tv, o_tv = rows.next()
                    gemm_row(wa, o_wa, jj * 128, ra, o_ra)
                    gemm_row(wv, o_wv, jj * 128, rv, o_rv)
                    conv3(ta[:, 1:T + 1], o_ta, ra, o_ra, l, P_FCW + 3 * j, T)
                    conv3(tv[:, 1:T + 1], o_tv, rv, o_rv, l, P_FCW + 3 * (22 + j), T)
                    K.op("act", lambda e: e.activation(out=ta[:, 1:T + 1], in_=ta[:, 1:T + 1], func=AF.Silu), reads=[o_ta], writes=[o_ta])
                    ob, o_ob = outs.next()
                    K.op("dve", lambda e: e.tensor_tensor(out=ob[:, 0:T], in0=ta[:, 1:T + 1], in1=tv[:, 1:T + 1], op=ALU.mult),
                         reads=[o_ta, o_tv], writes=[o_ob])
                    K.dma("pool", FA[j][:, 0:T], ob[:, 0:T], reads=[o_ob], writes=[o_FA[j]])
            K.barrier()
        with ExitStack() as st:
            K.mark('ff_up')
            wd = sb("ff_wd", [128, 22, 1024], BF16, st); o_wd_g = [Obj() for _ in range(4)]
            stgr = Ring([(sb("ff_stg2", [128, 8, 256], F32, st), Obj()) for _ in range(2)])
            for cg in range(4):
                for (k0, kn) in ((0, 8), (8, 8), (16, 6)):
                    load_w_into(wd[:, k0:k0 + kn, cg * 256:(cg + 1) * 256], o_wd_g[cg],
                                w_down[l, k0 * 128:(k0 + kn) * 128, cg * 256:(cg + 1) * 256], kn, 256, stgr)
            r_fa = Ring([(sb("ff_fa", [128, 22, 512], BF16, st), Obj()) for _ in range(2)])
            r_x = Ring([(sb("ff_x", [128, 8, 512], F32, st), Obj()) for _ in range(2)])
            for (t0, tw) in tl:
                fa, o_fa = r_fa.next()
                xr, o_xr = r_x.next()
                K.dma("sp", fa[:, :, 0:tw], FA[:, :, t0:t0 + tw].rearrange("c p t -> p c t"), reads=o_FA, writes=[o_fa])
                K.dma("sp", xr[:, :, 0:tw], XT[:, :, t0:t0 + tw].rearrange("c p t -> p c t"), reads=xto(t0), writes=[o_xr])
                for m in range(8):
                    b = psr.next()
                    for kc in range(22):
                        K.op("pe", lambda e: e.matmul(ps[b][:, 0:tw], lhsT=wd[:, kc, m * 128:(m + 1) * 128], rhs=fa[:, kc, 0:tw],
                                                      start=(kc == 0), stop=(kc == 21)), reads=[o_wd_g[m // 2], o_fa], writes=[o_ps[b]])
                    K.op("dve", lambda e: e.tensor_tensor(out=xr[:, m, 0:tw], in0=xr[:, m, 0:tw], in1=ps[b][:, 0:tw], op=ALU.add),
                         reads=[o_xr, o_ps[b]], writes=[o_xr])
                K.dma("pool", XT[:, :, t0:t0 + tw].rearrange("c p t -> p c t"), xr[:, :, 0:tw], reads=[o_xr], writes=xto(t0))
            K.barrier()

    seqs = [(xp, yp, i, 2048) for i in range(nP)] + [(xs, ys, i, 4096) for i in range(nS)]
    for (x_d, y_d, si, L) in seqs:
        phase_input(x_d, si, L)
        K.barrier()
        nch = 1 + L // 128
        with ExitStack() as sq:
            dtS = sb("dtS", [128, nch, 16], F32, sq)
            dtA = sb("dtA", [128, nch, 16], F32, sq)
            o_dt = Obj()
            for l in range(n_layers):
                phase1(l, L, dtS, dtA, o_dt)
                K.mark('phase1')
                if stop_after == "p1":
                    break
                if stop_after != "nossd":
                    phase_ssd(l, L, dtS, dtA, o_dt)
                    K.mark('phase_ssd')
                if stop_after == "ssd":
                    break
                phase_fnet(l, L)
                K.mark('phase_fnet')
                if (L, l) not in o_KF:
                    phase_hfilter(l, L)
                    K.mark('phase_hfilter')
                phase_hyena(l, L)
                K.mark('phase_hyena')
                if stop_after in ("hy", "nossd"):
                    break
                phase_merge(l, L)
                K.mark('phase_merge')
                if stop_after == "mix":
                    break
                phase_ffn(l, L)
                K.mark('phase_ffn')
            if debug:
                dbg = dscr("dbg_dt", [128, nch, 32], F32)
                K.dma("pool", dbg[:, :, 0:16], dtS[:], reads=[o_dt])
                K.dma("pool", dbg[:, :, 16:32], dtA[:], reads=[o_dt])
            K.barrier()
        phase_final(y_d, si, L)
        K.barrier()

    K.barrier()
    es.close()
    return nc, K


_BF = ml_dtypes.bfloat16


def _const_tables(L):
    T = L + 16
    N = 2 * T
    nch = 1 + L // 128
    a = np.arange(T + 1, dtype=np.int64)
    m = (a[:, None] * a[None, :]) % N
    ang = (2.0 * np.pi / N) * m.astype(np.float64)
    hc = np.cos(ang).astype(np.float32).astype(_BF)
    hs = np.sin(ang).astype(np.float32).astype(_BF)
    t = np.arange(T, dtype=np.int64)
    m2 = (t[:, None] * t[None, :]) % T
    ang2 = (2.0 * np.pi / T) * m2.astype(np.float64)
    sc = 1.0 / math.sqrt(T)
    fc = (np.cos(ang2) * sc).astype(np.float32).astype(_BF)
    fs = (-np.sin(ang2) * sc).astype(np.float32).astype(_BF)
    tt = np.linspace(0.0, 1.0, T, dtype=np.float32)[:, None]
    bands = 16
    w = (2.0 * math.pi / T) * np.arange(T, dtype=np.float32)[:, None]
    fr = np.linspace(1e-4, bands - 1, bands, dtype=np.float32)[None, :]
    z = np.concatenate([tt, np.cos(fr * w), -np.sin(fr * w)], axis=-1).astype(np.float32)
    zt = np.ascontiguousarray(z.T)
    ntt = np.zeros((128, nch), np.float32)
    wf = np.zeros((128, nch), np.float32)
    for ci, (t0, q) in enumerate(chunks_of(L)):
        ntt[0:q, ci] = -tt[t0:t0 + q, 0]
    wfull = np.full(T + 1, 2.0 / N, np.float32)
    wfull[0] = 1.0 / N
    wfull[T] = 1.0 / N
    for ci, (f0, q) in enumerate(fchunks_of(L)):
        wf[0:q, ci] = wfull[f0:f0 + q]
    tidx = np.full((nch, 128), T + 1, np.int64)
    fidx = np.full((nch, 128), T + 1, np.int64)
    for ci, (t0, q) in enumerate(chunks_of(L)):
        tidx[ci, 0:q] = np.arange(t0, t0 + q)
    for ci, (f0, q) in enumerate(fchunks_of(L)):
        fidx[ci, 0:q] = np.arange(f0, f0 + q)

    def blocked(tab):
        pad = np.zeros((T + 2, T + 2), tab.dtype)
        pad[0:T + 1, 0:T + 1] = tab
        out = np.empty((nch, 128, nch, 128), tab.dtype)
        for fi in range(nch):
            out[fi] = pad[tidx.T[:, :, None], fidx[fi][None, None, :]]
        return out
    return dict(fc=fc, fs=fs, hc=hc, hs=hs, hcb=blocked(hc), hsb=blocked(hs), zt=zt, ntt=ntt, wf=wf)


def _const_common():
    j = np.arange(128)
    cm = np.zeros((128, NCM), np.float32)
    cm[:, CM_ID:CM_ID + 128] = np.eye(128)
    cm[:, CM_LE:CM_LE + 128] = (j[:, None] <= j[None, :])
    cm[:, CM_GE:CM_GE + 128] = (j[:, None] >= j[None, :])
    cm[:, CM_GT:CM_GT + 128] = (j[:, None] > j[None, :])
    cm[:, CM_LT:CM_LT + 128] = (j[:, None] < j[None, :])
    cm[:, CM_ONE:CM_ONE + 128] = 1.0
    cmb = np.zeros((128, 512), np.float32)
    cmb[:, 0:128] = np.eye(128)
    cmb[:, 128:256] = 1.0
    ang = 2.0 * np.pi * ((j[:, None] * j[None, :]) % 128) / 128.0
    cmb[:, 256:384] = np.cos(ang) / math.sqrt(128.0)
    cmb[:, 384:512] = np.sin(ang) / math.sqrt(128.0)
    max_decay = math.log(1e-2) / 0.3
    min_decay = math.log(1e-2) / 1.5
    deltas = np.abs(np.linspace(min_decay, max_decay, BW, dtype=np.float32))
    dl = np.ascontiguousarray(np.broadcast_to(deltas[None, :], (128, BW))).astype(np.float32)
    return dict(cm=cm, cmb=cmb.astype(_BF), dl=dl)


def _pack_params(inp):
    pp = np.zeros((2, 128, NPP), np.float32)
    pb = np.zeros((2, 128, 32), np.float32)

    def cols(v):
        return np.ascontiguousarray(np.asarray(v, np.float32).reshape(-1, 128).T)

    for l in range(2):
        pp[l, :, P_NMIX:P_NMIX + 8] = cols(inp["norm_mix"][l])
        pp[l, :, P_NFFN:P_NFFN + 8] = cols(inp["norm_ffn"][l])
        pp[l, :, P_NFIN:P_NFIN + 8] = cols(inp["norm_final"])
        w = np.asarray(inp["ssm_conv_w"][l], np.float32)
        pp[l, :, P_SCW:P_SCW + 24] = w.reshape(3, 8, 128).transpose(2, 1, 0).reshape(128, 24)
        pp[l, :, P_SCB:P_SCB + 8] = cols(inp["ssm_conv_b"][l])
        pp[l, :, P_SD:P_SD + 4] = cols(np.repeat(np.asarray(inp["ssm_d"][l], np.float32), 64))
        pp[l, :, P_SNW:P_SNW + 4] = cols(inp["ssm_norm"][l])
        w = np.asarray(inp["hyena_conv_w"][l], np.float32)
        pp[l, :, P_HCW:P_HCW + 36] = w.reshape(3, 12, 128).transpose(2, 1, 0).reshape(128, 36)
        pp[l, :, P_HB:P_HB + 8] = cols(np.asarray(inp["hyena_bias"][l], np.float32).reshape(-1))
        w = np.asarray(inp["sc_conv_w"][l], np.float32)
        pp[l, :, P_CCW:P_CCW + 12] = w.reshape(3, 4, 128).transpose(2, 1, 0).reshape(128, 12)
        w = np.asarray(inp["ffn_conv_w"][l], np.float32)
        pp[l, :, P_FCW:P_FCW + 132] = w.reshape(3, 44, 128).transpose(2, 1, 0).reshape(128, 132)
        pp[l, 0:64, P_HB1] = np.asarray(inp["hyena_b1"][l], np.float32)
        pp[l, 0:64, P_HB2] = np.asarray(inp["hyena_b2"][l], np.float32)
        pp[l, 0:64, P_HFQ] = np.asarray(inp["hyena_freq"][l], np.float32)
        pb[l, :, 0:16] = np.broadcast_to(np.asarray(inp["ssm_dt_bias"][l], np.float32).reshape(1, 16), (128, 16))
        pb[l, :, 16:32] = np.broadcast_to(np.asarray(inp["ssm_a_log"][l], np.float32).reshape(1, 16), (128, 16))
    return pp, pb


def make_in_maps(inp, nP, nS, n_cores):
    common = _const_common()
    pp, pb = _pack_params(inp)
    base = dict(common)
    base["pp"] = pp
    base["pb"] = pb
    for k in ("meta_tokens", "w_in", "w_branch", "w_out", "w_up", "w_down", "hyena_w1", "hyena_w2", "hyena_w3"):
        base[k] = np.ascontiguousarray(np.asarray(inp[k], np.float32))
    for L in sorted(set(([2048] if nP else []) + ([4096] if nS else []))):
        for k, v in _const_tables(L).items():
            base["%s%d" % (k, L)] = v
    x_prompt = np.asarray(inp["x_prompt"], np.float32)
    x_sample = np.asarray(inp["x_sample"], np.float32)
    maps = []
    for c in range(n_cores):
        m = dict(base)
        m["xp"] = np.ascontiguousarray(x_prompt[c * nP:(c + 1) * nP]) if nP else np.zeros((1, 2048, D), np.float32)
        m["xs"] = np.ascontiguousarray(x_sample[c * nS:(c + 1) * nS]) if nS else np.zeros((1, 4096, D), np.float32)
        maps.append(m)
    return maps


def kernel(x_prompt, x_sample, meta_tokens, norm_mix, w_in, ssm_conv_w, ssm_conv_b, ssm_dt_bias,
           ssm_a_log, ssm_d, ssm_norm, hyena_conv_w, hyena_w1, hyena_b1, hyena_w2, hyena_b2, hyena_w3,
           hyena_freq, hyena_bias, sc_conv_w, w_branch, w_out, norm_ffn, ffn_conv_w, w_up, w_down,
           norm_final):
    inp = dict(x_prompt=x_prompt, x_sample=x_sample, meta_tokens=meta_tokens, norm_mix=norm_mix, w_in=w_in,
               ssm_conv_w=ssm_conv_w, ssm_conv_b=ssm_conv_b, ssm_dt_bias=ssm_dt_bias, ssm_a_log=ssm_a_log,
               ssm_d=ssm_d, ssm_norm=ssm_norm, hyena_conv_w=hyena_conv_w, hyena_w1=hyena_w1, hyena_b1=hyena_b1,
               hyena_w2=hyena_w2, hyena_b2=hyena_b2, hyena_w3=hyena_w3, hyena_freq=hyena_freq,
               hyena_bias=hyena_bias, sc_conv_w=sc_conv_w, w_branch=w_branch, w_out=w_out, norm_ffn=norm_ffn,
               ffn_conv_w=ffn_conv_w, w_up=w_up, w_down=w_down, norm_final=norm_final)
    nP, nS = 4, 1
    nc, _ = build_program(nP, nS)
    maps = make_in_maps(inp, nP, nS, 8)
    res = run_bass_kernel_spmd(nc, maps, core_ids=list(range(8)))
    y_p = np.concatenate([np.asarray(r["yp"], np.float32) for r in res.results], axis=0)
    y_s = np.concatenate([np.asarray(r["ys"], np.float32) for r in res.results], axis=0)
    return (y_p, y_s)
```

```python
import math
from contextlib import ExitStack

import numpy as np
import ml_dtypes

import concourse.bass as bass
import concourse.mybir as mybir
from concourse.bass_utils import run_bass_kernel_spmd

F32 = mybir.dt.float32
BF16 = mybir.dt.bfloat16
AF = mybir.ActivationFunctionType
ALU = mybir.AluOpType

D = 1024
NMETA = 16
EPS = 1e-6
BW = 512
DIN = 9232
DFF = 2816
TMAX = 4112
O_FN, O_Z, O_XBC, O_DT, O_HY, O_SC, O_G = 0, 512, 1024, 2048, 2064, 3600, 5136
R_FN, R_SZ, R_XS, R_B, R_C, R_V, R_X1, R_X2, R_G = 0, 4, 8, 12, 14, 16, 20, 24, 28
NPR = 60
P_NMIX, P_NFFN, P_NFIN, P_SCW, P_SCB, P_SD, P_SNW, P_HCW, P_HB, P_CCW, P_FCW, P_HB1, P_HB2, P_HFQ = (
    0, 8, 16, 24, 48, 56, 60, 64, 100, 108, 120, 252, 253, 254)
NPP = 256
CM_ID, CM_LE, CM_GE, CM_GT, CM_LT, CM_ONE = 0, 128, 256, 384, 512, 640
NCM = 768


def chunks_of(L):
    return [(0, 16)] + [(16 + 128 * i, 128) for i in range(L // 128)]


def tiles_of(L):
    return [(0, 16)] + [(16 + 512 * i, 512) for i in range(L // 512)]


def fchunks_of(L):
    return [(0, 17)] + [(17 + 128 * i, 128) for i in range(L // 128)]


class Obj:
    __slots__ = ("w", "r", "name")

    def __init__(self, name=""):
        self.w = None
        self.r = {}
        self.name = name


class Emit:
    NDS = 40

    def __init__(self, nc, es):
        self.nc = nc
        self.eng = {"pe": nc.tensor, "act": nc.scalar, "dve": nc.vector, "pool": nc.gpsimd, "sp": nc.sync}
        self.sem = {}
        self.cnt = {}
        for e in ("pe", "act", "dve", "pool"):
            self.sem[e] = es.enter_context(nc.semaphore("s_" + e))
            self.cnt[e] = 0
        self.dsem = [es.enter_context(nc.semaphore("d%d" % i)) for i in range(self.NDS)]
        self.dcnt = [0] * self.NDS
        self.dnext = 0
        self.seen = {e: {} for e in self.eng}
        self.nins = 0
        self.marks = []

    def _wait(self, e, deps):
        best = {}
        for (s, v) in deps:
            if e == "pe" and s is self.sem["pe"]:
                continue
            k = id(s)
            if k not in best or best[k][1] < v:
                best[k] = (s, v)
        sn = self.seen[e]
        for k, (s, v) in best.items():
            if sn.get(k, 0) >= v:
                continue
            self.eng[e].wait_ge(s, v)
            sn[k] = v

    def _deps(self, reads, writes):
        deps = []
        for o in reads:
            if o.w is not None:
                deps.append(o.w)
        for o in writes:
            if o.w is not None:
                deps.append(o.w)
            deps.extend(o.r.values())
        return deps

    def _mark(self, e, tok, reads, writes):
        for o in reads:
            o.r[e] = tok
        for o in writes:
            o.w = tok
            o.r = {}

    def op(self, e, fn, reads=(), writes=()):
        self._wait(e, self._deps(reads, writes))
        ins = fn(self.eng[e])
        self.cnt[e] += 1
        ins.then_inc(self.sem[e], 1)
        self._mark(e, (self.sem[e], self.cnt[e]), reads, writes)
        self.nins += 1
        return ins

    def dma(self, q, out, in_, reads=(), writes=()):
        deps = self._deps(reads, writes)
        i = self.dnext
        self.dnext = (i + 1) % self.NDS
        if self.dcnt[i]:
            deps.append((self.dsem[i], self.dcnt[i]))
        self._wait(q, deps)
        ins = self.eng[q].dma_start(out=out, in_=in_)
        self.dcnt[i] += 16
        ins.then_inc(self.dsem[i], 16)
        self._mark("dma%d" % i, (self.dsem[i], self.dcnt[i]), reads, writes)
        self.nins += 1
        return ins

    def mark(self, name):
        self.marks.append((name, self.cnt["pe"]))

    def barrier(self):
        toks = [(self.sem[e], self.cnt[e]) for e in self.sem if self.cnt[e]]
        toks += [(self.dsem[i], self.dcnt[i]) for i in range(self.NDS) if self.dcnt[i]]
        for e in self.eng:
            self._wait(e, toks)


class Ring:
    def __init__(self, items):
        self.items = items
        self.i = 0

    def next(self):
        it = self.items[self.i]
        self.i = (self.i + 1) % len(self.items)
        return it


def build_program(nP, nS, n_layers=2, stop_after=None, debug=False):
    nc = bass.Bass("TRN2", target_bir_lowering=False)
    es = ExitStack()
    K = Emit(nc, es)

    def din(name, shape, dt=F32):
        return nc.dram_tensor(name, list(shape), dt, kind="ExternalInput").ap()

    def dscr(name, shape, dt, out=False):
        kind = "ExternalOutput" if (out or debug) else "Internal"
        return nc.dram_tensor(name, list(shape), dt, kind=kind).ap()

    xp = din("xp", [max(nP, 1), 2048, D])
    xs = din("xs", [max(nS, 1), 4096, D])
    meta_d = din("meta_tokens", [NMETA, D])
    w_in = din("w_in", [2, D, DIN])
    w_branch = din("w_branch", [2, 4, BW, D])
    w_out = din("w_out", [2, D, D])
    w_up = din("w_up", [2, D, 2 * DFF])
    w_down = din("w_down", [2, DFF, D])
    hw1 = din("hyena_w1", [2, 33, 64])
    hw2 = din("hyena_w2", [2, 64, 64])
    hw3 = din("hyena_w3", [2, 64, 2048])
    pp_d = din("pp", [2, 128, NPP])
    pb_d = din("pb", [2, 128, 32])
    cm_d = din("cm", [128, NCM])
    cmb_d = din("cmb", [128, 512], BF16)
    dl_d = din("dl", [128, 512])
    tabs = {}
    for L in sorted(set(([2048] if nP else []) + ([4096] if nS else []))):
        T = L + 16
        tabs[L] = dict(
            fc=din("fc%d" % L, [T, T], BF16), fs=din("fs%d" % L, [T, T], BF16),
            hc=din("hc%d" % L, [T + 1, T + 1], BF16), hs=din("hs%d" % L, [T + 1, T + 1], BF16),
            hcb=din("hcb%d" % L, [1 + L // 128, 128, 1 + L // 128, 128], BF16),
            hsb=din("hsb%d" % L, [1 + L // 128, 128, 1 + L // 128, 128], BF16),
            zt=din("zt%d" % L, [33, T]), ntt=din("ntt%d" % L, [128, 1 + L // 128]),
            wf=din("wf%d" % L, [128, 1 + L // 128]),
            kf=[dscr("kf%d_%d" % (L, l), [1 + L // 128, 128, 2, 2, 512], F32) for l in range(n_layers)],
        )
    yp = nc.dram_tensor("yp", [max(nP, 1), 2048, D], F32, kind="ExternalOutput").ap()
    ys = nc.dram_tensor("ys", [max(nS, 1), 4096, D], F32, kind="ExternalOutput").ap()
    XT = dscr("XT", [8, 128, TMAX], F32)
    PR = dscr("PR", [NPR, 128, TMAX], BF16)
    BR = dscr("BR", [16, 128, TMAX], BF16)
    FA = dscr("FA", [22, 128, TMAX], BF16)
    o_XTt = [Obj("XT%d" % i) for i in range(10)]

    def xto(t0):
        return [o_XTt[0 if t0 < 16 else 1 + (t0 - 16) // 512]]
    o_PR = [Obj("PR%d" % i) for i in range(NPR)]
    o_BR = [Obj("BR%d" % i) for i in range(16)]
    o_FA = [Obj("FA%d" % i) for i in range(22)]

    uid = [0]

    def sb(name, shape, dt=F32, stack=es):
        uid[0] += 1
        return stack.enter_context(nc.sbuf_tensor("%s_%d" % (name, uid[0]), list(shape), dt))

    cm = sb("cm_sb", [128, NCM]); o_cm = Obj()
    cmb = sb("cmb_sb", [128, 512], BF16); o_cmb = Obj()
    pp = sb("pp_sb", [128, 2, NPP]); o_pp = Obj()
    pb = sb("pb_sb", [128, 2, 32]); o_pb = Obj()
    ps = [es.enter_context(nc.psum_tensor("ps%d" % i, [128, 512], F32)) for i in range(8)]
    o_ps = [Obj("ps%d" % i) for i in range(8)]
    psr = Ring(list(range(7)))

    K.dma("sp", cm[:], cm_d, writes=[o_cm])
    K.dma("sp", cmb[:], cmb_d, writes=[o_cmb])
    K.dma("sp", pp[:], pp_d.rearrange("l p c -> p l c"), writes=[o_pp])
    K.dma("sp", pb[:], pb_d.rearrange("l p c -> p l c"), writes=[o_pb])
    epsc = sb("epsc", [128, 2]);
    K.op("pool", lambda e: e.memset(epsc[:, 0:1], EPS), writes=[o_cm])
    K.op("pool", lambda e: e.memset(epsc[:, 1:2], -math.pi), writes=[o_cm])
    identf = cm[:, CM_ID:CM_ID + 128]
    identb = cmb[:, 0:128]
    onesb = cmb[:, 128:256]
    cs128 = cmb[:, 256:512]

    def ppc(l, col, n=1, rows=128):
        return pp[0:rows, l, col:col + n]

    def rstd_from(r_ap, o_r, ps_ap, o_p, scale):
        K.op("act", lambda e: e.activation(out=r_ap, in_=ps_ap, func=AF.Sqrt, bias=epsc[:, 0:1], scale=scale),
             reads=[o_p, o_cm], writes=[o_r])
        K.op("dve", lambda e: e.reciprocal(out=r_ap, in_=r_ap), reads=[o_r], writes=[o_r])

    def load_w(stack_bufs, src_ap, kc, ncols):
        wb, o_wb = stack_bufs["wb"].next()
        K.dma("pool", wb[:, 0:kc, 0:ncols], src_ap.rearrange("(k p) m -> p k m", p=128), writes=[o_wb])
        return wb, o_wb

    class WStream:
        def __init__(self, bufs, plan, lookahead):
            self.bufs, self.plan, self.la = bufs, plan, lookahead
            self.issued = 0
            self.got = {}

        def prime(self):
            self._fill(0)

        def get(self, i):
            self._fill(i)
            return self.got.pop(i)

        def _fill(self, i):
            while self.issued < min(len(self.plan), i + self.la + 1):
                src, kc, n = self.plan[self.issued]
                self.got[self.issued] = load_w(self.bufs, src, kc, n)
                self.issued += 1

    def rmsnorm_to(HT, o_HTt, l, pcol, tl, keep=None):
        with ExitStack() as own:
            st = keep if keep is not None else own
            xr = Ring([(sb("rn_x%d" % i, [128, 8, 512], F32, st), Obj()) for i in range(2)])
            sq = Ring([(sb("rn_s%d" % i, [128, 8, 512], BF16, st), Obj()) for i in range(2)])
            rs = Ring([(sb("rn_r%d" % i, [128, 512], F32, st), Obj()) for i in range(2)])
            for ti, (t0, tw) in enumerate(tl):
                x, o_x = xr.next()
                s, o_s = sq.next()
                r, o_r = rs.next()
                K.dma("sp", x[:, :, 0:tw], XT[:, :, t0:t0 + tw].rearrange("c p t -> p c t"),
                      reads=xto(t0), writes=[o_x])
                K.op("act", lambda e: e.activation(out=s[:, :, 0:tw], in_=x[:, :, 0:tw], func=AF.Square),
                     reads=[o_x], writes=[o_s])
                b = psr.next()
                for c in range(8):
                    K.op("pe", lambda e: e.matmul(ps[b][:, 0:tw], lhsT=onesb, rhs=s[:, c, 0:tw],
                                                  start=(c == 0), stop=(c == 7)),
                         reads=[o_s, o_cmb], writes=[o_ps[b]])
                rstd_from(r[:, 0:tw], o_r, ps[b][:, 0:tw], o_ps[b], 1.0 / D)
                for c in range(8):
                    K.op("dve", lambda e: e.scalar_tensor_tensor(
                        out=HT[:, c, t0:t0 + tw], in0=x[:, c, 0:tw], scalar=ppc(l, pcol + c),
                        in1=r[:, 0:tw], op0=ALU.mult, op1=ALU.mult),
                        reads=[o_x, o_r, o_pp], writes=[o_HTt[ti]])
            if keep is None:
                K.barrier()

    def phase_input(x_d, si, L):
        with ExitStack() as st:
            xin = Ring([(sb("pi_x%d" % i, [128, D], F32, st), Obj()) for i in range(4)])
            xo = Ring([(sb("pi_o%d" % i, [128, 8, 128], F32, st), Obj()) for i in range(4)])
            for (t0, q) in chunks_of(L):
                x, o_x = xin.next()
                o, o_o = xo.next()
                src = meta_d if t0 == 0 else x_d[si, t0 - 16:t0 - 16 + q, :]
                K.dma("sp", x[0:q, :], src, writes=[o_x])
                b0 = psr.next()
                b1 = psr.next()
                for c in range(8):
                    b = b0 if c < 4 else b1
                    K.op("pe", lambda e: e.matmul(ps[b][:, (c % 4) * 128:(c % 4) * 128 + q],
                                                  lhsT=x[0:q, c * 128:(c + 1) * 128], rhs=identf[0:q, 0:q],
                                                  start=True, stop=True),
                         reads=[o_x, o_cm], writes=[o_ps[b]])
                K.op("act", lambda e: e.activation(
                    out=o[:, 0:4, 0:q], in_=ps[b0][:].rearrange("p (c t) -> p c t", c=4)[:, :, 0:q], func=AF.Copy),
                    reads=[o_ps[b0]], writes=[o_o])
                K.op("dve", lambda e: e.tensor_copy(
                    out=o[:, 4:8, 0:q], in_=ps[b1][:].rearrange("p (c t) -> p c t", c=4)[:, :, 0:q]),
                    reads=[o_ps[b1]], writes=[o_o])
                K.dma("pool", XT[:, :, t0:t0 + q].rearrange("c p t -> p c t"), o[:, :, 0:q],
                      reads=[o_o], writes=xto(t0))

    def phase_final(y_d, si, L):
        with ExitStack() as st:
            HT = sb("pf_h", [128, 8, 512], F32, st)
            o_HT = Obj()
            yo = Ring([(sb("pf_y%d" % i, [128, D], F32, st), Obj()) for i in range(4)])
            xr_r = Ring([(sb("pf_x", [128, 8, 512], F32, st), Obj()) for _ in range(2)])
            s_r = Ring([(sb("pf_s", [128, 8, 512], BF16, st), Obj()) for _ in range(2)])
            r_r = Ring([(sb("pf_r", [128, 512], F32, st), Obj()) for _ in range(2)])
            for (t0, tw) in tiles_of(L)[1:]:
                if True:
                    xr, o_x = xr_r.next()
                    s, o_s = s_r.next()
                    r, o_r = r_r.next()
                    K.dma("sp", xr[:], XT[:, :, t0:t0 + tw].rearrange("c p t -> p c t"), reads=xto(t0), writes=[o_x])
                    K.op("act", lambda e: e.activation(out=s[:], in_=xr[:], func=AF.Square), reads=[o_x], writes=[o_s])
                    b = psr.next()
                    for c in range(8):
                        K.op("pe", lambda e: e.matmul(ps[b][:], lhsT=onesb, rhs=s[:, c, :], start=(c == 0), stop=(c == 7)),
                             reads=[o_s, o_cmb], writes=[o_ps[b]])
                    rstd_from(r[:], o_r, ps[b][:], o_ps[b], 1.0 / D)
                    for c in range(8):
                        K.op("dve", lambda e: e.scalar_tensor_tensor(
                            out=HT[:, c, :], in0=xr[:, c, :], scalar=ppc(0, P_NFIN + c), in1=r[:],
                            op0=ALU.mult, op1=ALU.mult), reads=[o_x, o_r, o_pp], writes=[o_HT])
                    for j in range(4):
                        y, o_y = yo.next()
                        b0 = psr.next()
                        b1 = psr.next()
                        for c in range(8):
                            b = b0 if c < 4 else b1
                            K.op("pe", lambda e: e.matmul(ps[b][:, (c % 4) * 128:(c % 4 + 1) * 128],
                                                          lhsT=HT[:, c, j * 128:(j + 1) * 128], rhs=identf,
                                                          start=True, stop=True),
                                 reads=[o_HT, o_cm], writes=[o_ps[b]])
                        K.op("act", lambda e: e.activation(out=y[:, 0:512], in_=ps[b0][:], func=AF.Copy),
                             reads=[o_ps[b0]], writes=[o_y])
                        K.op("dve", lambda e: e.tensor_copy(out=y[:, 512:1024], in_=ps[b1][:]),
                             reads=[o_ps[b1]], writes=[o_y])
                        tt = t0 - 16 + j * 128
                        K.dma("pool", y_d[si, tt:tt + 128, :], y[:], reads=[o_y], writes=[])

    def conv3(dst, o_dst, row, o_row, l, wcol, T, acc=None, o_acc=None):
        if acc is None:
            acc, o_acc = dst, o_dst
        K.op("act", lambda e: e.activation(out=acc, in_=row[:, 0:T], func=AF.Copy, scale=ppc(l, wcol)),
             reads=[o_row, o_pp], writes=[o_acc])
        K.op("dve", lambda e: e.scalar_tensor_tensor(out=acc, in0=row[:, 1:T + 1], scalar=ppc(l, wcol + 1), in1=acc,
                                                     op0=ALU.mult, op1=ALU.add), reads=[o_row, o_pp, o_acc], writes=[o_acc])
        K.op("dve", lambda e: e.scalar_tensor_tensor(out=dst, in0=row[:, 2:T + 2], scalar=ppc(l, wcol + 2), in1=acc,
                                                     op0=ALU.mult, op1=ALU.add), reads=[o_row, o_pp, o_acc], writes=[o_dst])

    def phase1(l, L, dtS, dtA, o_dt):
        T = L + 16
        tl = tiles_of(L)
        with ExitStack() as st:
            HT = sb("p1_ht", [128, 8, T], BF16, st); o_HTt = [Obj() for _ in tl]
            wbufs = dict(wb=Ring([(sb("p1_wb", [128, 8, 256], BF16, st), Obj()) for _ in range(4)]))
            flip = [0]
            plan = []
            for c0_ in (O_FN, O_FN + 256, O_Z, O_Z + 256):
                plan.append((w_in[l, :, c0_:c0_ + 256], 8, 256))
            for g_ in range(0, 8, 2):
                plan.append((w_in[l, :, O_XBC + g_ * 128:O_XBC + g_ * 128 + 256], 8, 256))
            plan.append((w_in[l, :, O_DT:O_DT + 16], 8, 16))
            for g_ in range(0, 12, 2):
                plan.append((w_in[l, :, O_HY + g_ * 128:O_HY + g_ * 128 + 256], 8, 256))
            for j_ in range(4):
                for cc_ in (O_SC + j_ * 128, O_SC + 512 + j_ * 128, O_SC + 1024 + j_ * 128):
                    plan.append((w_in[l, :, cc_:cc_ + 128], 8, 128))
            for g_ in range(0, 32, 2):
                plan.append((w_in[l, :, O_G + g_ * 128:O_G + g_ * 128 + 256], 8, 256))
            ws = WStream(wbufs, plan, 3)
            ws.prime()
            rmsnorm_to(HT, o_HTt, l, P_NMIX, tl, keep=(st if L == 2048 else None))
            K.mark('p1_norm')
            rows = Ring([(sb("p1_row", [128, T + 2], F32, st), Obj()) for _ in range(4)])
            outs = Ring([(sb("p1_out", [128, T], BF16, st), Obj()) for _ in range(2)])
            for (rw, o_rw) in rows.items:
                K.op("pool", lambda e: e.memset(rw[:, 0:1], 0.0), writes=[o_rw])
                K.op("pool", lambda e: e.memset(rw[:, T + 1:T + 2], 0.0), writes=[o_rw])
            wi = [0]

            def next_w():
                r_ = ws.get(wi[0])
                wi[0] += 1
                return r_

            def gemm_row(wb, o_wb, c0, dst, o_dst, off, func=None):
                for ti, (t0, tw) in enumerate(tl):
                    b = psr.next()
                    for kc in range(8):
                        K.op("pe", lambda e: e.matmul(ps[b][:, 0:tw], lhsT=wb[:, kc, c0:c0 + 128],
                                                      rhs=HT[:, kc, t0:t0 + tw], start=(kc == 0), stop=(kc == 7)),
                             reads=[o_wb, o_HTt[ti]], writes=[o_ps[b]])
                    d = dst[:, off + t0:off + t0 + tw]
                    flip[0] ^= 1
                    if func is not None:
                        K.op("act", lambda e: e.activation(out=d, in_=ps[b][:, 0:tw], func=func),
                             reads=[o_ps[b]], writes=[o_dst])
                    else:
                        K.op("act", lambda e: e.activation(out=d, in_=ps[b][:, 0:tw], func=AF.Copy),
                             reads=[o_ps[b]], writes=[o_dst])

            def store(dst_d, o_d, ob, o_ob):
                K.dma("sp", dst_d[:, 0:T], ob[:, 0:T], reads=[o_ob], writes=[o_d])

            def wsrc(c0, n):
                return w_in[l, :, c0:c0 + n]

            def simple_rows(col0, nrows, pr0, func):
                for g in range(0, nrows, 2):
                    wb, o_wb = next_w()
                    for j in range(2):
                        ob, o_ob = outs.next()
                        gemm_row(wb, o_wb, j * 128, ob, o_ob, 0, func)
                        store(PR[pr0 + g + j], o_PR[pr0 + g + j], ob, o_ob)

            simple_rows(O_FN, 4, R_FN, None)
            simple_rows(O_Z, 4, R_SZ, AF.Silu)
            K.mark('p1_fnz')
            for g in range(0, 8, 2):
                wb, o_wb = next_w()
                for j in range(2):
                    c = g + j
                    rw, o_rw = rows.next()
                    tmp, o_tmp = rows.next()
                    gemm_row(wb, o_wb, j * 128, rw, o_rw, 1)
                    conv3(tmp[:, 1:T + 1], o_tmp, rw, o_rw, l, P_SCW + 3 * c, T)
                    ob, o_ob = outs.next()
                    K.op("act", lambda e: e.activation(out=ob[:, 0:T], in_=tmp[:, 1:T + 1], func=AF.Silu,
                                                       bias=ppc(l, P_SCB + c), scale=1.0),
                         reads=[o_tmp, o_pp], writes=[o_ob])
                    store(PR[R_XS + c], o_PR[R_XS + c], ob, o_ob)
            K.mark('p1_xbc')
            wb, o_wb = next_w()
            with ExitStack() as st2:
                av = sb("p1_a", [128, 16], F32, st2); o_av = Obj()
                tmpd = Ring([(sb("p1_td", [128, 16], F32, st2), Obj()) for _ in range(2)])
                K.op("act", lambda e: e.activation(out=av[:], in_=pb[:, l, 16:32], func=AF.Exp), reads=[o_pb], writes=[o_av])
                K.op("dve", lambda e: e.tensor_scalar(out=av[:], in0=av[:], scalar1=-1.0, scalar2=None, op0=ALU.mult),
                     reads=[o_av], writes=[o_av])
                for ci, (q0, q) in enumerate(chunks_of(L)):
                    b = psr.next()
                    td, o_td = tmpd.next()
                    for kc in range(8):
                        K.op("pe", lambda e: e.matmul(ps[b][0:q, 0:16], lhsT=HT[:, kc, q0:q0 + q], rhs=wb[:, kc, 0:16],
                                                      start=(kc == 0), stop=(kc == 7)),
                             reads=[o_wb, o_HTt[0 if q0 < 16 else 1 + (q0 - 16) // 512]], writes=[o_ps[b]])
                    K.op("dve", lambda e: e.tensor_tensor(out=td[0:q, :], in0=ps[b][0:q, 0:16], in1=pb[0:q, l, 0:16],
                                                          op=ALU.add), reads=[o_ps[b], o_pb], writes=[o_td])
                    K.op("act", lambda e: e.activation(out=td[0:q, :], in_=td[0:q, :], func=AF.Exp),
                         reads=[o_td], writes=[o_td])
                    K.op("act", lambda e: e.activation(out=dtS[0:q, ci, :], in_=td[0:q, :], func=AF.Ln, bias=1.0),
                         reads=[o_td], writes=[o_dt])
                    K.op("dve", lambda e: e.tensor_tensor(out=dtA[0:q, ci, :], in0=dtS[0:q, ci, :], in1=av[0:q, :],
                                                          op=ALU.mult), reads=[o_dt, o_av], writes=[o_dt])
                K.barrier()
            K.mark('p1_dt')
            for g in range(0, 12, 2):
                wb, o_wb = next_w()
                for j in range(2):
                    c = g + j
                    rw, o_rw = rows.next()
                    tmp, o_tmp = rows.next()
                    gemm_row(wb, o_wb, j * 128, rw, o_rw, 1)
                    ob, o_ob = outs.next()
                    conv3(ob[:, 0:T], o_ob, rw, o_rw, l, P_HCW + 3 * c, T, acc=tmp[:, 1:T + 1], o_acc=o_tmp)
                    store(PR[R_V + c], o_PR[R_V + c], ob, o_ob)
            K.mark('p1_hy')
            for j in range(4):
                rb, o_rb = rows.next()
                rc, o_rc = rows.next()
                rx, o_rx = rows.next()
                rt, o_rt = rows.next()
                for (r_, o_r_, cc) in ((rb, o_rb, O_SC + j * 128), (rc, o_rc, O_SC + 512 + j * 128),
                                       (rx, o_rx, O_SC + 1024 + j * 128)):
                    wb, o_wb = next_w()
                    gemm_row(wb, o_wb, 0, r_, o_r_, 1)
                K.op("dve", lambda e: e.tensor_tensor(out=rc[:, 1:T + 1], in0=rc[:, 1:T + 1], in1=rx[:, 1:T + 1],
                                                      op=ALU.mult), reads=[o_rc, o_rx], writes=[o_rc])
                conv3(rt[:, 1:T + 1], o_rt, rc, o_rc, l, P_CCW + 3 * j, T)
                ob, o_ob = outs.next()
                K.op("dve", lambda e: e.tensor_tensor(out=ob[:, 0:T], in0=rb[:, 1:T + 1], in1=rt[:, 1:T + 1],
                                                      op=ALU.mult), reads=[o_rb, o_rt], writes=[o_ob])
                store(BR[12 + j], o_BR[12 + j], ob, o_ob)
            K.mark('p1_sc')
            simple_rows(O_G, 32, R_G, AF.Sigmoid)
            K.barrier()

    def phase_ssd(l, L, dtS, dtA, o_dt):
        T = L + 16
        ch = chunks_of(L)
        nch = len(ch)
        with ExitStack() as st:
            xsT = sb("ss_xs", [128, 4, T], BF16, st); o_xs = Obj()
            bcT = sb("ss_bc", [128, 4, T], BF16, st); o_bc = Obj()
            Yp = sb("ss_y", [128, nch, 512], F32, st); o_Y = [Obj() for _ in ch]
            hst2 = [(sb("ss_h", [128, 512], F32, st), Obj()) for _ in range(2)]
            hbf2 = [(sb("ss_hb", [128, 512], BF16, st), Obj()) for _ in range(2)]
            K.dma("sp", xsT[:], PR[R_XS:R_XS + 4, :, 0:T].rearrange("c p t -> p c t"), reads=o_PR[R_XS:R_XS + 4], writes=[o_xs])
            K.dma("sp", bcT[:], PR[R_B:R_B + 4, :, 0:T].rearrange("c p t -> p c t"), reads=o_PR[R_B:R_B + 4], writes=[o_bc])

            def ring(name, shape, dt, n=2):
                return Ring([(sb(name, shape, dt, st), Obj()) for _ in range(n)])
            r_xdt = ring("ss_xdt", [128, 512], BF16, 3)
            r_xw = ring("ss_xw", [128, 512], BF16, 3)
            r_bt = ring("ss_bt", [128, 256], BF16, 3)
            r_cbm = ring("ss_cbm", [128, 2, 128], F32)
            r_E = ring("ss_E", [128, 8], F32, 8)
            r_wd = ring("ss_wd", [128, 8], F32)
            r_cd = ring("ss_cd", [128, 8], F32, 8)
            r_lm = ring("ss_lm", [128, 8, 128], F32)
            r_LT = ring("ss_LT", [128, 8, 128], F32)
            r_MT = ring("ss_MT", [128, 8, 128], BF16, 3)
            r_yd = ring("ss_yd", [128, 512], F32, 3)
            r_sts = ring("ss_sts", [128, 512], F32, 3)
            r_yo = ring("ss_yo", [128, 512], F32)
            r_yz = ring("ss_yz", [128, 4, 128], F32)
            r_sq = ring("ss_sq", [128, 4, 128], BF16)
            r_rs = ring("ss_rs", [128, 2, 128], F32)
            r_ob = ring("ss_ob", [128, 4, 128], BF16)
            r_sz = ring("ss_szc", [128, 4, 128], BF16)

            def mm(b, out, lhsT, rhs, reads, start=True, stop=True):
                K.op("pe", lambda e: e.matmul(out, lhsT=lhsT, rhs=rhs, start=start, stop=stop),
                     reads=reads, writes=[o_ps[b]])

            def v8(ap):
                return ap.rearrange("p (h d) -> p h d", h=8)

            for d in (0, 1):
                K.op("pool", lambda e: e.memset(hst2[d][0][:], 0.0), writes=[hst2[d][1]])
                K.op("pool", lambda e: e.memset(hbf2[d][0][:], 0.0), writes=[hbf2[d][1]])

            def stage_a(d, ci):
                tri = cm[:, CM_LE:CM_LE + 128] if d == 0 else cm[:, CM_GE:CM_GE + 128]
                strict = cm[:, CM_GT:CM_GT + 128] if d == 0 else cm[:, CM_LT:CM_LT + 128]
                q0, q = ch[ci]
                dtv = dtS[0:q, ci, d * 8:(d + 1) * 8]
                dav = dtA[0:q, ci, d * 8:(d + 1) * 8]
                xdt, o_xdt = r_xdt.next(); xw, o_xw = r_xw.next(); bt, o_bt = r_bt.next()
                cbm, o_cbm = r_cbm.next(); E, o_E = r_E.next(); wd, o_wd = r_wd.next(); cd, o_cd = r_cd.next()
                lm, o_lm = r_lm.next(); LT, o_LT = r_LT.next(); MT, o_MT = r_MT.next()
                for g in range(2):
                    K.op("pool", lambda e: e.tensor_tensor(
                        out=lm[0:q, g * 4:(g + 1) * 4, 0:q],
                        in0=strict[0:q, 0:q].unsqueeze(1).to_broadcast([q, 4, q]),
                        in1=dav[:, g * 4:(g + 1) * 4].unsqueeze(2).to_broadcast([q, 4, q]), op=ALU.mult),
                        reads=[o_cm, o_dt], writes=[o_lm])
                bx = psr.next()
                for j in range(4):
                    mm(bx, ps[bx][0:q, j * 128:(j + 1) * 128], xsT[:, j, q0:q0 + q], identb, [o_xs, o_cmb])
                ba = psr.next()
                mm(ba, ps[ba][0:q, 0:8], tri[0:q, 0:q], dav, [o_cm, o_dt])
                mm(ba, ps[ba][0:q, 8:16], strict[0:q, 0:q], dav, [o_cm, o_dt])
                mm(ba, ps[ba][:, 16:24], cm[0:q, CM_ONE:CM_ONE + 128], dav, [o_cm, o_dt])
                K.op("act", lambda e: e.activation(out=E[0:q, :], in_=ps[ba][0:q, 0:8], func=AF.Exp), reads=[o_ps[ba]], writes=[o_E])
                K.op("act", lambda e: e.activation(out=wd[0:q, :], in_=ps[ba][0:q, 8:16], func=AF.Exp), reads=[o_ps[ba]], writes=[o_wd])
                K.op("act", lambda e: e.activation(out=cd[:, :], in_=ps[ba][:, 16:24], func=AF.Exp), reads=[o_ps[ba]], writes=[o_cd])
                K.op("dve", lambda e: e.tensor_tensor(out=wd[0:q, :], in0=wd[0:q, :], in1=dtv, op=ALU.mult), reads=[o_wd, o_dt], writes=[o_wd])
                K.op("dve", lambda e: e.tensor_tensor(out=v8(xdt[0:q, :]), in0=v8(ps[bx][0:q, :]),
                                                      in1=dtv.unsqueeze(2).to_broadcast([q, 8, 64]), op=ALU.mult),
                     reads=[o_ps[bx], o_dt], writes=[o_xdt])
                K.op("dve", lambda e: e.tensor_tensor(out=v8(xw[0:q, :]), in0=v8(ps[bx][0:q, :]),
                                                      in1=wd[0:q, :].unsqueeze(2).to_broadcast([q, 8, 64]), op=ALU.mult),
                     reads=[o_ps[bx], o_wd], writes=[o_xw])
                bb = psr.next()
                for g in range(2):
                    mm(bb, ps[bb][0:q, g * 128:(g + 1) * 128], bcT[:, g, q0:q0 + q], identb, [o_bc, o_cmb])
                K.op("act", lambda e: e.activation(out=bt[0:q, :], in_=ps[bb][0:q, 0:256], func=AF.Copy), reads=[o_ps[bb]], writes=[o_bt])
                bc = psr.next()
                for g in range(2):
                    mm(bc, ps[bc][0:q, g * 128:g * 128 + q], bcT[:, g, q0:q0 + q], bcT[:, 2 + g, q0:q0 + q], [o_bc])
                for g in range(2):
                    K.op("dve", lambda e: e.tensor_tensor(out=cbm[0:q, g, 0:q], in0=ps[bc][0:q, g * 128:g * 128 + q],
                                                          in1=tri[0:q, 0:q], op=ALU.mult), reads=[o_ps[bc], o_cm], writes=[o_cbm])
                for g in range(2):
                    bD = psr.next()
                    for hh in range(4):
                        mm(bD, ps[bD][0:q, hh * 128:hh * 128 + q], lm[0:q, g * 4 + hh, 0:q], tri[0:q, 0:q], [o_lm, o_cm])
                    K.op("act", lambda e: e.activation(out=LT[0:q, g * 4:(g + 1) * 4, 0:q],
                                                       in_=ps[bD][0:q, :].rearrange("p (h s) -> p h s", h=4)[:, :, 0:q],
                                                       func=AF.Exp), reads=[o_ps[bD]], writes=[o_LT])
                    K.op("dve", lambda e: e.tensor_tensor(
                        out=MT[0:q, g * 4:(g + 1) * 4, 0:q], in0=LT[0:q, g * 4:(g + 1) * 4, 0:q],
                        in1=cbm[0:q, g, 0:q].unsqueeze(1).to_broadcast([q, 4, q]), op=ALU.mult),
                        reads=[o_LT, o_cbm], writes=[o_MT])
                return dict(d=d, ci=ci, E=E, o_E=o_E, cd=cd, o_cd=o_cd, MT=MT, o_MT=o_MT, xdt=xdt, o_xdt=o_xdt,
                            xw=xw, o_xw=o_xw, bt=bt, o_bt=o_bt)

            def stage_a2(c):
                q0, q = ch[c["ci"]]
                MT, o_MT, xdt, o_xdt, xw, o_xw, bt, o_bt = (c["MT"], c["o_MT"], c["xdt"], c["o_xdt"], c["xw"], c["o_xw"],
                                                            c["bt"], c["o_bt"])
                yd, o_yd = r_yd.next(); sts, o_sts = r_sts.next()
                by = psr.next()
                for h in range(8):
                    mm(by, ps[by][0:q, h * 64:(h + 1) * 64], MT[0:q, h, 0:q], xdt[0:q, h * 64:(h + 1) * 64], [o_MT, o_xdt])
                K.op("act", lambda e: e.activation(out=yd[0:q, :], in_=ps[by][0:q, :], func=AF.Copy), reads=[o_ps[by]], writes=[o_yd])
                bs = psr.next()
                for g in range(2):
                    mm(bs, ps[bs][:, g * 256:(g + 1) * 256], bt[0:q, g * 128:(g + 1) * 128], xw[0:q, g * 256:(g + 1) * 256], [o_bt, o_xw])
                K.op("act", lambda e: e.activation(out=sts[:, :], in_=ps[bs][:, :], func=AF.Copy), reads=[o_ps[bs]], writes=[o_sts])
                c.update(yd=yd, o_yd=o_yd, sts=sts, o_sts=o_sts)
                return c

            ywritten = [False] * nch

            def stage_b(c):
                d, ci = c["d"], c["ci"]
                q0, q = ch[ci]
                hst, o_h = hst2[d]
                hbf, o_hb = hbf2[d]
                E, o_E, cd, o_cd, yd, o_yd, sts, o_sts = c["E"], c["o_E"], c["cd"], c["o_cd"], c["yd"], c["o_yd"], c["sts"], c["o_sts"]
                first = not ywritten[ci]
                ywritten[ci] = True
                yo, o_yo = r_yo.next()
                bg = psr.next()
                for g in range(2):
                    mm(bg, ps[bg][0:q, g * 256:(g + 1) * 256], bcT[:, 2 + g, q0:q0 + q], hbf[:, g * 256:(g + 1) * 256], [o_bc, o_hb])
                K.op("dve", lambda e: e.tensor_tensor(out=v8(yo[0:q, :]), in0=v8(ps[bg][0:q, :]),
                                                      in1=E[0:q, :].unsqueeze(2).to_broadcast([q, 8, 64]), op=ALU.mult),
                     reads=[o_ps[bg], o_E], writes=[o_yo])
                if first:
                    K.op("dve", lambda e: e.tensor_tensor(out=Yp[0:q, ci, :], in0=yd[0:q, :], in1=yo[0:q, :], op=ALU.add),
                         reads=[o_yd, o_yo], writes=[o_Y[ci]])
                else:
                    K.op("pool", lambda e: e.tensor_tensor(out=yo[0:q, :], in0=yd[0:q, :], in1=yo[0:q, :], op=ALU.add),
                         reads=[o_yd, o_yo], writes=[o_yo])
                    K.op("pool", lambda e: e.tensor_tensor(out=Yp[0:q, ci, :], in0=Yp[0:q, ci, :], in1=yo[0:q, :], op=ALU.add),
                         reads=[o_Y[ci], o_yo], writes=[o_Y[ci]])
                K.op("dve", lambda e: e.tensor_tensor(out=v8(hst[:, :]), in0=v8(hst[:, :]),
                                                      in1=cd[:, :].unsqueeze(2).to_broadcast([128, 8, 64]), op=ALU.mult),
                     reads=[o_h, o_cd], writes=[o_h])
                K.op("dve", lambda e: e.tensor_tensor(out=hst[:, :], in0=hst[:, :], in1=sts[:, :], op=ALU.add),
                     reads=[o_h, o_sts], writes=[o_h])
                K.op("act", lambda e: e.activation(out=hbf[:, :], in_=hst[:, :], func=AF.Copy), reads=[o_h], writes=[o_hb])
                return None if first else ci

            def stage_c(ci):
                q0, q = ch[ci]
                if True:
                    yz, o_yz = r_yz.next(); sq, o_sq = r_sq.next(); rs, o_rs = r_rs.next(); ob, o_ob = r_ob.next()
                    btp = psr.next()
                    for j in range(4):
                        mm(btp, ps[btp][:, j * 128:j * 128 + q], Yp[0:q, ci, j * 128:(j + 1) * 128], identf[0:q, 0:q], [o_Y[ci], o_cm])
                    for j in range(4):
                        K.op("dve", lambda e: e.scalar_tensor_tensor(
                            out=yz[:, j, 0:q], in0=xsT[:, j, q0:q0 + q], scalar=ppc(l, P_SD + j),
                            in1=ps[btp][:, j * 128:j * 128 + q], op0=ALU.mult, op1=ALU.add),
                            reads=[o_xs, o_pp, o_ps[btp]], writes=[o_yz])
                    szc, o_sz = r_sz.next()
                    K.dma("sp", szc[:, :, 0:q], PR[R_SZ:R_SZ + 4, :, q0:q0 + q].rearrange("c p t -> p c t"),
                          reads=o_PR[R_SZ:R_SZ + 4], writes=[o_sz])
                    K.op("pool", lambda e: e.tensor_tensor(out=yz[:, :, 0:q], in0=yz[:, :, 0:q], in1=szc[:, :, 0:q], op=ALU.mult),
                         reads=[o_yz, o_sz], writes=[o_yz])
                    K.op("act", lambda e: e.activation(out=sq[:, :, 0:q], in_=yz[:, :, 0:q], func=AF.Square), reads=[o_yz], writes=[o_sq])
                    bn = psr.next()
                    for g in range(2):
                        for jj in range(2):
                            mm(bn, ps[bn][:, g * 128:g * 128 + q], onesb, sq[:, 2 * g + jj, 0:q], [o_cmb, o_sq],
                               start=(jj == 0), stop=(jj == 1))
                    rstd_from(rs[:, :, 0:q], o_rs, ps[bn][:, 0:256].rearrange("p (g s) -> p g s", g=2)[:, :, 0:q], o_ps[bn], 1.0 / 256)
                    for j in range(4):
                        K.op("dve", lambda e: e.scalar_tensor_tensor(
                            out=ob[:, j, 0:q], in0=yz[:, j, 0:q], scalar=ppc(l, P_SNW + j), in1=rs[:, j // 2, 0:q],
                            op0=ALU.mult, op1=ALU.mult), reads=[o_yz, o_pp, o_rs], writes=[o_ob])
                    K.dma("pool", BR[4:8, :, q0:q0 + q].rearrange("c p t -> p c t"), ob[:, :, 0:q],
                          reads=[o_ob], writes=o_BR[4:8])

            sched = []
            for i_ in range(nch):
                sched.append((0, i_))
                sched.append((1, nch - 1 - i_))
            pend = []
            pend_c = []

            def run_b():
                r_ = stage_b(pend.pop(0))
                if r_ is not None:
                    pend_c.append(r_)
                while len(pend_c) > 3:
                    stage_c(pend_c.pop(0))

            pend_a2 = []
            for (d, ci) in sched:
                pend_a2.append(stage_a(d, ci))
                if len(pend_a2) > 2:
                    pend.append(stage_a2(pend_a2.pop(0)))
                if len(pend) > 2:
                    run_b()
            while pend_a2:
                pend.append(stage_a2(pend_a2.pop(0)))
                if len(pend) > 2:
                    run_b()
            while pend:
                run_b()
            while pend_c:
                stage_c(pend_c.pop(0))
            K.barrier()

    def phase_fnet(l, L):
        T = L + 16
        ch = chunks_of(L)
        nch = len(ch)
        tb = tabs[L]
        with ExitStack() as st:
            ufT = sb("fn_u", [128, 4, T], BF16, st); o_u = Obj()
            Pcs = sb("fn_p", [128, nch, 2, 512], BF16, st); o_P = Obj()
            K.dma("sp", ufT[:], PR[R_FN:R_FN + 4, :, 0:T].rearrange("c p t -> p c t"), reads=o_PR[R_FN:R_FN + 4], writes=[o_u])
            for ci, (q0, q) in enumerate(ch):
                for gp in range(2):
                    b = psr.next()
                    for gg in range(2):
                        g = gp * 2 + gg
                        K.op("pe", lambda e: e.matmul(ps[b][0:q, gg * 256:(gg + 1) * 256], lhsT=ufT[:, g, q0:q0 + q], rhs=cs128,
                                                      start=True, stop=True), reads=[o_u, o_cmb], writes=[o_ps[b]])
                    eng = "act" if gp == 0 else "dve"
                    src = ps[b][0:q, :].rearrange("p (g s c) -> p s g c", g=2, s=2)
                    dst = Pcs[0:q, ci, :, gp * 256:(gp + 1) * 256].rearrange("p s (g c) -> p s g c", g=2)
                    if eng == "act":
                        K.op("act", lambda e: e.activation(out=dst, in_=src, func=AF.Copy), reads=[o_ps[b]], writes=[o_P])
                    else:
                        K.op("dve", lambda e: e.tensor_copy(out=dst, in_=src), reads=[o_ps[b]], writes=[o_P])
            tcr = Ring([(sb("fn_tc", [128, 512], BF16, st), Obj()) for _ in range(6)])
            tsr = Ring([(sb("fn_ts", [128, 512], BF16, st), Obj()) for _ in range(6)])
            obr = Ring([(sb("fn_ob", [128, 4, 512], BF16, st), Obj()) for _ in range(2)])
            for (f0, fw) in tiles_of(L):
                bs_ = [psr.next() for _ in range(4)]
                for ci, (q0, q) in enumerate(ch):
                    tc, o_tc = tcr.next()
                    ts, o_ts = tsr.next()
                    K.dma("sp", tc[0:q, 0:fw], tb["fc"][q0:q0 + q, f0:f0 + fw], writes=[o_tc])
                    K.dma("sp", ts[0:q, 0:fw], tb["fs"][q0:q0 + q, f0:f0 + fw], writes=[o_ts])
                    for g in range(4):
                        b = bs_[g]
                        K.op("pe", lambda e: e.matmul(ps[b][:, 0:fw], lhsT=Pcs[0:q, ci, 0, g * 128:(g + 1) * 128], rhs=tc[0:q, 0:fw],
                                                      start=(ci == 0), stop=False), reads=[o_P, o_tc], writes=[o_ps[b]])
                        K.op("pe", lambda e: e.matmul(ps[b][:, 0:fw], lhsT=Pcs[0:q, ci, 1, g * 128:(g + 1) * 128], rhs=ts[0:q, 0:fw],
                                                      start=False, stop=(ci == nch - 1)), reads=[o_P, o_ts], writes=[o_ps[b]])
                ob, o_ob = obr.next()
                for g in range(4):
                    b = bs_[g]
                    if g % 2 == 0:
                        K.op("act", lambda e: e.activation(out=ob[:, g, 0:fw], in_=ps[b][:, 0:fw], func=AF.Copy), reads=[o_ps[b]], writes=[o_ob])
                    else:
                        K.op("dve", lambda e: e.tensor_copy(out=ob[:, g, 0:fw], in_=ps[b][:, 0:fw]), reads=[o_ps[b]], writes=[o_ob])
                K.dma("pool", BR[0:4, :, f0:f0 + fw].rearrange("c p t -> p c t"), ob[:, :, 0:fw], reads=[o_ob], writes=o_BR[0:4])
            K.barrier()

    TWO_PI = 2.0 * math.pi

    def sin_act(dst, o_dst, a, o_a, ki, kf_, o_k, rows, q):
        K.op("dve", lambda e: e.tensor_scalar(out=ki[0:rows, 0:q], in0=a[0:rows, 0:q], scalar1=1.0 / TWO_PI, scalar2=None, op0=ALU.mult),
             reads=[o_a], writes=[o_k])
        K.op("dve", lambda e: e.tensor_copy(out=kf_[0:rows, 0:q], in_=ki[0:rows, 0:q]), reads=[o_k], writes=[o_k])
        K.op("dve", lambda e: e.scalar_tensor_tensor(out=a[0:rows, 0:q], in0=kf_[0:rows, 0:q], scalar=-TWO_PI, in1=a[0:rows, 0:q],
                                                     op0=ALU.mult, op1=ALU.add), reads=[o_k, o_a], writes=[o_a])
        K.op("dve", lambda e: e.tensor_scalar(out=a[0:rows, 0:q], in0=a[0:rows, 0:q], scalar1=-3.14159, scalar2=3.14159,
                                              op0=ALU.max, op1=ALU.min), reads=[o_a], writes=[o_a])
        K.op("act", lambda e: e.activation(out=dst[0:rows, 0:q], in_=a[0:rows, 0:q], func=AF.Sin), reads=[o_a], writes=[o_dst])

    def load_strip(strip, o_strip, tabb, fi):
        K.dma("sp", strip[:, :, :], tabb[fi], writes=[o_strip])

    o_KF = {}

    def phase_hfilter(l, L):
        T = L + 16
        ch = chunks_of(L)
        nch = len(ch)
        fch = fchunks_of(L)
        tb = tabs[L]
        o_KF[(L, l)] = Obj()
        with ExitStack() as st:
            w1s = sb("hf_w1", [33, 64], F32, st); w2s = sb("hf_w2", [64, 64], F32, st); w3s = sb("hf_w3", [64, 2048], F32, st)
            dls = sb("hf_dl", [128, 512], F32, st); ntt = sb("hf_ntt", [128, nch], F32, st); wfs = sb("hf_wf", [128, nch], F32, st)
            o_w = Obj()
            K.dma("sp", w1s[:], hw1[l], writes=[o_w]); K.dma("sp", w2s[:], hw2[l], writes=[o_w]); K.dma("sp", w3s[:], hw3[l], writes=[o_w])
            K.dma("sp", dls[:], dl_d, writes=[o_w]); K.dma("sp", ntt[:], tb["ntt"], writes=[o_w]); K.dma("sp", wfs[:], tb["wf"], writes=[o_w])
            S = sb("hf_S", [128, nch, 512], BF16, st); o_S = Obj()
            Dd = sb("hf_D", [128, nch, 512], BF16, st); o_D = Obj()
            rl1 = sb("hf_rl1", [128, 512], F32, st); o_rl = Obj()
            h2all = sb("hf_h2all", [64, T], F32, st); o_h2c = [Obj() for _ in ch]

            def ring(name, shape, dt, n=2):
                return Ring([(sb(name, shape, dt, st), Obj()) for _ in range(n)])
            r_z = ring("hf_z", [33, 128], F32); r_a = ring("hf_a", [64, 128], F32); r_h = ring("hf_h", [64, 128], F32)
            r_ki = ring("hf_ki", [64, 128], mybir.dt.int32); r_kf = ring("hf_kf", [64, 128], F32)
            r_wn = ring("hf_wn", [128, 512], F32); r_hf = ring("hf_hf", [128, 512], F32); r_hb = ring("hf_hb", [128, 512], F32)
            r_af = ring("hf_af", [128, 512], BF16); r_ab = ring("hf_ab", [128, 512], BF16)
            r_st = ring("hf_strip", [128, nch, 128], BF16, 4)
            r_kt = ring("hf_kt", [128, 2, 512], F32)
            for o in range(2):
                for ci, (q0, q) in enumerate(ch):
                    h2, o_h2 = h2all[:, q0:q0 + q], o_h2c[ci]
                    if o == 0:
                      z, o_z = r_z.next(); a, o_a = r_a.next(); h1, o_h1 = r_h.next()
                      ki, o_k = r_ki.next(); kf_, _ = r_kf.next()
                      K.dma("sp", z[0:33, 0:q], tb["zt"][:, q0:q0 + q], writes=[o_z])
                      b = psr.next()
                      K.op("pe", lambda e: e.matmul(ps[b][0:64, 0:q], lhsT=w1s[0:33, 0:64], rhs=z[0:33, 0:q], start=True, stop=True),
                           reads=[o_w, o_z], writes=[o_ps[b]])
                      K.op("dve", lambda e: e.tensor_scalar(out=a[0:64, 0:q], in0=ps[b][0:64, 0:q], scalar1=ppc(l, P_HB1, rows=64),
                                                            scalar2=ppc(l, P_HFQ, rows=64), op0=ALU.add, op1=ALU.mult),
                           reads=[o_ps[b], o_pp], writes=[o_a])
                      sin_act(h1, o_h1, a, o_a, ki, kf_, o_k, 64, q)
                      b = psr.next()
                      K.op("pe", lambda e: e.matmul(ps[b][0:64, 0:q], lhsT=w2s[0:64, 0:64], rhs=h1[0:64, 0:q], start=True, stop=True),
                           reads=[o_w, o_h1], writes=[o_ps[b]])
                      K.op("dve", lambda e: e.tensor_scalar(out=a[0:64, 0:q], in0=ps[b][0:64, 0:q], scalar1=ppc(l, P_HB2, rows=64),
                                                            scalar2=ppc(l, P_HFQ, rows=64), op0=ALU.add, op1=ALU.mult),
                           reads=[o_ps[b], o_pp], writes=[o_a])
                      sin_act(h2, o_h2, a, o_a, ki, kf_, o_k, 64, q)
                    bf_ = psr.next(); bb_ = psr.next()
                    K.op("pe", lambda e: e.matmul(ps[bf_][0:q, :], lhsT=h2[0:64, 0:q], rhs=w3s[0:64, o * 1024:o * 1024 + 512], start=True, stop=True),
                         reads=[o_w, o_h2], writes=[o_ps[bf_]])
                    K.op("pe", lambda e: e.matmul(ps[bb_][0:q, :], lhsT=h2[0:64, 0:q], rhs=w3s[0:64, o * 1024 + 512:o * 1024 + 1024], start=True, stop=True),
                         reads=[o_w, o_h2], writes=[o_ps[bb_]])
                    wn, o_wn = r_wn.next(); hf, o_hf = r_hf.next(); hb, o_hb = r_hb.next(); af, o_af = r_af.next(); ab, o_ab = r_ab.next()
                    K.op("act", lambda e: e.activation(out=wn[0:q, :], in_=dls[0:q, :], func=AF.Exp, scale=ntt[0:q, ci:ci + 1]),
                         reads=[o_w], writes=[o_wn])
                    K.op("dve", lambda e: e.tensor_tensor(out=hf[0:q, :], in0=ps[bf_][0:q, :], in1=wn[0:q, :], op=ALU.mult),
                         reads=[o_ps[bf_], o_wn], writes=[o_hf])
                    K.op("dve", lambda e: e.tensor_tensor(out=hb[0:q, :], in0=ps[bb_][0:q, :], in1=wn[0:q, :], op=ALU.mult),
                         reads=[o_ps[bb_], o_wn], writes=[o_hb])
                    if ci == 0:
                        K.op("dve", lambda e: e.memset(hb[0:1, :], 0.0), writes=[o_hb])
                    K.op("pool", lambda e: e.tensor_tensor(out=S[0:q, ci, :], in0=hf[0:q, :], in1=hb[0:q, :], op=ALU.add),
                         reads=[o_hf, o_hb], writes=[o_S])
                    K.op("pool", lambda e: e.tensor_tensor(out=Dd[0:q, ci, :], in0=hb[0:q, :], in1=hf[0:q, :], op=ALU.subtract),
                         reads=[o_hf, o_hb], writes=[o_D])
                    K.op("act", lambda e: e.activation(out=af[0:q, :], in_=hf[0:q, :], func=AF.Abs), reads=[o_hf], writes=[o_af])
                    K.op("act", lambda e: e.activation(out=ab[0:q, :], in_=hb[0:q, :], func=AF.Abs), reads=[o_hb], writes=[o_ab])
                    K.op("pe", lambda e: e.matmul(ps[7][:, :], lhsT=onesb[0:q, :], rhs=af[0:q, :], start=(ci == 0), stop=False),
                         reads=[o_cmb, o_af], writes=[o_ps[7]])
                    K.op("pe", lambda e: e.matmul(ps[7][:, :], lhsT=onesb[0:q, :], rhs=ab[0:q, :], start=False, stop=(ci == nch - 1)),
                         reads=[o_cmb, o_ab], writes=[o_ps[7]])
                K.mark('hf_mlp')
                K.op("dve", lambda e: e.reciprocal(out=rl1[:], in_=ps[7][:, :]), reads=[o_ps[7]], writes=[o_rl])
                for fi, (f0, mf) in enumerate(fch):
                    sc_, o_sc = r_st.next(); ss_, o_ss = r_st.next()
                    load_strip(sc_, o_sc, tb["hcb"], fi)
                    load_strip(ss_, o_ss, tb["hsb"], fi)
                    bre = psr.next(); bim = psr.next()
                    for ci, (q0, q) in enumerate(ch):
                        K.op("pe", lambda e: e.matmul(ps[bre][0:mf, :], lhsT=sc_[0:q, ci, 0:mf], rhs=S[0:q, ci, :], start=(ci == 0), stop=(ci == nch - 1)),
                             reads=[o_sc, o_S], writes=[o_ps[bre]])
                    for ci, (q0, q) in enumerate(ch):
                        K.op("pe", lambda e: e.matmul(ps[bim][0:mf, :], lhsT=ss_[0:q, ci, 0:mf], rhs=Dd[0:q, ci, :], start=(ci == 0), stop=(ci == nch - 1)),
                             reads=[o_ss, o_D], writes=[o_ps[bim]])
                    kt, o_kt = r_kt.next()
                    K.op("dve", lambda e: e.scalar_tensor_tensor(out=kt[0:mf, 0, :], in0=ps[bre][0:mf, :], scalar=wfs[0:mf, fi:fi + 1], in1=rl1[0:mf, :],
                                                                 op0=ALU.mult, op1=ALU.mult), reads=[o_ps[bre], o_w, o_rl], writes=[o_kt])
                    K.op("dve", lambda e: e.scalar_tensor_tensor(out=kt[0:mf, 1, :], in0=ps[bim][0:mf, :], scalar=wfs[0:mf, fi:fi + 1], in1=rl1[0:mf, :],
                                                                 op0=ALU.mult, op1=ALU.mult), reads=[o_ps[bim], o_w, o_rl], writes=[o_kt])
                    K.dma("pool", tb["kf"][l][fi, 0:mf, o, :, :], kt[0:mf, :, :], reads=[o_kt], writes=[o_KF[(L, l)]])
            K.barrier()

    def phase_hyena(l, L):
        T = L + 16
        ch = chunks_of(L)
        nch = len(ch)
        fch = fchunks_of(L)
        nfc = len(fch)
        tl = tiles_of(L)
        tb = tabs[L]
        okf = o_KF[(L, l)]
        with ExitStack() as st:
            inT = sb("hy_in", [128, 4, T], BF16, st); o_in = Obj()
            itok = sb("hy_tok", [128, nch, 512], BF16, st); o_tok = Obj()
            PQ = sb("hy_pq", [128, nfc, 2, 512], BF16, st); o_PQ = Obj()

            def ring(name, shape, dt, n=2):
                return Ring([(sb(name, shape, dt, st), Obj()) for _ in range(n)])
            r_st = ring("hy_strip", [128, nch, 128], BF16, 3)
            r_kt = ring("hy_kt", [128, 2, 512], F32)
            r_t = ring("hy_t", [128, 512], F32, 4)
            r_tc = ring("hy_tc", [128, 512], BF16, 4); r_ts = ring("hy_ts", [128, 512], BF16, 4)
            r_xt = ring("hy_xt", [128, 4, 512], BF16); r_ob = ring("hy_ob", [128, 4, 512], BF16)
            K.dma("sp", inT[:], PR[R_V:R_V + 4, :, 0:T].rearrange("c p t -> p c t"), reads=o_PR[R_V:R_V + 4], writes=[o_in])
            for o in range(2):
                for ci, (q0, q) in enumerate(ch):
                    b = psr.next()
                    for c in range(4):
                        K.op("pe", lambda e: e.matmul(ps[b][0:q, c * 128:(c + 1) * 128], lhsT=inT[:, c, q0:q0 + q], rhs=identb, start=True, stop=True),
                             reads=[o_in, o_cmb], writes=[o_ps[b]])
                    if ci % 2 == 0:
                        K.op("act", lambda e: e.activation(out=itok[0:q, ci, :], in_=ps[b][0:q, :], func=AF.Copy), reads=[o_ps[b]], writes=[o_tok])
                    else:
                        K.op("dve", lambda e: e.tensor_copy(out=itok[0:q, ci, :], in_=ps[b][0:q, :]), reads=[o_ps[b]], writes=[o_tok])
                K.mark('hy_tr')
                for fi, (f0, mf) in enumerate(fch):
                    sc_, o_sc = r_st.next(); ss_, o_ss = r_st.next()
                    load_strip(sc_, o_sc, tb["hcb"], fi)
                    load_strip(ss_, o_ss, tb["hsb"], fi)
                    kt, o_kt = r_kt.next()
                    K.dma("sp", kt[0:mf, :, :], tb["kf"][l][fi, 0:mf, o, :, :], reads=[okf], writes=[o_kt])
                    bA = psr.next(); bB = psr.next()
                    for ci, (q0, q) in enumerate(ch):
                        K.op("pe", lambda e: e.matmul(ps[bA][0:mf, :], lhsT=sc_[0:q, ci, 0:mf], rhs=itok[0:q, ci, :], start=(ci == 0), stop=(ci == nch - 1)),
                             reads=[o_sc, o_tok], writes=[o_ps[bA]])
                    for ci, (q0, q) in enumerate(ch):
                        K.op("pe", lambda e: e.matmul(ps[bB][0:mf, :], lhsT=ss_[0:q, ci, 0:mf], rhs=itok[0:q, ci, :], start=(ci == 0), stop=(ci == nch - 1)),
                             reads=[o_ss, o_tok], writes=[o_ps[bB]])
                    t1, o_t1 = r_t.next(); t2, o_t2 = r_t.next(); t3, o_t3 = r_t.next(); t4, o_t4 = r_t.next()
                    K.op("dve", lambda e: e.tensor_tensor(out=t1[0:mf, :], in0=ps[bA][0:mf, :], in1=kt[0:mf, 0, :], op=ALU.mult), reads=[o_ps[bA], o_kt], writes=[o_t1])
                    K.op("dve", lambda e: e.tensor_tensor(out=t2[0:mf, :], in0=ps[bB][0:mf, :], in1=kt[0:mf, 1, :], op=ALU.mult), reads=[o_ps[bB], o_kt], writes=[o_t2])
                    K.op("dve", lambda e: e.tensor_tensor(out=t3[0:mf, :], in0=ps[bB][0:mf, :], in1=kt[0:mf, 0, :], op=ALU.mult), reads=[o_ps[bB], o_kt], writes=[o_t3])
                    K.op("dve", lambda e: e.tensor_tensor(out=t4[0:mf, :], in0=ps[bA][0:mf, :], in1=kt[0:mf, 1, :], op=ALU.mult), reads=[o_ps[bA], o_kt], writes=[o_t4])
                    K.op("pool", lambda e: e.tensor_tensor(out=PQ[0:mf, fi, 0, :], in0=t1[0:mf, :], in1=t2[0:mf, :], op=ALU.add), reads=[o_t1, o_t2], writes=[o_PQ])
                    K.op("pool", lambda e: e.tensor_tensor(out=PQ[0:mf, fi, 1, :], in0=t3[0:mf, :], in1=t4[0:mf, :], op=ALU.subtract), reads=[o_t3, o_t4], writes=[o_PQ])
                K.mark('hy_fwd')
                for (t0, tw) in tl:
                    bs_ = [psr.next() for _ in range(4)]
                    for fi, (f0, mf) in enumerate(fch):
                        tc, o_tc = r_tc.next(); ts, o_ts = r_ts.next()
                        K.dma("sp", tc[0:mf, 0:tw], tb["hc"][f0:f0 + mf, t0:t0 + tw], writes=[o_tc])
                        K.dma("sp", ts[0:mf, 0:tw], tb["hs"][f0:f0 + mf, t0:t0 + tw], writes=[o_ts])
                        for c in range(4):
                            b = bs_[c]
                            K.op("pe", lambda e: e.matmul(ps[b][:, 0:tw], lhsT=PQ[0:mf, fi, 0, c * 128:(c + 1) * 128], rhs=tc[0:mf, 0:tw],
                                                          start=(fi == 0), stop=False), reads=[o_PQ, o_tc], writes=[o_ps[b]])
                            K.op("pe", lambda e: e.matmul(ps[b][:, 0:tw], lhsT=PQ[0:mf, fi, 1, c * 128:(c + 1) * 128], rhs=ts[0:mf, 0:tw],
                                                          start=False, stop=(fi == nfc - 1)), reads=[o_PQ, o_ts], writes=[o_ps[b]])
                    xt, o_xt = r_xt.next()
                    r0 = R_X1 if o == 0 else R_X2
                    K.dma("sp", xt[:, :, 0:tw], PR[r0:r0 + 4, :, t0:t0 + tw].rearrange("c p t -> p c t"), reads=o_PR[r0:r0 + 4], writes=[o_xt])
                    ob, o_ob = r_ob.next()
                    for c in range(4):
                        b = bs_[c]
                        tt_, o_tt = r_t.next()
                        K.op("dve", lambda e: e.scalar_tensor_tensor(out=tt_[:, 0:tw], in0=inT[:, c, t0:t0 + tw], scalar=ppc(l, P_HB + o * 4 + c),
                                                                     in1=ps[b][:, 0:tw], op0=ALU.mult, op1=ALU.add),
                             reads=[o_in, o_pp, o_ps[b]], writes=[o_tt])
                        if o == 0:
                            K.op("pool", lambda e: e.tensor_tensor(out=inT[:, c, t0:t0 + tw], in0=tt_[:, 0:tw], in1=xt[:, c, 0:tw], op=ALU.mult),
                                 reads=[o_tt, o_xt], writes=[o_in])
                        else:
                            K.op("pool", lambda e: e.tensor_tensor(out=ob[:, c, 0:tw], in0=tt_[:, 0:tw], in1=xt[:, c, 0:tw], op=ALU.mult),
                                 reads=[o_tt, o_xt], writes=[o_ob])
                    if o == 1:
                        K.dma("pool", BR[8:12, :, t0:t0 + tw].rearrange("c p t -> p c t"), ob[:, :, 0:tw], reads=[o_ob], writes=o_BR[8:12])
                K.mark('hy_inv')
            K.barrier()

    def load_w_into(dst3, o_dst, src_ap, kc, ncols, bufs=None):
        K.dma("pool", dst3, src_ap.rearrange("(k p) m -> p k m", p=128), writes=[o_dst])

    def phase_merge(l, L):
        T = L + 16
        tl = tiles_of(L)
        with ExitStack() as st:
            wbr = sb("mg_wbr", [128, 16, 1024], BF16, st); o_wbr_g = [Obj() for _ in range(4)]
            wo = sb("mg_wo", [128, 8, 1024], BF16, st); o_wo_g = [Obj() for _ in range(4)]
            stgr = None
            wbv = w_branch[l].rearrange("k r m -> (k r) m")
            for cg in range(4):
                for kg in range(2):
                    load_w_into(wbr[:, kg * 8:(kg + 1) * 8, cg * 256:(cg + 1) * 256], o_wbr_g[cg],
                                wbv[kg * 1024:(kg + 1) * 1024, cg * 256:(cg + 1) * 256], 8, 256, stgr)
            for cg in range(4):
                load_w_into(wo[:, :, cg * 256:(cg + 1) * 256], o_wo_g[cg], w_out[l, :, cg * 256:(cg + 1) * 256], 8, 256, stgr)
            brt2 = [[(sb("mg_br", [128, 4, 512], BF16, st), Obj()) for _ in range(4)] for _ in range(2)]
            gt = [(sb("mg_g", [128, 8, 512], BF16, st), Obj()) for _ in range(4)]
            r_acc = Ring([(sb("mg_acc", [128, 512], F32, st), Obj()) for _ in range(2)])
            r_tmp = Ring([(sb("mg_tmp", [128, 512], F32, st), Obj()) for _ in range(2)])
            r_mg = Ring([(sb("mg_m", [128, 8, 512], BF16, st), Obj()) for _ in range(2)])
            r_x = Ring([(sb("mg_x", [128, 8, 512], F32, st), Obj()) for _ in range(2)])
            for tix, (t0, tw) in enumerate(tl):
                brt = brt2[tix % 2]
                for k in range(4):
                    K.dma("sp", brt[k][0][:, :, 0:tw], BR[4 * k:4 * k + 4, :, t0:t0 + tw].rearrange("c p t -> p c t"),
                          reads=o_BR[4 * k:4 * k + 4], writes=[brt[k][1]])
                    K.dma("sp", gt[k][0][:, :, 0:tw], PR[R_G + 8 * k:R_G + 8 * k + 8, :, t0:t0 + tw].rearrange("c p t -> p c t"),
                          reads=o_PR[R_G + 8 * k:R_G + 8 * k + 8], writes=[gt[k][1]])
                xr, o_xr = r_x.next()
                K.dma("sp", xr[:, :, 0:tw], XT[:, :, t0:t0 + tw].rearrange("c p t -> p c t"), reads=xto(t0), writes=[o_xr])
                mg, o_mg = r_mg.next()
                for m in range(8):
                    acc, o_acc = r_acc.next()
                    for k in range(4):
                        b = psr.next()
                        for kc in range(4):
                            K.op("pe", lambda e: e.matmul(ps[b][:, 0:tw], lhsT=wbr[:, k * 4 + kc, m * 128:(m + 1) * 128],
                                                          rhs=brt[k][0][:, kc, 0:tw], start=(kc == 0), stop=(kc == 3)),
                                 reads=[o_wbr_g[m // 2], brt[k][1]], writes=[o_ps[b]])
                        if k == 0:
                            K.op("dve", lambda e: e.tensor_tensor(out=acc[:, 0:tw], in0=ps[b][:, 0:tw], in1=gt[k][0][:, m, 0:tw], op=ALU.mult),
                                 reads=[o_ps[b], gt[k][1]], writes=[o_acc])
                        else:
                            tmp, o_tmp = r_tmp.next()
                            K.op("dve", lambda e: e.tensor_tensor(out=tmp[:, 0:tw], in0=ps[b][:, 0:tw], in1=gt[k][0][:, m, 0:tw], op=ALU.mult),
                                 reads=[o_ps[b], gt[k][1]], writes=[o_tmp])
                            if k < 3:
                                K.op("dve", lambda e: e.tensor_tensor(out=acc[:, 0:tw], in0=acc[:, 0:tw], in1=tmp[:, 0:tw], op=ALU.add),
                                     reads=[o_acc, o_tmp], writes=[o_acc])
                            else:
                                K.op("dve", lambda e: e.tensor_tensor(out=mg[:, m, 0:tw], in0=acc[:, 0:tw], in1=tmp[:, 0:tw], op=ALU.add),
                                     reads=[o_acc, o_tmp], writes=[o_mg])
                for m2 in range(8):
                    b = psr.next()
                    for kc in range(8):
                        K.op("pe", lambda e: e.matmul(ps[b][:, 0:tw], lhsT=wo[:, kc, m2 * 128:(m2 + 1) * 128], rhs=mg[:, kc, 0:tw],
                                                      start=(kc == 0), stop=(kc == 7)), reads=[o_wo_g[m2 // 2], o_mg], writes=[o_ps[b]])
                    K.op("dve", lambda e: e.tensor_tensor(out=xr[:, m2, 0:tw], in0=xr[:, m2, 0:tw], in1=ps[b][:, 0:tw], op=ALU.add),
                         reads=[o_xr, o_ps[b]], writes=[o_xr])
                K.dma("pool", XT[:, :, t0:t0 + tw].rearrange("c p t -> p c t"), xr[:, :, 0:tw], reads=[o_xr], writes=xto(t0))
            K.barrier()

    def phase_ffn(l, L):
        T = L + 16
        tl = tiles_of(L)
        with ExitStack() as st:
            HT = sb("ff_ht", [128, 8, T], BF16, st); o_HTt = [Obj() for _ in tl]
            wbufs = dict(wb=Ring([(sb("ff_wb", [128, 8, 256], BF16, st), Obj()) for _ in range(6)]))
            plan = []
            for jp_ in range(11):
                plan.append((w_up[l, :, jp_ * 256:(jp_ + 1) * 256], 8, 256))
                plan.append((w_up[l, :, DFF + jp_ * 256:DFF + (jp_ + 1) * 256], 8, 256))
            ws = WStream(wbufs, plan, 4)
            ws.prime()
            rmsnorm_to(HT, o_HTt, l, P_NFFN, tl, keep=(st if L == 2048 else None))
            K.mark('ff_norm')
            rows = Ring([(sb("ff_row", [128, T + 2], F32, st), Obj()) for _ in range(4)])
            outs = Ring([(sb("ff_out", [128, T], BF16, st), Obj()) for _ in range(2)])
            for (rw, o_rw) in rows.items:
                K.op("pool", lambda e: e.memset(rw[:, 0:1], 0.0), writes=[o_rw])
                K.op("pool", lambda e: e.memset(rw[:, T + 1:T + 2], 0.0), writes=[o_rw])
            flip = [0]

            def gemm_row(wb, o_wb, c0, dst, o_dst):
                for ti, (t0, tw) in enumerate(tl):
                    b = psr.next()
                    for kc in range(8):
                        K.op("pe", lambda e: e.matmul(ps[b][:, 0:tw], lhsT=wb[:, kc, c0:c0 + 128], rhs=HT[:, kc, t0:t0 + tw],
                                                      start=(kc == 0), stop=(kc == 7)), reads=[o_wb, o_HTt[ti]], writes=[o_ps[b]])
                    d = dst[:, 1 + t0:1 + t0 + tw]
                    flip[0] ^= 1
                    if flip[0]:
                        K.op("act", lambda e: e.activation(out=d, in_=ps[b][:, 0:tw], func=AF.Copy), reads=[o_ps[b]], writes=[o_dst])
                    else:
                        K.op("dve", lambda e: e.tensor_copy(out=d, in_=ps[b][:, 0:tw]), reads=[o_ps[b]], writes=[o_dst])

            for jp in range(11):
                wa, o_wa = ws.get(2 * jp)
                wv, o_wv = ws.get(2 * jp + 1)
                for jj in range(2):
                    j = jp * 2 + jj
                    ra, o_ra = rows.next(); ta, o_ta = rows.next(); rv, o_rv = rows.next(); tv, o_tv = rows.next()
                    gemm_row(wa, o_wa, jj * 128, ra, o_ra)
                    gemm_row(wv, o_wv, jj * 128, rv, o_rv)
                    conv3(ta[:, 1:T + 1], o_ta, ra, o_ra, l, P_FCW + 3 * j, T)
                    conv3(tv[:, 1:T + 1], o_tv, rv, o_rv, l, P_FCW + 3 * (22 + j), T)
                    K.op("act", lambda e: e.activation(out=ta[:, 1:T + 1], in_=ta[:, 1:T + 1], func=AF.Silu), reads=[o_ta], writes=[o_ta])
                    ob, o_ob = outs.next()
                    K.op("dve", lambda e: e.tensor_tensor(out=ob[:, 0:T], in0=ta[:, 1:T + 1], in1=tv[:, 1:T + 1], op=ALU.mult),
                         reads=[o_ta, o_tv], writes=[o_ob])
                    K.dma("sp", FA[j][:, 0:T], ob[:, 0:T], reads=[o_ob], writes=[o_FA[j]])
            K.barrier()
        with ExitStack() as st:
            K.mark('ff_up')
            wd = sb("ff_wd", [128, 22, 1024], BF16, st); o_wd_g = [Obj() for _ in range(4)]
            stgr = None
            for cg in range(4):
                for (k0, kn) in ((0, 8), (8, 8), (16, 6)):
                    load_w_into(wd[:, k0:k0 + kn, cg * 256:(cg + 1) * 256], o_wd_g[cg],
                                w_down[l, k0 * 128:(k0 + kn) * 128, cg * 256:(cg + 1) * 256], kn, 256, stgr)
            r_fa = Ring([(sb("ff_fa", [128, 22, 512], BF16, st), Obj()) for _ in range(2)])
            r_x = Ring([(sb("ff_x", [128, 8, 512], F32, st), Obj()) for _ in range(2)])
            for (t0, tw) in tl:
                fa, o_fa = r_fa.next()
                xr, o_xr = r_x.next()
                K.dma("sp", fa[:, :, 0:tw], FA[:, :, t0:t0 + tw].rearrange("c p t -> p c t"), reads=o_FA, writes=[o_fa])
                K.dma("sp", xr[:, :, 0:tw], XT[:, :, t0:t0 + tw].rearrange("c p t -> p c t"), reads=xto(t0), writes=[o_xr])
                for m in range(8):
                    b = psr.next()
                    for kc in range(22):
                        K.op("pe", lambda e: e.matmul(ps[b][:, 0:tw], lhsT=wd[:, kc, m * 128:(m + 1) * 128], rhs=fa[:, kc, 0:tw],
                                                      start=(kc == 0), stop=(kc == 21)), reads=[o_wd_g[m // 2], o_fa], writes=[o_ps[b]])
                    K.op("dve", lambda e: e.tensor_tensor(out=xr[:, m, 0:tw], in0=xr[:, m, 0:tw], in1=ps[b][:, 0:tw], op=ALU.add),
                         reads=[o_xr, o_ps[b]], writes=[o_xr])
                K.dma("pool", XT[:, :, t0:t0 + tw].rearrange("c p t -> p c t"), xr[:, :, 0:tw], reads=[o_xr], writes=xto(t0))
            K.barrier()

    seqs = [(xp, yp, i, 2048) for i in range(nP)] + [(xs, ys, i, 4096) for i in range(nS)]
    for (x_d, y_d, si, L) in seqs:
        phase_input(x_d, si, L)
        K.barrier()
        nch = 1 + L // 128
        with ExitStack() as sq:
            dtS = sb("dtS", [128, nch, 16], F32, sq)
            dtA = sb("dtA", [128, nch, 16], F32, sq)
            o_dt = Obj()
            for l in range(n_layers):
                phase1(l, L, dtS, dtA, o_dt)
                K.mark('phase1')
                if stop_after == "p1":
                    break
                if stop_after != "nossd":
                    phase_ssd(l, L, dtS, dtA, o_dt)
                    K.mark('phase_ssd')
                if stop_after == "ssd":
                    break
                phase_fnet(l, L)
                K.mark('phase_fnet')
                if (L, l) not in o_KF:
                    phase_hfilter(l, L)
                    K.mark('phase_hfilter')
                phase_hyena(l, L)
                K.mark('phase_hyena')
                if stop_after in ("hy", "nossd"):
                    break
                phase_merge(l, L)
                K.mark('phase_merge')
                if stop_after == "mix":
                    break
                phase_ffn(l, L)
                K.mark('phase_ffn')
            if debug:
                dbg = dscr("dbg_dt", [128, nch, 32], F32)
                K.dma("pool", dbg[:, :, 0:16], dtS[:], reads=[o_dt])
                K.dma("pool", dbg[:, :, 16:32], dtA[:], reads=[o_dt])
            K.barrier()
        phase_final(y_d, si, L)
        K.barrier()

    K.barrier()
    es.close()
    return nc, K


_BF = ml_dtypes.bfloat16


def _const_tables(L):
    T = L + 16
    N = 2 * T
    nch = 1 + L // 128
    a = np.arange(T + 1, dtype=np.int64)
    m = (a[:, None] * a[None, :]) % N
    ang = (2.0 * np.pi / N) * m.astype(np.float64)
    hc = np.cos(ang).astype(np.float32).astype(_BF)
    hs = np.sin(ang).astype(np.float32).astype(_BF)
    t = np.arange(T, dtype=np.int64)
    m2 = (t[:, None] * t[None, :]) % T
    ang2 = (2.0 * np.pi / T) * m2.astype(np.float64)
    sc = 1.0 / math.sqrt(T)
    fc = (np.cos(ang2) * sc).astype(np.float32).astype(_BF)
    fs = (-np.sin(ang2) * sc).astype(np.float32).astype(_BF)
    tt = np.linspace(0.0, 1.0, T, dtype=np.float32)[:, None]
    bands = 16
    w = (2.0 * math.pi / T) * np.arange(T, dtype=np.float32)[:, None]
    fr = np.linspace(1e-4, bands - 1, bands, dtype=np.float32)[None, :]
    z = np.concatenate([tt, np.cos(fr * w), -np.sin(fr * w)], axis=-1).astype(np.float32)
    zt = np.ascontiguousarray(z.T)
    ntt = np.zeros((128, nch), np.float32)
    wf = np.zeros((128, nch), np.float32)
    for ci, (t0, q) in enumerate(chunks_of(L)):
        ntt[0:q, ci] = -tt[t0:t0 + q, 0]
    wfull = np.full(T + 1, 2.0 / N, np.float32)
    wfull[0] = 1.0 / N
    wfull[T] = 1.0 / N
    for ci, (f0, q) in enumerate(fchunks_of(L)):
        wf[0:q, ci] = wfull[f0:f0 + q]
    tidx = np.full((nch, 128), T + 1, np.int64)
    fidx = np.full((nch, 128), T + 1, np.int64)
    for ci, (t0, q) in enumerate(chunks_of(L)):
        tidx[ci, 0:q] = np.arange(t0, t0 + q)
    for ci, (f0, q) in enumerate(fchunks_of(L)):
        fidx[ci, 0:q] = np.arange(f0, f0 + q)

    def blocked(tab):
        pad = np.zeros((T + 2, T + 2), tab.dtype)
        pad[0:T + 1, 0:T + 1] = tab
        out = np.empty((nch, 128, nch, 128), tab.dtype)
        for fi in range(nch):
            out[fi] = pad[tidx.T[:, :, None], fidx[fi][None, None, :]]
        return out
    return dict(fc=fc, fs=fs, hc=hc, hs=hs, hcb=blocked(hc), hsb=blocked(hs), zt=zt, ntt=ntt, wf=wf)


def _const_common():
    j = np.arange(128)
    cm = np.zeros((128, NCM), np.float32)
    cm[:, CM_ID:CM_ID + 128] = np.eye(128)
    cm[:, CM_LE:CM_LE + 128] = (j[:, None] <= j[None, :])
    cm[:, CM_GE:CM_GE + 128] = (j[:, None] >= j[None, :])
    cm[:, CM_GT:CM_GT + 128] = (j[:, None] > j[None, :])
    cm[:, CM_LT:CM_LT + 128] = (j[:, None] < j[None, :])
    cm[:, CM_ONE:CM_ONE + 128] = 1.0
    cmb = np.zeros((128, 512), np.float32)
    cmb[:, 0:128] = np.eye(128)
    cmb[:, 128:256] = 1.0
    ang = 2.0 * np.pi * ((j[:, None] * j[None, :]) % 128) / 128.0
    cmb[:, 256:384] = np.cos(ang) / math.sqrt(128.0)
    cmb[:, 384:512] = np.sin(ang) / math.sqrt(128.0)
    max_decay = math.log(1e-2) / 0.3
    min_decay = math.log(1e-2) / 1.5
    deltas = np.abs(np.linspace(min_decay, max_decay, BW, dtype=np.float32))
    dl = np.ascontiguousarray(np.broadcast_to(deltas[None, :], (128, BW))).astype(np.float32)
    return dict(cm=cm, cmb=cmb.astype(_BF), dl=dl)


def _pack_params(inp):
    pp = np.zeros((2, 128, NPP), np.float32)
    pb = np.zeros((2, 128, 32), np.float32)

    def cols(v):
        return np.ascontiguousarray(np.asarray(v, np.float32).reshape(-1, 128).T)

    for l in range(2):
        pp[l, :, P_NMIX:P_NMIX + 8] = cols(inp["norm_mix"][l])
        pp[l, :, P_NFFN:P_NFFN + 8] = cols(inp["norm_ffn"][l])
        pp[l, :, P_NFIN:P_NFIN + 8] = cols(inp["norm_final"])
        w = np.asarray(inp["ssm_conv_w"][l], np.float32)
        pp[l, :, P_SCW:P_SCW + 24] = w.reshape(3, 8, 128).transpose(2, 1, 0).reshape(128, 24)
        pp[l, :, P_SCB:P_SCB + 8] = cols(inp["ssm_conv_b"][l])
        pp[l, :, P_SD:P_SD + 4] = cols(np.repeat(np.asarray(inp["ssm_d"][l], np.float32), 64))
        pp[l, :, P_SNW:P_SNW + 4] = cols(inp["ssm_norm"][l])
        w = np.asarray(inp["hyena_conv_w"][l], np.float32)
        pp[l, :, P_HCW:P_HCW + 36] = w.reshape(3, 12, 128).transpose(2, 1, 0).reshape(128, 36)
        pp[l, :, P_HB:P_HB + 8] = cols(np.asarray(inp["hyena_bias"][l], np.float32).reshape(-1))
        w = np.asarray(inp["sc_conv_w"][l], np.float32)
        pp[l, :, P_CCW:P_CCW + 12] = w.reshape(3, 4, 128).transpose(2, 1, 0).reshape(128, 12)
        w = np.asarray(inp["ffn_conv_w"][l], np.float32)
        pp[l, :, P_FCW:P_FCW + 132] = w.reshape(3, 44, 128).transpose(2, 1, 0).reshape(128, 132)
        pp[l, 0:64, P_HB1] = np.asarray(inp["hyena_b1"][l], np.float32)
        pp[l, 0:64, P_HB2] = np.asarray(inp["hyena_b2"][l], np.float32)
        pp[l, 0:64, P_HFQ] = np.asarray(inp["hyena_freq"][l], np.float32)
        pb[l, :, 0:16] = np.broadcast_to(np.asarray(inp["ssm_dt_bias"][l], np.float32).reshape(1, 16), (128, 16))
        pb[l, :, 16:32] = np.broadcast_to(np.asarray(inp["ssm_a_log"][l], np.float32).reshape(1, 16), (128, 16))
    return pp, pb


def make_in_maps(inp, nP, nS, n_cores):
    common = _const_common()
    pp, pb = _pack_params(inp)
    base = dict(common)
    base["pp"] = pp
    base["pb"] = pb
    for k in ("meta_tokens", "w_in", "w_branch", "w_out", "w_up", "w_down", "hyena_w1", "hyena_w2", "hyena_w3"):
        base[k] = np.ascontiguousarray(np.asarray(inp[k], np.float32))
    for L in sorted(set(([2048] if nP else []) + ([4096] if nS else []))):
        for k, v in _const_tables(L).items():
            base["%s%d" % (k, L)] = v
    x_prompt = np.asarray(inp["x_prompt"], np.float32)
    x_sample = np.asarray(inp["x_sample"], np.float32)
    maps = []
    for c in range(n_cores):
        m = dict(base)
        m["xp"] = np.ascontiguousarray(x_prompt[c * nP:(c + 1) * nP]) if nP else np.zeros((1, 2048, D), np.float32)
        m["xs"] = np.ascontiguousarray(x_sample[c * nS:(c + 1) * nS]) if nS else np.zeros((1, 4096, D), np.float32)
        maps.append(m)
    return maps


def kernel(x_prompt, x_sample, meta_tokens, norm_mix, w_in, ssm_conv_w, ssm_conv_b, ssm_dt_bias,
           ssm_a_log, ssm_d, ssm_norm, hyena_conv_w, hyena_w1, hyena_b1, hyena_w2, hyena_b2, hyena_w3,
           hyena_freq, hyena_bias, sc_conv_w, w_branch, w_out, norm_ffn, ffn_conv_w, w_up, w_down,
           norm_final):
    inp = dict(x_prompt=x_prompt, x_sample=x_sample, meta_tokens=meta_tokens, norm_mix=norm_mix, w_in=w_in,
               ssm_conv_w=ssm_conv_w, ssm_conv_b=ssm_conv_b, ssm_dt_bias=ssm_dt_bias, ssm_a_log=ssm_a_log,
               ssm_d=ssm_d, ssm_norm=ssm_norm, hyena_conv_w=hyena_conv_w, hyena_w1=hyena_w1, hyena_b1=hyena_b1,
               hyena_w2=hyena_w2, hyena_b2=hyena_b2, hyena_w3=hyena_w3, hyena_freq=hyena_freq,
               hyena_bias=hyena_bias, sc_conv_w=sc_conv_w, w_branch=w_branch, w_out=w_out, norm_ffn=norm_ffn,
               ffn_conv_w=ffn_conv_w, w_up=w_up, w_down=w_down, norm_final=norm_final)
    nP, nS = 4, 1
    nc, _ = build_program(nP, nS)
    maps = make_in_maps(inp, nP, nS, 8)
    res = run_bass_kernel_spmd(nc, maps, core_ids=list(range(8)))
    y_p = np.concatenate([np.asarray(r["yp"], np.float32) for r in res.results], axis=0)
    y_s = np.concatenate([np.asarray(r["ys"], np.float32) for r in res.results], axis=0)
    return (y_p, y_s)
```

```python
import math
from contextlib import ExitStack

import numpy as np
import ml_dtypes

import concourse.bass as bass
import concourse.mybir as mybir
from concourse.bass_utils import run_bass_kernel_spmd

F32 = mybir.dt.float32
BF16 = mybir.dt.bfloat16
AF = mybir.ActivationFunctionType
ALU = mybir.AluOpType

D = 1024
NMETA = 16
EPS = 1e-6
BW = 512
DIN = 9232
DFF = 2816
TMAX = 4112
O_FN, O_Z, O_XBC, O_DT, O_HY, O_SC, O_G = 0, 512, 1024, 2048, 2064, 3600, 5136
R_FN, R_SZ, R_XS, R_B, R_C, R_V, R_X1, R_X2, R_G = 0, 4, 8, 12, 14, 16, 20, 24, 28
NPR = 60
P_NMIX, P_NFFN, P_NFIN, P_SCW, P_SCB, P_SD, P_SNW, P_HCW, P_HB, P_CCW, P_FCW, P_HB1, P_HB2, P_HFQ = (
    0, 8, 16, 24, 48, 56, 60, 64, 100, 108, 120, 252, 253, 254)
NPP = 256
CM_ID, CM_LE, CM_GE, CM_GT, CM_LT, CM_ONE = 0, 128, 256, 384, 512, 640
NCM = 768


def chunks_of(L):
    return [(0, 16)] + [(16 + 128 * i, 128) for i in range(L // 128)]


def tiles_of(L):
    return [(0, 16)] + [(16 + 512 * i, 512) for i in range(L // 512)]


def fchunks_of(L):
    return [(0, 17)] + [(17 + 128 * i, 128) for i in range(L // 128)]


class Obj:
    __slots__ = ("w", "r", "name")

    def __init__(self, name=""):
        self.w = None
        self.r = {}
        self.name = name


class Emit:
    NDS = 40

    def __init__(self, nc, es):
        self.nc = nc
        self.eng = {"pe": nc.tensor, "act": nc.scalar, "dve": nc.vector, "pool": nc.gpsimd, "sp": nc.sync}
        self.sem = {}
        self.cnt = {}
        for e in ("pe", "act", "dve", "pool"):
            self.sem[e] = es.enter_context(nc.semaphore("s_" + e))
            self.cnt[e] = 0
        self.dsem = [es.enter_context(nc.semaphore("d%d" % i)) for i in range(self.NDS)]
        self.dcnt = [0] * self.NDS
        self.dnext = 0
        self.seen = {e: {} for e in self.eng}
        self.nins = 0
        self.marks = []

    def _wait(self, e, deps):
        best = {}
        for (s, v) in deps:
            if e == "pe" and s is self.sem["pe"]:
                continue
            k = id(s)
            if k not in best or best[k][1] < v:
                best[k] = (s, v)
        sn = self.seen[e]
        for k, (s, v) in best.items():
            if sn.get(k, 0) >= v:
                continue
            self.eng[e].wait_ge(s, v)
            sn[k] = v

    def _deps(self, reads, writes):
        deps = []
        for o in reads:
            if o.w is not None:
                deps.append(o.w)
        for o in writes:
            if o.w is not None:
                deps.append(o.w)
            deps.extend(o.r.values())
        return deps

    def _mark(self, e, tok, reads, writes):
        for o in reads:
            o.r[e] = tok
        for o in writes:
            o.w = tok
            o.r = {}

    def op(self, e, fn, reads=(), writes=()):
        self._wait(e, self._deps(reads, writes))
        ins = fn(self.eng[e])
        self.cnt[e] += 1
        ins.then_inc(self.sem[e], 1)
        self._mark(e, (self.sem[e], self.cnt[e]), reads, writes)
        self.nins += 1
        return ins

    def dma(self, q, out, in_, reads=(), writes=()):
        deps = self._deps(reads, writes)
        i = self.dnext
        self.dnext = (i + 1) % self.NDS
        if self.dcnt[i]:
            deps.append((self.dsem[i], self.dcnt[i]))
        self._wait(q, deps)
        ins = self.eng[q].dma_start(out=out, in_=in_)
        self.dcnt[i] += 16
        ins.then_inc(self.dsem[i], 16)
        self._mark("dma%d" % i, (self.dsem[i], self.dcnt[i]), reads, writes)
        self.nins += 1
        return ins

    def mark(self, name):
        self.marks.append((name, self.cnt["pe"]))

    def barrier(self):
        toks = [(self.sem[e], self.cnt[e]) for e in self.sem if self.cnt[e]]
        toks += [(self.dsem[i], self.dcnt[i]) for i in range(self.NDS) if self.dcnt[i]]
        for e in self.eng:
            self._wait(e, toks)


class Ring:
    def __init__(self, items):
        self.items = items
        self.i = 0

    def next(self):
        it = self.items[self.i]
        self.i = (self.i + 1) % len(self.items)
        return it


def build_program(nP, nS, n_layers=2, stop_after=None, debug=False):
    nc = bass.Bass("TRN2", target_bir_lowering=False)
    es = ExitStack()
    K = Emit(nc, es)

    def din(name, shape, dt=F32):
        return nc.dram_tensor(name, list(shape), dt, kind="ExternalInput").ap()

    def dscr(name, shape, dt, out=False):
        kind = "ExternalOutput" if (out or debug) else "Internal"
        return nc.dram_tensor(name, list(shape), dt, kind=kind).ap()

    xp = din("xp", [max(nP, 1), 2048, D])
    xs = din("xs", [max(nS, 1), 4096, D])
    meta_d = din("meta_tokens", [NMETA, D])
    w_in = din("w_in", [2, D, DIN])
    w_branch = din("w_branch", [2, 4, BW, D])
    w_out = din("w_out", [2, D, D])
    w_up = din("w_up", [2, D, 2 * DFF])
    w_down = din("w_down", [2, DFF, D])
    hw1 = din("hyena_w1", [2, 33, 64])
    hw2 = din("hyena_w2", [2, 64, 64])
    hw3 = din("hyena_w3", [2, 64, 2048])
    pp_d = din("pp", [2, 128, NPP])
    pb_d = din("pb", [2, 128, 32])
    cm_d = din("cm", [128, NCM])
    cmb_d = din("cmb", [128, 512], BF16)
    dl_d = din("dl", [128, 512])
    tabs = {}
    for L in sorted(set(([2048] if nP else []) + ([4096] if nS else []))):
        T = L + 16
        tabs[L] = dict(
            fc=din("fc%d" % L, [T, T], BF16), fs=din("fs%d" % L, [T, T], BF16),
            hc=din("hc%d" % L, [T + 1, T + 1], BF16), hs=din("hs%d" % L, [T + 1, T + 1], BF16),
            hcb=din("hcb%d" % L, [1 + L // 128, 128, 1 + L // 128, 128], BF16),
            hsb=din("hsb%d" % L, [1 + L // 128, 128, 1 + L // 128, 128], BF16),
            zt=din("zt%d" % L, [33, T]), ntt=din("ntt%d" % L, [128, 1 + L // 128]),
            wf=din("wf%d" % L, [128, 1 + L // 128]),
            kf=[dscr("kf%d_%d" % (L, l), [1 + L // 128, 128, 2, 2, 512], F32) for l in range(n_layers)],
        )
    yp = nc.dram_tensor("yp", [max(nP, 1), 2048, D], F32, kind="ExternalOutput").ap()
    ys = nc.dram_tensor("ys", [max(nS, 1), 4096, D], F32, kind="ExternalOutput").ap()
    XT = dscr("XT", [8, 128, TMAX], F32)
    PR = dscr("PR", [NPR, 128, TMAX], BF16)
    BR = dscr("BR", [16, 128, TMAX], BF16)
    FA = dscr("FA", [22, 128, TMAX], BF16)
    o_XTt = [Obj("XT%d" % i) for i in range(10)]

    def xto(t0):
        return [o_XTt[0 if t0 < 16 else 1 + (t0 - 16) // 512]]
    o_PR = [Obj("PR%d" % i) for i in range(NPR)]
    o_BR = [Obj("BR%d" % i) for i in range(16)]
    o_FA = [Obj("FA%d" % i) for i in range(22)]

    uid = [0]

    def sb(name, shape, dt=F32, stack=es):
        uid[0] += 1
        return stack.enter_context(nc.sbuf_tensor("%s_%d" % (name, uid[0]), list(shape), dt))

    cm = sb("cm_sb", [128, NCM]); o_cm = Obj()
    cmb = sb("cmb_sb", [128, 512], BF16); o_cmb = Obj()
    pp = sb("pp_sb", [128, 2, NPP]); o_pp = Obj()
    pb = sb("pb_sb", [128, 2, 32]); o_pb = Obj()
    ps = [es.enter_context(nc.psum_tensor("ps%d" % i, [128, 512], F32)) for i in range(8)]
    o_ps = [Obj("ps%d" % i) for i in range(8)]
    psr = Ring(list(range(7)))

    K.dma("sp", cm[:], cm_d, writes=[o_cm])
    K.dma("sp", cmb[:], cmb_d, writes=[o_cmb])
    K.dma("sp", pp[:], pp_d.rearrange("l p c -> p l c"), writes=[o_pp])
    K.dma("sp", pb[:], pb_d.rearrange("l p c -> p l c"), writes=[o_pb])
    epsc = sb("epsc", [128, 2]);
    K.op("pool", lambda e: e.memset(epsc[:, 0:1], EPS), writes=[o_cm])
    K.op("pool", lambda e: e.memset(epsc[:, 1:2], -math.pi), writes=[o_cm])
    identf = cm[:, CM_ID:CM_ID + 128]
    identb = cmb[:, 0:128]
    onesb = cmb[:, 128:256]
    cs128 = cmb[:, 256:512]

    def ppc(l, col, n=1, rows=128):
        return pp[0:rows, l, col:col + n]

    def rstd_from(r_ap, o_r, ps_ap, o_p, scale):
        K.op("act", lambda e: e.activation(out=r_ap, in_=ps_ap, func=AF.Sqrt, bias=epsc[:, 0:1], scale=scale),
             reads=[o_p, o_cm], writes=[o_r])
        K.op("dve", lambda e: e.reciprocal(out=r_ap, in_=r_ap), reads=[o_r], writes=[o_r])

    def load_w(stack_bufs, src_ap, kc, ncols):
        wb, o_wb = stack_bufs["wb"].next()
        K.dma("pool", wb[:, 0:kc, 0:ncols], src_ap.rearrange("(k p) m -> p k m", p=128), writes=[o_wb])
        return wb, o_wb

    class WStream:
        def __init__(self, bufs, plan, lookahead):
            self.bufs, self.plan, self.la = bufs, plan, lookahead
            self.issued = 0
            self.got = {}

        def prime(self):
            self._fill(0)

        def get(self, i):
            self._fill(i)
            return self.got.pop(i)

        def _fill(self, i):
            while self.issued < min(len(self.plan), i + self.la + 1):
                src, kc, n = self.plan[self.issued]
                self.got[self.issued] = load_w(self.bufs, src, kc, n)
                self.issued += 1

    def rmsnorm_to(HT, o_HTt, l, pcol, tl, keep=None):
        with ExitStack() as own:
            st = keep if keep is not None else own
            xr = Ring([(sb("rn_x%d" % i, [128, 8, 512], F32, st), Obj()) for i in range(2)])
            sq = Ring([(sb("rn_s%d" % i, [128, 8, 512], BF16, st), Obj()) for i in range(2)])
            rs = Ring([(sb("rn_r%d" % i, [128, 512], F32, st), Obj()) for i in range(2)])
            for ti, (t0, tw) in enumerate(tl):
                x, o_x = xr.next()
                s, o_s = sq.next()
                r, o_r = rs.next()
                K.dma("sp", x[:, :, 0:tw], XT[:, :, t0:t0 + tw].rearrange("c p t -> p c t"),
                      reads=xto(t0), writes=[o_x])
                K.op("act", lambda e: e.activation(out=s[:, :, 0:tw], in_=x[:, :, 0:tw], func=AF.Square),
                     reads=[o_x], writes=[o_s])
                b = psr.next()
                for c in range(8):
                    K.op("pe", lambda e: e.matmul(ps[b][:, 0:tw], lhsT=onesb, rhs=s[:, c, 0:tw],
                                                  start=(c == 0), stop=(c == 7)),
                         reads=[o_s, o_cmb], writes=[o_ps[b]])
                rstd_from(r[:, 0:tw], o_r, ps[b][:, 0:tw], o_ps[b], 1.0 / D)
                for c in range(8):
                    K.op("dve", lambda e: e.scalar_tensor_tensor(
                        out=HT[:, c, t0:t0 + tw], in0=x[:, c, 0:tw], scalar=ppc(l, pcol + c),
                        in1=r[:, 0:tw], op0=ALU.mult, op1=ALU.mult),
                        reads=[o_x, o_r, o_pp], writes=[o_HTt[ti]])
            if keep is None:
                K.barrier()

    def phase_input(x_d, si, L):
        with ExitStack() as st:
            xin = Ring([(sb("pi_x%d" % i, [128, D], F32, st), Obj()) for i in range(4)])
            xo = Ring([(sb("pi_o%d" % i, [128, 8, 128], F32, st), Obj()) for i in range(4)])
            for (t0, q) in chunks_of(L):
                x, o_x = xin.next()
                o, o_o = xo.next()
                src = meta_d if t0 == 0 else x_d[si, t0 - 16:t0 - 16 + q, :]
                K.dma("sp", x[0:q, :], src, writes=[o_x])
                b0 = psr.next()
                b1 = psr.next()
                for c in range(8):
                    b = b0 if c < 4 else b1
                    K.op("pe", lambda e: e.matmul(ps[b][:, (c % 4) * 128:(c % 4) * 128 + q],
                                                  lhsT=x[0:q, c * 128:(c + 1) * 128], rhs=identf[0:q, 0:q],
                                                  start=True, stop=True),
                         reads=[o_x, o_cm], writes=[o_ps[b]])
                K.op("act", lambda e: e.activation(
                    out=o[:, 0:4, 0:q], in_=ps[b0][:].rearrange("p (c t) -> p c t", c=4)[:, :, 0:q], func=AF.Copy),
                    reads=[o_ps[b0]], writes=[o_o])
                K.op("dve", lambda e: e.tensor_copy(
                    out=o[:, 4:8, 0:q], in_=ps[b1][:].rearrange("p (c t) -> p c t", c=4)[:, :, 0:q]),
                    reads=[o_ps[b1]], writes=[o_o])
                K.dma("pool", XT[:, :, t0:t0 + q].rearrange("c p t -> p c t"), o[:, :, 0:q],
                      reads=[o_o], writes=xto(t0))

    def phase_final(y_d, si, L):
        with ExitStack() as st:
            HT = sb("pf_h", [128, 8, 512], F32, st)
            o_HT = Obj()
            yo = Ring([(sb("pf_y%d" % i, [128, D], F32, st), Obj()) for i in range(4)])
            xr_r = Ring([(sb("pf_x", [128, 8, 512], F32, st), Obj()) for _ in range(2)])
            s_r = Ring([(sb("pf_s", [128, 8, 512], BF16, st), Obj()) for _ in range(2)])
            r_r = Ring([(sb("pf_r", [128, 512], F32, st), Obj()) for _ in range(2)])
            for (t0, tw) in tiles_of(L)[1:]:
                if True:
                    xr, o_x = xr_r.next()
                    s, o_s = s_r.next()
                    r, o_r = r_r.next()
                    K.dma("sp", xr[:], XT[:, :, t0:t0 + tw].rearrange("c p t -> p c t"), reads=xto(t0), writes=[o_x])
                    K.op("act", lambda e: e.activation(out=s[:], in_=xr[:], func=AF.Square), reads=[o_x], writes=[o_s])
                    b = psr.next()
                    for c in range(8):
                        K.op("pe", lambda e: e.matmul(ps[b][:], lhsT=onesb, rhs=s[:, c, :], start=(c == 0), stop=(c == 7)),
                             reads=[o_s, o_cmb], writes=[o_ps[b]])
                    rstd_from(r[:], o_r, ps[b][:], o_ps[b], 1.0 / D)
                    for c in range(8):
                        K.op("dve", lambda e: e.scalar_tensor_tensor(
                            out=HT[:, c, :], in0=xr[:, c, :], scalar=ppc(0, P_NFIN + c), in1=r[:],
                            op0=ALU.mult, op1=ALU.mult), reads=[o_x, o_r, o_pp], writes=[o_HT])
                    for j in range(4):
                        y, o_y = yo.next()
                        b0 = psr.next()
                        b1 = psr.next()
                        for c in range(8):
                            b = b0 if c < 4 else b1
                            K.op("pe", lambda e: e.matmul(ps[b][:, (c % 4) * 128:(c % 4 + 1) * 128],
                                                          lhsT=HT[:, c, j * 128:(j + 1) * 128], rhs=identf,
                                                          start=True, stop=True),
                                 reads=[o_HT, o_cm], writes=[o_ps[b]])
                        K.op("act", lambda e: e.activation(out=y[:, 0:512], in_=ps[b0][:], func=AF.Copy),
                             reads=[o_ps[b0]], writes=[o_y])
                        K.op("dve", lambda e: e.tensor_copy(out=y[:, 512:1024], in_=ps[b1][:]),
                             reads=[o_ps[b1]], writes=[o_y])
                        tt = t0 - 16 + j * 128
                        K.dma("pool", y_d[si, tt:tt + 128, :], y[:], reads=[o_y], writes=[])

    def conv3(dst, o_dst, row, o_row, l, wcol, T, acc=None, o_acc=None):
        if acc is None:
            acc, o_acc = dst, o_dst
        K.op("act", lambda e: e.activation(out=acc, in_=row[:, 0:T], func=AF.Copy, scale=ppc(l, wcol)),
             reads=[o_row, o_pp], writes=[o_acc])
        K.op("dve", lambda e: e.scalar_tensor_tensor(out=acc, in0=row[:, 1:T + 1], scalar=ppc(l, wcol + 1), in1=acc,
                                                     op0=ALU.mult, op1=ALU.add), reads=[o_row, o_pp, o_acc], writes=[o_acc])
        K.op("dve", lambda e: e.scalar_tensor_tensor(out=dst, in0=row[:, 2:T + 2], scalar=ppc(l, wcol + 2), in1=acc,
                                                     op0=ALU.mult, op1=ALU.add), reads=[o_row, o_pp, o_acc], writes=[o_dst])

    def phase1(l, L, dtS, dtA, o_dt):
        T = L + 16
        tl = tiles_of(L)
        with ExitStack() as st:
            HT = sb("p1_ht", [128, 8, T], BF16, st); o_HTt = [Obj() for _ in tl]
            wbufs = dict(wb=Ring([(sb("p1_wb", [128, 8, 256], BF16, st), Obj()) for _ in range(4)]))
            flip = [0]
            plan = []
            for c0_ in (O_FN, O_FN + 256, O_Z, O_Z + 256):
                plan.append((w_in[l, :, c0_:c0_ + 256], 8, 256))
            for g_ in range(0, 8, 2):
                plan.append((w_in[l, :, O_XBC + g_ * 128:O_XBC + g_ * 128 + 256], 8, 256))
            plan.append((w_in[l, :, O_DT:O_DT + 16], 8, 16))
            for g_ in range(0, 12, 2):
                plan.append((w_in[l, :, O_HY + g_ * 128:O_HY + g_ * 128 + 256], 8, 256))
            for j_ in range(4):
                for cc_ in (O_SC + j_ * 128, O_SC + 512 + j_ * 128, O_SC + 1024 + j_ * 128):
                    plan.append((w_in[l, :, cc_:cc_ + 128], 8, 128))
            for g_ in range(0, 32, 2):
                plan.append((w_in[l, :, O_G + g_ * 128:O_G + g_ * 128 + 256], 8, 256))
            ws = WStream(wbufs, plan, 3)
            ws.prime()
            rmsnorm_to(HT, o_HTt, l, P_NMIX, tl, keep=(st if L == 2048 else None))
            K.mark('p1_norm')
            rows = Ring([(sb("p1_row", [128, T + 2], F32, st), Obj()) for _ in range(4)])
            outs = Ring([(sb("p1_out", [128, T], BF16, st), Obj()) for _ in range(2)])
            for (rw, o_rw) in rows.items:
                K.op("pool", lambda e: e.memset(rw[:, 0:1], 0.0), writes=[o_rw])
                K.op("pool", lambda e: e.memset(rw[:, T + 1:T + 2], 0.0), writes=[o_rw])
            wi = [0]

            def next_w():
                r_ = ws.get(wi[0])
                wi[0] += 1
                return r_

            def gemm_row(wb, o_wb, c0, dst, o_dst, off, func=None):
                for ti, (t0, tw) in enumerate(tl):
                    b = psr.next()
                    for kc in range(8):
                        K.op("pe", lambda e: e.matmul(ps[b][:, 0:tw], lhsT=wb[:, kc, c0:c0 + 128],
                                                      rhs=HT[:, kc, t0:t0 + tw], start=(kc == 0), stop=(kc == 7)),
                             reads=[o_wb, o_HTt[ti]], writes=[o_ps[b]])
                    d = dst[:, off + t0:off + t0 + tw]
                    flip[0] ^= 1
                    if func is not None:
                        K.op("act", lambda e: e.activation(out=d, in_=ps[b][:, 0:tw], func=func),
                             reads=[o_ps[b]], writes=[o_dst])
                    else:
                        K.op("act", lambda e: e.activation(out=d, in_=ps[b][:, 0:tw], func=AF.Copy),
                             reads=[o_ps[b]], writes=[o_dst])

            def store(dst_d, o_d, ob, o_ob):
                K.dma("sp", dst_d[:, 0:T], ob[:, 0:T], reads=[o_ob], writes=[o_d])

            def wsrc(c0, n):
                return w_in[l, :, c0:c0 + n]

            def simple_rows(col0, nrows, pr0, func):
                for g in range(0, nrows, 2):
                    wb, o_wb = next_w()
                    for j in range(2):
                        ob, o_ob = outs.next()
                        gemm_row(wb, o_wb, j * 128, ob, o_ob, 0, func)
                        store(PR[pr0 + g + j], o_PR[pr0 + g + j], ob, o_ob)

            simple_rows(O_FN, 4, R_FN, None)
            simple_rows(O_Z, 4, R_SZ, AF.Silu)
            K.mark('p1_fnz')
            for g in range(0, 8, 2):
                wb, o_wb = next_w()
                for j in range(2):
                    c = g + j
                    rw, o_rw = rows.next()
                    tmp, o_tmp = rows.next()
                    gemm_row(wb, o_wb, j * 128, rw, o_rw, 1)
                    conv3(tmp[:, 1:T + 1], o_tmp, rw, o_rw, l, P_SCW + 3 * c, T)
                    ob, o_ob = outs.next()
                    K.op("act", lambda e: e.activation(out=ob[:, 0:T], in_=tmp[:, 1:T + 1], func=AF.Silu,
                                                       bias=ppc(l, P_SCB + c), scale=1.0),
                         reads=[o_tmp, o_pp], writes=[o_ob])
                    store(PR[R_XS + c], o_PR[R_XS + c], ob, o_ob)
            K.mark('p1_xbc')
            wb, o_wb = next_w()
            with ExitStack() as st2:
                av = sb("p1_a", [128, 16], F32, st2); o_av = Obj()
                tmpd = Ring([(sb("p1_td", [128, 16], F32, st2), Obj()) for _ in range(2)])
                K.op("act", lambda e: e.activation(out=av[:], in_=pb[:, l, 16:32], func=AF.Exp), reads=[o_pb], writes=[o_av])
                K.op("dve", lambda e: e.tensor_scalar(out=av[:], in0=av[:], scalar1=-1.0, scalar2=None, op0=ALU.mult),
                     reads=[o_av], writes=[o_av])
                for ci, (q0, q) in enumerate(chunks_of(L)):
                    b = psr.next()
                    td, o_td = tmpd.next()
                    for kc in range(8):
                        K.op("pe", lambda e: e.matmul(ps[b][0:q, 0:16], lhsT=HT[:, kc, q0:q0 + q], rhs=wb[:, kc, 0:16],
                                                      start=(kc == 0), stop=(kc == 7)),
                             reads=[o_wb, o_HTt[0 if q0 < 16 else 1 + (q0 - 16) // 512]], writes=[o_ps[b]])
                    K.op("dve", lambda e: e.tensor_tensor(out=td[0:q, :], in0=ps[b][0:q, 0:16], in1=pb[0:q, l, 0:16],
                                                          op=ALU.add), reads=[o_ps[b], o_pb], writes=[o_td])
                    K.op("act", lambda e: e.activation(out=td[0:q, :], in_=td[0:q, :], func=AF.Exp),
                         reads=[o_td], writes=[o_td])
                    K.op("act", lambda e: e.activation(out=dtS[0:q, ci, :], in_=td[0:q, :], func=AF.Ln, bias=1.0),
                         reads=[o_td], writes=[o_dt])
                    K.op("dve", lambda e: e.tensor_tensor(out=dtA[0:q, ci, :], in0=dtS[0:q, ci, :], in1=av[0:q, :],
                                                          op=ALU.mult), reads=[o_dt, o_av], writes=[o_dt])
                K.barrier()
            K.mark('p1_dt')
            for g in range(0, 12, 2):
                wb, o_wb = next_w()
                for j in range(2):
                    c = g + j
                    rw, o_rw = rows.next()
                    tmp, o_tmp = rows.next()
                    gemm_row(wb, o_wb, j * 128, rw, o_rw, 1)
                    ob, o_ob = outs.next()
                    conv3(ob[:, 0:T], o_ob, rw, o_rw, l, P_HCW + 3 * c, T, acc=tmp[:, 1:T + 1], o_acc=o_tmp)
                    store(PR[R_V + c], o_PR[R_V + c], ob, o_ob)
            K.mark('p1_hy')
            for j in range(4):
                rb, o_rb = rows.next()
                rc, o_rc = rows.next()
                rx, o_rx = rows.next()
                rt, o_rt = rows.next()
                for (r_, o_r_, cc) in ((rb, o_rb, O_SC + j * 128), (rc, o_rc, O_SC + 512 + j * 128),
                                       (rx, o_rx, O_SC + 1024 + j * 128)):
                    wb, o_wb = next_w()
                    gemm_row(wb, o_wb, 0, r_, o_r_, 1)
                K.op("dve", lambda e: e.tensor_tensor(out=rc[:, 1:T + 1], in0=rc[:, 1:T + 1], in1=rx[:, 1:T + 1],
                                                      op=ALU.mult), reads=[o_rc, o_rx], writes=[o_rc])
                conv3(rt[:, 1:T + 1], o_rt, rc, o_rc, l, P_CCW + 3 * j, T)
                ob, o_ob = outs.next()
                K.op("dve", lambda e: e.tensor_tensor(out=ob[:, 0:T], in0=rb[:, 1:T + 1], in1=rt[:, 1:T + 1],
                                                      op=ALU.mult), reads=[o_rb, o_rt], writes=[o_ob])
                store(BR[12 + j], o_BR[12 + j], ob, o_ob)
            K.mark('p1_sc')
            simple_rows(O_G, 32, R_G, AF.Sigmoid)
            K.barrier()

    def phase_ssd(l, L, dtS, dtA, o_dt):
        T = L + 16
        ch = chunks_of(L)
        nch = len(ch)
        with ExitStack() as st:
            xsT = sb("ss_xs", [128, 4, T], BF16, st); o_xs = Obj()
            bcT = sb("ss_bc", [128, 4, T], BF16, st); o_bc = Obj()
            Yp = sb("ss_y", [128, nch, 512], F32, st); o_Y = [Obj() for _ in ch]
            hst2 = [(sb("ss_h", [128, 512], F32, st), Obj()) for _ in range(2)]
            hbf2 = [(sb("ss_hb", [128, 512], BF16, st), Obj()) for _ in range(2)]
            K.dma("sp", xsT[:], PR[R_XS:R_XS + 4, :, 0:T].rearrange("c p t -> p c t"), reads=o_PR[R_XS:R_XS + 4], writes=[o_xs])
            K.dma("sp", bcT[:], PR[R_B:R_B + 4, :, 0:T].rearrange("c p t -> p c t"), reads=o_PR[R_B:R_B + 4], writes=[o_bc])

            def ring(name, shape, dt, n=2):
                return Ring([(sb(name, shape, dt, st), Obj()) for _ in range(n)])
            r_xdt = ring("ss_xdt", [128, 512], BF16, 3)
            r_xw = ring("ss_xw", [128, 512], BF16, 3)
            r_bt = ring("ss_bt", [128, 256], BF16, 3)
            r_cbm = ring("ss_cbm", [128, 2, 128], F32)
            r_E = ring("ss_E", [128, 8], F32, 8)
            r_wd = ring("ss_wd", [128, 8], F32)
            r_cd = ring("ss_cd", [128, 8], F32, 8)
            r_lm = ring("ss_lm", [128, 8, 128], F32)
            r_LT = ring("ss_LT", [128, 8, 128], F32)
            r_MT = ring("ss_MT", [128, 8, 128], BF16, 3)
            r_yd = ring("ss_yd", [128, 512], F32, 3)
            r_sts = ring("ss_sts", [128, 512], F32, 3)
            r_yo = ring("ss_yo", [128, 512], F32)
            r_yz = ring("ss_yz", [128, 4, 128], F32)
            r_sq = ring("ss_sq", [128, 4, 128], BF16)
            r_rs = ring("ss_rs", [128, 2, 128], F32)
            r_ob = ring("ss_ob", [128, 4, 128], BF16)
            r_sz = ring("ss_szc", [128, 4, 128], BF16)

            def mm(b, out, lhsT, rhs, reads, start=True, stop=True):
                K.op("pe", lambda e: e.matmul(out, lhsT=lhsT, rhs=rhs, start=start, stop=stop),
                     reads=reads, writes=[o_ps[b]])

            def v8(ap):
                return ap.rearrange("p (h d) -> p h d", h=8)

            for d in (0, 1):
                K.op("pool", lambda e: e.memset(hst2[d][0][:], 0.0), writes=[hst2[d][1]])
                K.op("pool", lambda e: e.memset(hbf2[d][0][:], 0.0), writes=[hbf2[d][1]])

            def stage_a(d, ci):
                tri = cm[:, CM_LE:CM_LE + 128] if d == 0 else cm[:, CM_GE:CM_GE + 128]
                strict = cm[:, CM_GT:CM_GT + 128] if d == 0 else cm[:, CM_LT:CM_LT + 128]
                q0, q = ch[ci]
                dtv = dtS[0:q, ci, d * 8:(d + 1) * 8]
                dav = dtA[0:q, ci, d * 8:(d + 1) * 8]
                xdt, o_xdt = r_xdt.next(); xw, o_xw = r_xw.next(); bt, o_bt = r_bt.next()
                cbm, o_cbm = r_cbm.next(); E, o_E = r_E.next(); wd, o_wd = r_wd.next(); cd, o_cd = r_cd.next()
                lm, o_lm = r_lm.next(); LT, o_LT = r_LT.next(); MT, o_MT = r_MT.next()
                for g in range(2):
                    K.op("pool", lambda e: e.tensor_tensor(
                        out=lm[0:q, g * 4:(g + 1) * 4, 0:q],
                        in0=strict[0:q, 0:q].unsqueeze(1).to_broadcast([q, 4, q]),
                        in1=dav[:, g * 4:(g + 1) * 4].unsqueeze(2).to_broadcast([q, 4, q]), op=ALU.mult),
                        reads=[o_cm, o_dt], writes=[o_lm])
                bx = psr.next()
                for j in range(4):
                    mm(bx, ps[bx][0:q, j * 128:(j + 1) * 128], xsT[:, j, q0:q0 + q], identb, [o_xs, o_cmb])
                ba = psr.next()
                mm(ba, ps[ba][0:q, 0:8], tri[0:q, 0:q], dav, [o_cm, o_dt])
                mm(ba, ps[ba][0:q, 8:16], strict[0:q, 0:q], dav, [o_cm, o_dt])
                mm(ba, ps[ba][:, 16:24], cm[0:q, CM_ONE:CM_ONE + 128], dav, [o_cm, o_dt])
                K.op("act", lambda e: e.activation(out=E[0:q, :], in_=ps[ba][0:q, 0:8], func=AF.Exp), reads=[o_ps[ba]], writes=[o_E])
                K.op("act", lambda e: e.activation(out=wd[0:q, :], in_=ps[ba][0:q, 8:16], func=AF.Exp), reads=[o_ps[ba]], writes=[o_wd])
                K.op("act", lambda e: e.activation(out=cd[:, :], in_=ps[ba][:, 16:24], func=AF.Exp), reads=[o_ps[ba]], writes=[o_cd])
                K.op("dve", lambda e: e.tensor_tensor(out=wd[0:q, :], in0=wd[0:q, :], in1=dtv, op=ALU.mult), reads=[o_wd, o_dt], writes=[o_wd])
                K.op("dve", lambda e: e.tensor_tensor(out=v8(xdt[0:q, :]), in0=v8(ps[bx][0:q, :]),
                                                      in1=dtv.unsqueeze(2).to_broadcast([q, 8, 64]), op=ALU.mult),
                     reads=[o_ps[bx], o_dt], writes=[o_xdt])
                K.op("dve", lambda e: e.tensor_tensor(out=v8(xw[0:q, :]), in0=v8(ps[bx][0:q, :]),
                                                      in1=wd[0:q, :].unsqueeze(2).to_broadcast([q, 8, 64]), op=ALU.mult),
                     reads=[o_ps[bx], o_wd], writes=[o_xw])
                bb = psr.next()
                for g in range(2):
                    mm(bb, ps[bb][0:q, g * 128:(g + 1) * 128], bcT[:, g, q0:q0 + q], identb, [o_bc, o_cmb])
                K.op("act", lambda e: e.activation(out=bt[0:q, :], in_=ps[bb][0:q, 0:256], func=AF.Copy), reads=[o_ps[bb]], writes=[o_bt])
                bc = psr.next()
                for g in range(2):
                    mm(bc, ps[bc][0:q, g * 128:g * 128 + q], bcT[:, g, q0:q0 + q], bcT[:, 2 + g, q0:q0 + q], [o_bc])
                for g in range(2):
                    K.op("dve", lambda e: e.tensor_tensor(out=cbm[0:q, g, 0:q], in0=ps[bc][0:q, g * 128:g * 128 + q],
                                                          in1=tri[0:q, 0:q], op=ALU.mult), reads=[o_ps[bc], o_cm], writes=[o_cbm])
                for g in range(2):
                    bD = psr.next()
                    for hh in range(4):
                        mm(bD, ps[bD][0:q, hh * 128:hh * 128 + q], lm[0:q, g * 4 + hh, 0:q], tri[0:q, 0:q], [o_lm, o_cm])
                    K.op("act", lambda e: e.activation(out=LT[0:q, g * 4:(g + 1) * 4, 0:q],
                                                       in_=ps[bD][0:q, :].rearrange("p (h s) -> p h s", h=4)[:, :, 0:q],
                                                       func=AF.Exp), reads=[o_ps[bD]], writes=[o_LT])
                    K.op("dve", lambda e: e.tensor_tensor(
                        out=MT[0:q, g * 4:(g + 1) * 4, 0:q], in0=LT[0:q, g * 4:(g + 1) * 4, 0:q],
                        in1=cbm[0:q, g, 0:q].unsqueeze(1).to_broadcast([q, 4, q]), op=ALU.mult),
                        reads=[o_LT, o_cbm], writes=[o_MT])
                return dict(d=d, ci=ci, E=E, o_E=o_E, cd=cd, o_cd=o_cd, MT=MT, o_MT=o_MT, xdt=xdt, o_xdt=o_xdt,
                            xw=xw, o_xw=o_xw, bt=bt, o_bt=o_bt)

            def stage_a2(c):
                q0, q = ch[c["ci"]]
                MT, o_MT, xdt, o_xdt, xw, o_xw, bt, o_bt = (c["MT"], c["o_MT"], c["xdt"], c["o_xdt"], c["xw"], c["o_xw"],
                                                            c["bt"], c["o_bt"])
                yd, o_yd = r_yd.next(); sts, o_sts = r_sts.next()
                by = psr.next()
                for h in range(8):
                    mm(by, ps[by][0:q, h * 64:(h + 1) * 64], MT[0:q, h, 0:q], xdt[0:q, h * 64:(h + 1) * 64], [o_MT, o_xdt])
                K.op("act", lambda e: e.activation(out=yd[0:q, :], in_=ps[by][0:q, :], func=AF.Copy), reads=[o_ps[by]], writes=[o_yd])
                bs = psr.next()
                for g in range(2):
                    mm(bs, ps[bs][:, g * 256:(g + 1) * 256], bt[0:q, g * 128:(g + 1) * 128], xw[0:q, g * 256:(g + 1) * 256], [o_bt, o_xw])
                K.op("act", lambda e: e.activation(out=sts[:, :], in_=ps[bs][:, :], func=AF.Copy), reads=[o_ps[bs]], writes=[o_sts])
                c.update(yd=yd, o_yd=o_yd, sts=sts, o_sts=o_sts)
                return c

            ywritten = [False] * nch

            def stage_b(c):
                d, ci = c["d"], c["ci"]
                q0, q = ch[ci]
                hst, o_h = hst2[d]
                hbf, o_hb = hbf2[d]
                E, o_E, cd, o_cd, yd, o_yd, sts, o_sts = c["E"], c["o_E"], c["cd"], c["o_cd"], c["yd"], c["o_yd"], c["sts"], c["o_sts"]
                first = not ywritten[ci]
                ywritten[ci] = True
                yo, o_yo = r_yo.next()
                bg = psr.next()
                for g in range(2):
                    mm(bg, ps[bg][0:q, g * 256:(g + 1) * 256], bcT[:, 2 + g, q0:q0 + q], hbf[:, g * 256:(g + 1) * 256], [o_bc, o_hb])
                K.op("dve", lambda e: e.tensor_tensor(out=v8(yo[0:q, :]), in0=v8(ps[bg][0:q, :]),
                                                      in1=E[0:q, :].unsqueeze(2).to_broadcast([q, 8, 64]), op=ALU.mult),
                     reads=[o_ps[bg], o_E], writes=[o_yo])
                if first:
                    K.op("dve", lambda e: e.tensor_tensor(out=Yp[0:q, ci, :], in0=yd[0:q, :], in1=yo[0:q, :], op=ALU.add),
                         reads=[o_yd, o_yo], writes=[o_Y[ci]])
                else:
                    K.op("pool", lambda e: e.tensor_tensor(out=yo[0:q, :], in0=yd[0:q, :], in1=yo[0:q, :], op=ALU.add),
                         reads=[o_yd, o_yo], writes=[o_yo])
                    K.op("pool", lambda e: e.tensor_tensor(out=Yp[0:q, ci, :], in0=Yp[0:q, ci, :], in1=yo[0:q, :], op=ALU.add),
                         reads=[o_Y[ci], o_yo], writes=[o_Y[ci]])
                K.op("dve", lambda e: e.tensor_tensor(out=v8(hst[:, :]), in0=v8(hst[:, :]),
                                                      in1=cd[:, :].unsqueeze(2).to_broadcast([128, 8, 64]), op=ALU.mult),
                     reads=[o_h, o_cd], writes=[o_h])
                K.op("dve", lambda e: e.tensor_tensor(out=hst[:, :], in0=hst[:, :], in1=sts[:, :], op=ALU.add),
                     reads=[o_h, o_sts], writes=[o_h])
                K.op("act", lambda e: e.activation(out=hbf[:, :], in_=hst[:, :], func=AF.Copy), reads=[o_h], writes=[o_hb])
                return None if first else ci

            def stage_c(ci):
                q0, q = ch[ci]
                if True:
                    yz, o_yz = r_yz.next(); sq, o_sq = r_sq.next(); rs, o_rs = r_rs.next(); ob, o_ob = r_ob.next()
                    btp = psr.next()
                    for j in range(4):
                        mm(btp, ps[btp][:, j * 128:j * 128 + q], Yp[0:q, ci, j * 128:(j + 1) * 128], identf[0:q, 0:q], [o_Y[ci], o_cm])
                    for j in range(4):
                        K.op("dve", lambda e: e.scalar_tensor_tensor(
                            out=yz[:, j, 0:q], in0=xsT[:, j, q0:q0 + q], scalar=ppc(l, P_SD + j),
                            in1=ps[btp][:, j * 128:j * 128 + q], op0=ALU.mult, op1=ALU.add),
                            reads=[o_xs, o_pp, o_ps[btp]], writes=[o_yz])
                    szc, o_sz = r_sz.next()
                    K.dma("sp", szc[:, :, 0:q], PR[R_SZ:R_SZ + 4, :, q0:q0 + q].rearrange("c p t -> p c t"),
                          reads=o_PR[R_SZ:R_SZ + 4], writes=[o_sz])
                    K.op("pool", lambda e: e.tensor_tensor(out=yz[:, :, 0:q], in0=yz[:, :, 0:q], in1=szc[:, :, 0:q], op=ALU.mult),
                         reads=[o_yz, o_sz], writes=[o_yz])
                    K.op("act", lambda e: e.activation(out=sq[:, :, 0:q], in_=yz[:, :, 0:q], func=AF.Square), reads=[o_yz], writes=[o_sq])
                    bn = psr.next()
                    for g in range(2):
                        for jj in range(2):
                            mm(bn, ps[bn][:, g * 128:g * 128 + q], onesb, sq[:, 2 * g + jj, 0:q], [o_cmb, o_sq],
                               start=(jj == 0), stop=(jj == 1))
                    rstd_from(rs[:, :, 0:q], o_rs, ps[bn][:, 0:256].rearrange("p (g s) -> p g s", g=2)[:, :, 0:q], o_ps[bn], 1.0 / 256)
                    for j in range(4):
                        K.op("dve", lambda e: e.scalar_tensor_tensor(
                            out=ob[:, j, 0:q], in0=yz[:, j, 0:q], scalar=ppc(l, P_SNW + j), in1=rs[:, j // 2, 0:q],
                            op0=ALU.mult, op1=ALU.mult), reads=[o_yz, o_pp, o_rs], writes=[o_ob])
                    K.dma("pool", BR[4:8, :, q0:q0 + q].rearrange("c p t -> p c t"), ob[:, :, 0:q],
                          reads=[o_ob], writes=o_BR[4:8])

            sched = []
            for i_ in range(nch):
                sched.append((0, i_))
                sched.append((1, nch - 1 - i_))
            pend = []
            pend_c = []

            def run_b():
                r_ = stage_b(pend.pop(0))
                if r_ is not None:
                    pend_c.append(r_)
                while len(pend_c) > 3:
                    stage_c(pend_c.pop(0))

            pend_a2 = []
            for (d, ci) in sched:
                pend_a2.append(stage_a(d, ci))
                if len(pend_a2) > 2:
                    pend.append(stage_a2(pend_a2.pop(0)))
                if len(pend) > 2:
                    run_b()
            while pend_a2:
                pend.append(stage_a2(pend_a2.pop(0)))
                if len(pend) > 2:
                    run_b()
            while pend:
                run_b()
            while pend_c:
                stage_c(pend_c.pop(0))
            K.barrier()

    def phase_fnet(l, L):
        T = L + 16
        ch = chunks_of(L)
        nch = len(ch)
        tb = tabs[L]
        with ExitStack() as st:
            ufT = sb("fn_u", [128, 4, T], BF16, st); o_u = Obj()
            Pcs = sb("fn_p", [128, nch, 2, 512], BF16, st); o_P = Obj()
            K.dma("sp", ufT[:], PR[R_FN:R_FN + 4, :, 0:T].rearrange("c p t -> p c t"), reads=o_PR[R_FN:R_FN + 4], writes=[o_u])
            for ci, (q0, q) in enumerate(ch):
                for gp in range(2):
                    b = psr.next()
                    for gg in range(2):
                        g = gp * 2 + gg
                        K.op("pe", lambda e: e.matmul(ps[b][0:q, gg * 256:(gg + 1) * 256], lhsT=ufT[:, g, q0:q0 + q], rhs=cs128,
                                                      start=True, stop=True), reads=[o_u, o_cmb], writes=[o_ps[b]])
                    eng = "act" if gp == 0 else "dve"
                    src = ps[b][0:q, :].rearrange("p (g s c) -> p s g c", g=2, s=2)
                    dst = Pcs[0:q, ci, :, gp * 256:(gp + 1) * 256].rearrange("p s (g c) -> p s g c", g=2)
                    if eng == "act":
                        K.op("act", lambda e: e.activation(out=dst, in_=src, func=AF.Copy), reads=[o_ps[b]], writes=[o_P])
                    else:
                        K.op("dve", lambda e: e.tensor_copy(out=dst, in_=src), reads=[o_ps[b]], writes=[o_P])
            tcr = Ring([(sb("fn_tc", [128, 512], BF16, st), Obj()) for _ in range(6)])
            tsr = Ring([(sb("fn_ts", [128, 512], BF16, st), Obj()) for _ in range(6)])
            lor = Ring([(sb("fn_lo", [128, 4, 512], BF16, st), Obj()) for _ in range(2)])
            hir = Ring([(sb("fn_hi", [128, 4, 512], BF16, st), Obj()) for _ in range(2)])
            tbr = Ring([(sb("fn_tb", [128, 4, 512], F32, st), Obj()) for _ in range(2)])
            H = T // 2
            for f0 in range(0, H + 1, 512):
                fw = min(512, H + 1 - f0)
                bA = [0, 1, 2, 3]
                bB = [4, 5, 6, 7]
                for ci, (q0, q) in enumerate(ch):
                    tc, o_tc = tcr.next()
                    ts, o_ts = tsr.next()
                    K.dma("sp", tc[0:q, 0:fw], tb["fc"][q0:q0 + q, f0:f0 + fw], writes=[o_tc])
                    K.dma("sp", ts[0:q, 0:fw], tb["fs"][q0:q0 + q, f0:f0 + fw], writes=[o_ts])
                    for g in range(4):
                        K.op("pe", lambda e: e.matmul(ps[bA[g]][:, 0:fw], lhsT=Pcs[0:q, ci, 0, g * 128:(g + 1) * 128], rhs=tc[0:q, 0:fw],
                                                      start=(ci == 0), stop=(ci == nch - 1)), reads=[o_P, o_tc], writes=[o_ps[bA[g]]])
                        K.op("pe", lambda e: e.matmul(ps[bB[g]][:, 0:fw], lhsT=Pcs[0:q, ci, 1, g * 128:(g + 1) * 128], rhs=ts[0:q, 0:fw],
                                                      start=(ci == 0), stop=(ci == nch - 1)), reads=[o_P, o_ts], writes=[o_ps[bB[g]]])
                lo, o_lo = lor.next()
                hi, o_hi = hir.next()
                tbt, o_tb = tbr.next()
                a_ = max(f0, 1)
                b_ = min(f0 + fw - 1, H - 1)
                n_ = b_ - a_ + 1
                for g in range(4):
                    K.op("act", lambda e: e.activation(out=tbt[:, g, 0:fw], in_=ps[bB[g]][:, 0:fw], func=AF.Copy),
                         reads=[o_ps[bB[g]]], writes=[o_tb])
                    K.op("dve", lambda e: e.tensor_tensor(out=lo[:, g, 0:fw], in0=ps[bA[g]][:, 0:fw], in1=tbt[:, g, 0:fw], op=ALU.add),
                         reads=[o_ps[bA[g]], o_tb], writes=[o_lo])
                    if n_ > 0:
                        K.op("dve", lambda e: e.tensor_tensor(out=hi[:, g, n_ - 1::-1], in0=ps[bA[g]][:, a_ - f0:b_ - f0 + 1],
                                                              in1=tbt[:, g, a_ - f0:b_ - f0 + 1], op=ALU.subtract),
                             reads=[o_ps[bA[g]], o_tb], writes=[o_hi])
                K.dma("pool", BR[0:4, :, f0:f0 + fw].rearrange("c p t -> p c t"), lo[:, :, 0:fw], reads=[o_lo], writes=o_BR[0:4])
                if n_ > 0:
                    K.dma("pool", BR[0:4, :, T - b_:T - b_ + n_].rearrange("c p t -> p c t"), hi[:, :, 0:n_], reads=[o_hi], writes=o_BR[0:4])
            K.barrier()

    TWO_PI = 2.0 * math.pi

    def sin_act(dst, o_dst, a, o_a, ki, kf_, o_k, rows, q):
        K.op("dve", lambda e: e.tensor_scalar(out=ki[0:rows, 0:q], in0=a[0:rows, 0:q], scalar1=1.0 / TWO_PI, scalar2=None, op0=ALU.mult),
             reads=[o_a], writes=[o_k])
        K.op("dve", lambda e: e.tensor_copy(out=kf_[0:rows, 0:q], in_=ki[0:rows, 0:q]), reads=[o_k], writes=[o_k])
        K.op("dve", lambda e: e.scalar_tensor_tensor(out=a[0:rows, 0:q], in0=kf_[0:rows, 0:q], scalar=-TWO_PI, in1=a[0:rows, 0:q],
                                                     op0=ALU.mult, op1=ALU.add), reads=[o_k, o_a], writes=[o_a])
        K.op("dve", lambda e: e.tensor_scalar(out=a[0:rows, 0:q], in0=a[0:rows, 0:q], scalar1=-3.14159, scalar2=3.14159,
                                              op0=ALU.max, op1=ALU.min), reads=[o_a], writes=[o_a])
        K.op("act", lambda e: e.activation(out=dst[0:rows, 0:q], in_=a[0:rows, 0:q], func=AF.Sin), reads=[o_a], writes=[o_dst])

    def load_strip(strip, o_strip, tabb, fi):
        K.dma("sp", strip[:, :, :], tabb[fi], writes=[o_strip])

    o_KF = {}

    def phase_hfilter(l, L):
        T = L + 16
        ch = chunks_of(L)
        nch = len(ch)
        fch = fchunks_of(L)
        tb = tabs[L]
        o_KF[(L, l)] = Obj()
        with ExitStack() as st:
            w1s = sb("hf_w1", [33, 64], F32, st); w2s = sb("hf_w2", [64, 64], F32, st); w3s = sb("hf_w3", [64, 2048], F32, st)
            dls = sb("hf_dl", [128, 512], F32, st); ntt = sb("hf_ntt", [128, nch], F32, st); wfs = sb("hf_wf", [128, nch], F32, st)
            o_w = Obj()
            K.dma("sp", w1s[:], hw1[l], writes=[o_w]); K.dma("sp", w2s[:], hw2[l], writes=[o_w]); K.dma("sp", w3s[:], hw3[l], writes=[o_w])
            K.dma("sp", dls[:], dl_d, writes=[o_w]); K.dma("sp", ntt[:], tb["ntt"], writes=[o_w]); K.dma("sp", wfs[:], tb["wf"], writes=[o_w])
            S = sb("hf_S", [128, nch, 512], BF16, st); o_S = Obj()
            Dd = sb("hf_D", [128, nch, 512], BF16, st); o_D = Obj()
            rl1 = sb("hf_rl1", [128, 512], F32, st); o_rl = Obj()
            h2all = sb("hf_h2all", [64, T], F32, st); o_h2c = [Obj() for _ in ch]

            def ring(name, shape, dt, n=2):
                return Ring([(sb(name, shape, dt, st), Obj()) for _ in range(n)])
            r_z = ring("hf_z", [33, 128], F32); r_a = ring("hf_a", [64, 128], F32); r_h = ring("hf_h", [64, 128], F32)
            r_ki = ring("hf_ki", [64, 128], mybir.dt.int32); r_kf = ring("hf_kf", [64, 128], F32)
            r_wn = ring("hf_wn", [128, 512], F32); r_hf = ring("hf_hf", [128, 512], F32); r_hb = ring("hf_hb", [128, 512], F32)
            r_af = ring("hf_af", [128, 512], BF16); r_ab = ring("hf_ab", [128, 512], BF16)
            r_st = ring("hf_strip", [128, nch, 128], BF16, 4)
            r_kt = ring("hf_kt", [128, 2, 512], F32)
            for o in range(2):
                for ci, (q0, q) in enumerate(ch):
                    h2, o_h2 = h2all[:, q0:q0 + q], o_h2c[ci]
                    if o == 0:
                      z, o_z = r_z.next(); a, o_a = r_a.next(); h1, o_h1 = r_h.next()
                      ki, o_k = r_ki.next(); kf_, _ = r_kf.next()
                      K.dma("sp", z[0:33, 0:q], tb["zt"][:, q0:q0 + q], writes=[o_z])
                      b = psr.next()
                      K.op("pe", lambda e: e.matmul(ps[b][0:64, 0:q], lhsT=w1s[0:33, 0:64], rhs=z[0:33, 0:q], start=True, stop=True),
                           reads=[o_w, o_z], writes=[o_ps[b]])
                      K.op("dve", lambda e: e.tensor_scalar(out=a[0:64, 0:q], in0=ps[b][0:64, 0:q], scalar1=ppc(l, P_HB1, rows=64),
                                                            scalar2=ppc(l, P_HFQ, rows=64), op0=ALU.add, op1=ALU.mult),
                           reads=[o_ps[b], o_pp], writes=[o_a])
                      sin_act(h1, o_h1, a, o_a, ki, kf_, o_k, 64, q)
                      b = psr.next()
                      K.op("pe", lambda e: e.matmul(ps[b][0:64, 0:q], lhsT=w2s[0:64, 0:64], rhs=h1[0:64, 0:q], start=True, stop=True),
                           reads=[o_w, o_h1], writes=[o_ps[b]])
                      K.op("dve", lambda e: e.tensor_scalar(out=a[0:64, 0:q], in0=ps[b][0:64, 0:q], scalar1=ppc(l, P_HB2, rows=64),
                                                            scalar2=ppc(l, P_HFQ, rows=64), op0=ALU.add, op1=ALU.mult),
                           reads=[o_ps[b], o_pp], writes=[o_a])
                      sin_act(h2, o_h2, a, o_a, ki, kf_, o_k, 64, q)
                    bf_ = psr.next(); bb_ = psr.next()
                    K.op("pe", lambda e: e.matmul(ps[bf_][0:q, :], lhsT=h2[0:64, 0:q], rhs=w3s[0:64, o * 1024:o * 1024 + 512], start=True, stop=True),
                         reads=[o_w, o_h2], writes=[o_ps[bf_]])
                    K.op("pe", lambda e: e.matmul(ps[bb_][0:q, :], lhsT=h2[0:64, 0:q], rhs=w3s[0:64, o * 1024 + 512:o * 1024 + 1024], start=True, stop=True),
                         reads=[o_w, o_h2], writes=[o_ps[bb_]])
                    wn, o_wn = r_wn.next(); hf, o_hf = r_hf.next(); hb, o_hb = r_hb.next(); af, o_af = r_af.next(); ab, o_ab = r_ab.next()
                    K.op("act", lambda e: e.activation(out=wn[0:q, :], in_=dls[0:q, :], func=AF.Exp, scale=ntt[0:q, ci:ci + 1]),
                         reads=[o_w], writes=[o_wn])
                    K.op("dve", lambda e: e.tensor_tensor(out=hf[0:q, :], in0=ps[bf_][0:q, :], in1=wn[0:q, :], op=ALU.mult),
                         reads=[o_ps[bf_], o_wn], writes=[o_hf])
                    K.op("dve", lambda e: e.tensor_tensor(out=hb[0:q, :], in0=ps[bb_][0:q, :], in1=wn[0:q, :], op=ALU.mult),
                         reads=[o_ps[bb_], o_wn], writes=[o_hb])
                    if ci == 0:
                        K.op("dve", lambda e: e.memset(hb[0:1, :], 0.0), writes=[o_hb])
                    K.op("pool", lambda e: e.tensor_tensor(out=S[0:q, ci, :], in0=hf[0:q, :], in1=hb[0:q, :], op=ALU.add),
                         reads=[o_hf, o_hb], writes=[o_S])
                    K.op("pool", lambda e: e.tensor_tensor(out=Dd[0:q, ci, :], in0=hb[0:q, :], in1=hf[0:q, :], op=ALU.subtract),
                         reads=[o_hf, o_hb], writes=[o_D])
                    K.op("act", lambda e: e.activation(out=af[0:q, :], in_=hf[0:q, :], func=AF.Abs), reads=[o_hf], writes=[o_af])
                    K.op("act", lambda e: e.activation(out=ab[0:q, :], in_=hb[0:q, :], func=AF.Abs), reads=[o_hb], writes=[o_ab])
                    K.op("pe", lambda e: e.matmul(ps[7][:, :], lhsT=onesb[0:q, :], rhs=af[0:q, :], start=(ci == 0), stop=False),
                         reads=[o_cmb, o_af], writes=[o_ps[7]])
                    K.op("pe", lambda e: e.matmul(ps[7][:, :], lhsT=onesb[0:q, :], rhs=ab[0:q, :], start=False, stop=(ci == nch - 1)),
                         reads=[o_cmb, o_ab], writes=[o_ps[7]])
                K.mark('hf_mlp')
                K.op("dve", lambda e: e.reciprocal(out=rl1[:], in_=ps[7][:, :]), reads=[o_ps[7]], writes=[o_rl])
                for fi, (f0, mf) in enumerate(fch):
                    sc_, o_sc = r_st.next(); ss_, o_ss = r_st.next()
                    load_strip(sc_, o_sc, tb["hcb"], fi)
                    load_strip(ss_, o_ss, tb["hsb"], fi)
                    bre = psr.next(); bim = psr.next()
                    for ci, (q0, q) in enumerate(ch):
                        K.op("pe", lambda e: e.matmul(ps[bre][0:mf, :], lhsT=sc_[0:q, ci, 0:mf], rhs=S[0:q, ci, :], start=(ci == 0), stop=(ci == nch - 1)),
                             reads=[o_sc, o_S], writes=[o_ps[bre]])
                    for ci, (q0, q) in enumerate(ch):
                        K.op("pe", lambda e: e.matmul(ps[bim][0:mf, :], lhsT=ss_[0:q, ci, 0:mf], rhs=Dd[0:q, ci, :], start=(ci == 0), stop=(ci == nch - 1)),
                             reads=[o_ss, o_D], writes=[o_ps[bim]])
                    kt, o_kt = r_kt.next()
                    K.op("dve", lambda e: e.scalar_tensor_tensor(out=kt[0:mf, 0, :], in0=ps[bre][0:mf, :], scalar=wfs[0:mf, fi:fi + 1], in1=rl1[0:mf, :],
                                                                 op0=ALU.mult, op1=ALU.mult), reads=[o_ps[bre], o_w, o_rl], writes=[o_kt])
                    K.op("dve", lambda e: e.scalar_tensor_tensor(out=kt[0:mf, 1, :], in0=ps[bim][0:mf, :], scalar=wfs[0:mf, fi:fi + 1], in1=rl1[0:mf, :],
                                                                 op0=ALU.mult, op1=ALU.mult), reads=[o_ps[bim], o_w, o_rl], writes=[o_kt])
                    K.dma("pool", tb["kf"][l][fi, 0:mf, o, :, :], kt[0:mf, :, :], reads=[o_kt], writes=[o_KF[(L, l)]])
            K.barrier()

    def phase_hyena(l, L):
        T = L + 16
        ch = chunks_of(L)
        nch = len(ch)
        fch = fchunks_of(L)
        nfc = len(fch)
        tl = tiles_of(L)
        tb = tabs[L]
        okf = o_KF[(L, l)]
        with ExitStack() as st:
            inT = sb("hy_in", [128, 4, T], BF16, st); o_in = Obj()
            itok = sb("hy_tok", [128, nch, 512], BF16, st); o_tok = Obj()
            PQ = sb("hy_pq", [128, nfc, 2, 512], BF16, st); o_PQ = Obj()

            def ring(name, shape, dt, n=2):
                return Ring([(sb(name, shape, dt, st), Obj()) for _ in range(n)])
            r_st = ring("hy_strip", [128, nch, 128], BF16, 3)
            r_kt = ring("hy_kt", [128, 2, 512], F32)
            r_t = ring("hy_t", [128, 512], F32, 4)
            r_tc = ring("hy_tc", [128, 512], BF16, 4); r_ts = ring("hy_ts", [128, 512], BF16, 4)
            r_xt = ring("hy_xt", [128, 4, 512], BF16); r_ob = ring("hy_ob", [128, 4, 512], BF16)
            K.dma("sp", inT[:], PR[R_V:R_V + 4, :, 0:T].rearrange("c p t -> p c t"), reads=o_PR[R_V:R_V + 4], writes=[o_in])
            for o in range(2):
                for ci, (q0, q) in enumerate(ch):
                    b = psr.next()
                    for c in range(4):
                        K.op("pe", lambda e: e.matmul(ps[b][0:q, c * 128:(c + 1) * 128], lhsT=inT[:, c, q0:q0 + q], rhs=identb, start=True, stop=True),
                             reads=[o_in, o_cmb], writes=[o_ps[b]])
                    if ci % 2 == 0:
                        K.op("act", lambda e: e.activation(out=itok[0:q, ci, :], in_=ps[b][0:q, :], func=AF.Copy), reads=[o_ps[b]], writes=[o_tok])
                    else:
                        K.op("dve", lambda e: e.tensor_copy(out=itok[0:q, ci, :], in_=ps[b][0:q, :]), reads=[o_ps[b]], writes=[o_tok])
                K.mark('hy_tr')
                for fi, (f0, mf) in enumerate(fch):
                    sc_, o_sc = r_st.next(); ss_, o_ss = r_st.next()
                    load_strip(sc_, o_sc, tb["hcb"], fi)
                    load_strip(ss_, o_ss, tb["hsb"], fi)
                    kt, o_kt = r_kt.next()
                    K.dma("sp", kt[0:mf, :, :], tb["kf"][l][fi, 0:mf, o, :, :], reads=[okf], writes=[o_kt])
                    bA = psr.next(); bB = psr.next()
                    for ci, (q0, q) in enumerate(ch):
                        K.op("pe", lambda e: e.matmul(ps[bA][0:mf, :], lhsT=sc_[0:q, ci, 0:mf], rhs=itok[0:q, ci, :], start=(ci == 0), stop=(ci == nch - 1)),
                             reads=[o_sc, o_tok], writes=[o_ps[bA]])
                    for ci, (q0, q) in enumerate(ch):
                        K.op("pe", lambda e: e.matmul(ps[bB][0:mf, :], lhsT=ss_[0:q, ci, 0:mf], rhs=itok[0:q, ci, :], start=(ci == 0), stop=(ci == nch - 1)),
                             reads=[o_ss, o_tok], writes=[o_ps[bB]])
                    t1, o_t1 = r_t.next(); t2, o_t2 = r_t.next(); t3, o_t3 = r_t.next(); t4, o_t4 = r_t.next()
                    K.op("dve", lambda e: e.tensor_tensor(out=t1[0:mf, :], in0=ps[bA][0:mf, :], in1=kt[0:mf, 0, :], op=ALU.mult), reads=[o_ps[bA], o_kt], writes=[o_t1])
                    K.op("dve", lambda e: e.tensor_tensor(out=t2[0:mf, :], in0=ps[bB][0:mf, :], in1=kt[0:mf, 1, :], op=ALU.mult), reads=[o_ps[bB], o_kt], writes=[o_t2])
                    K.op("dve", lambda e: e.tensor_tensor(out=t3[0:mf, :], in0=ps[bB][0:mf, :], in1=kt[0:mf, 0, :], op=ALU.mult), reads=[o_ps[bB], o_kt], writes=[o_t3])
                    K.op("dve", lambda e: e.tensor_tensor(out=t4[0:mf, :], in0=ps[bA][0:mf, :], in1=kt[0:mf, 1, :], op=ALU.mult), reads=[o_ps[bA], o_kt], writes=[o_t4])
                    K.op("pool", lambda e: e.tensor_tensor(out=PQ[0:mf, fi, 0, :], in0=t1[0:mf, :], in1=t2[0:mf, :], op=ALU.add), reads=[o_t1, o_t2], writes=[o_PQ])
                    K.op("pool", lambda e: e.tensor_tensor(out=PQ[0:mf, fi, 1, :], in0=t3[0:mf, :], in1=t4[0:mf, :], op=ALU.subtract), reads=[o_t3, o_t4], writes=[o_PQ])
                K.mark('hy_fwd')
                for (t0, tw) in tl:
                    bs_ = [psr.next() for _ in range(4)]
                    for fi, (f0, mf) in enumerate(fch):
                        tc, o_tc = r_tc.next(); ts, o_ts = r_ts.next()
                        K.dma("sp", tc[0:mf, 0:tw], tb["hc"][f0:f0 + mf, t0:t0 + tw], writes=[o_tc])
                        K.dma("sp", ts[0:mf, 0:tw], tb["hs"][f0:f0 + mf, t0:t0 + tw], writes=[o_ts])
                        for c in range(4):
                            b = bs_[c]
                            K.op("pe", lambda e: e.matmul(ps[b][:, 0:tw], lhsT=PQ[0:mf, fi, 0, c * 128:(c + 1) * 128], rhs=tc[0:mf, 0:tw],
                                                          start=(fi == 0), stop=False), reads=[o_PQ, o_tc], writes=[o_ps[b]])
                            K.op("pe", lambda e: e.matmul(ps[b][:, 0:tw], lhsT=PQ[0:mf, fi, 1, c * 128:(c + 1) * 128], rhs=ts[0:mf, 0:tw],
                                                          start=False, stop=(fi == nfc - 1)), reads=[o_PQ, o_ts], writes=[o_ps[b]])
                    xt, o_xt = r_xt.next()
                    r0 = R_X1 if o == 0 else R_X2
                    K.dma("sp", xt[:, :, 0:tw], PR[r0:r0 + 4, :, t0:t0 + tw].rearrange("c p t -> p c t"), reads=o_PR[r0:r0 + 4], writes=[o_xt])
                    ob, o_ob = r_ob.next()
                    for c in range(4):
                        b = bs_[c]
                        tt_, o_tt = r_t.next()
                        K.op("dve", lambda e: e.scalar_tensor_tensor(out=tt_[:, 0:tw], in0=inT[:, c, t0:t0 + tw], scalar=ppc(l, P_HB + o * 4 + c),
                                                                     in1=ps[b][:, 0:tw], op0=ALU.mult, op1=ALU.add),
                             reads=[o_in, o_pp, o_ps[b]], writes=[o_tt])
                        if o == 0:
                            K.op("pool", lambda e: e.tensor_tensor(out=inT[:, c, t0:t0 + tw], in0=tt_[:, 0:tw], in1=xt[:, c, 0:tw], op=ALU.mult),
                                 reads=[o_tt, o_xt], writes=[o_in])
                        else:
                            K.op("pool", lambda e: e.tensor_tensor(out=ob[:, c, 0:tw], in0=tt_[:, 0:tw], in1=xt[:, c, 0:tw], op=ALU.mult),
                                 reads=[o_tt, o_xt], writes=[o_ob])
                    if o == 1:
                        K.dma("pool", BR[8:12, :, t0:t0 + tw].rearrange("c p t -> p c t"), ob[:, :, 0:tw], reads=[o_ob], writes=o_BR[8:12])
                K.mark('hy_inv')
            K.barrier()

    def load_w_into(dst3, o_dst, src_ap, kc, ncols, bufs=None):
        K.dma("pool", dst3, src_ap.rearrange("(k p) m -> p k m", p=128), writes=[o_dst])

    def phase_merge(l, L):
        T = L + 16
        tl = tiles_of(L)
        with ExitStack() as st:
            wbr = sb("mg_wbr", [128, 16, 1024], BF16, st); o_wbr_g = [Obj() for _ in range(4)]
            wo = sb("mg_wo", [128, 8, 1024], BF16, st); o_wo_g = [Obj() for _ in range(4)]
            stgr = None
            wbv = w_branch[l].rearrange("k r m -> (k r) m")
            for cg in range(4):
                for kg in range(2):
                    load_w_into(wbr[:, kg * 8:(kg + 1) * 8, cg * 256:(cg + 1) * 256], o_wbr_g[cg],
                                wbv[kg * 1024:(kg + 1) * 1024, cg * 256:(cg + 1) * 256], 8, 256, stgr)
            for cg in range(4):
                load_w_into(wo[:, :, cg * 256:(cg + 1) * 256], o_wo_g[cg], w_out[l, :, cg * 256:(cg + 1) * 256], 8, 256, stgr)
            brt2 = [[(sb("mg_br", [128, 4, 512], BF16, st), Obj()) for _ in range(4)] for _ in range(2)]
            gt = [(sb("mg_g", [128, 8, 512], BF16, st), Obj()) for _ in range(4)]
            r_acc = Ring([(sb("mg_acc", [128, 512], F32, st), Obj()) for _ in range(2)])
            r_tmp = Ring([(sb("mg_tmp", [128, 512], F32, st), Obj()) for _ in range(2)])
            r_mg = Ring([(sb("mg_m", [128, 8, 512], BF16, st), Obj()) for _ in range(2)])
            r_x = Ring([(sb("mg_x", [128, 8, 512], F32, st), Obj()) for _ in range(2)])
            for tix, (t0, tw) in enumerate(tl):
                brt = brt2[tix % 2]
                for k in range(4):
                    K.dma("sp", brt[k][0][:, :, 0:tw], BR[4 * k:4 * k + 4, :, t0:t0 + tw].rearrange("c p t -> p c t"),
                          reads=o_BR[4 * k:4 * k + 4], writes=[brt[k][1]])
                    K.dma("sp", gt[k][0][:, :, 0:tw], PR[R_G + 8 * k:R_G + 8 * k + 8, :, t0:t0 + tw].rearrange("c p t -> p c t"),
                          reads=o_PR[R_G + 8 * k:R_G + 8 * k + 8], writes=[gt[k][1]])
                xr, o_xr = r_x.next()
                K.dma("sp", xr[:, :, 0:tw], XT[:, :, t0:t0 + tw].rearrange("c p t -> p c t"), reads=xto(t0), writes=[o_xr])
                mg, o_mg = r_mg.next()
                for m in range(8):
                    acc, o_acc = r_acc.next()
                    for k in range(4):
                        b = psr.next()
                        for kc in range(4):
                            K.op("pe", lambda e: e.matmul(ps[b][:, 0:tw], lhsT=wbr[:, k * 4 + kc, m * 128:(m + 1) * 128],
                                                          rhs=brt[k][0][:, kc, 0:tw], start=(kc == 0), stop=(kc == 3)),
                                 reads=[o_wbr_g[m // 2], brt[k][1]], writes=[o_ps[b]])
                        if k == 0:
                            K.op("dve", lambda e: e.tensor_tensor(out=acc[:, 0:tw], in0=ps[b][:, 0:tw], in1=gt[k][0][:, m, 0:tw], op=ALU.mult),
                                 reads=[o_ps[b], gt[k][1]], writes=[o_acc])
                        else:
                            tmp, o_tmp = r_tmp.next()
                            K.op("dve", lambda e: e.tensor_tensor(out=tmp[:, 0:tw], in0=ps[b][:, 0:tw], in1=gt[k][0][:, m, 0:tw], op=ALU.mult),
                                 reads=[o_ps[b], gt[k][1]], writes=[o_tmp])
                            if k < 3:
                                K.op("dve", lambda e: e.tensor_tensor(out=acc[:, 0:tw], in0=acc[:, 0:tw], in1=tmp[:, 0:tw], op=ALU.add),
                                     reads=[o_acc, o_tmp], writes=[o_acc])
                            else:
                                K.op("dve", lambda e: e.tensor_tensor(out=mg[:, m, 0:tw], in0=acc[:, 0:tw], in1=tmp[:, 0:tw], op=ALU.add),
                                     reads=[o_acc, o_tmp], writes=[o_mg])
                for m2 in range(8):
                    b = psr.next()
                    for kc in range(8):
                        K.op("pe", lambda e: e.matmul(ps[b][:, 0:tw], lhsT=wo[:, kc, m2 * 128:(m2 + 1) * 128], rhs=mg[:, kc, 0:tw],
                                                      start=(kc == 0), stop=(kc == 7)), reads=[o_wo_g[m2 // 2], o_mg], writes=[o_ps[b]])
                    K.op("dve", lambda e: e.tensor_tensor(out=xr[:, m2, 0:tw], in0=xr[:, m2, 0:tw], in1=ps[b][:, 0:tw], op=ALU.add),
                         reads=[o_xr, o_ps[b]], writes=[o_xr])
                K.dma("pool", XT[:, :, t0:t0 + tw].rearrange("c p t -> p c t"), xr[:, :, 0:tw], reads=[o_xr], writes=xto(t0))
            K.barrier()

    def phase_ffn(l, L):
        T = L + 16
        tl = tiles_of(L)
        with ExitStack() as st:
            HT = sb("ff_ht", [128, 8, T], BF16, st); o_HTt = [Obj() for _ in tl]
            wbufs = dict(wb=Ring([(sb("ff_wb", [128, 8, 256], BF16, st), Obj()) for _ in range(6)]))
            plan = []
            for jp_ in range(11):
                plan.append((w_up[l, :, jp_ * 256:(jp_ + 1) * 256], 8, 256))
                plan.append((w_up[l, :, DFF + jp_ * 256:DFF + (jp_ + 1) * 256], 8, 256))
            ws = WStream(wbufs, plan, 4)
            ws.prime()
            rmsnorm_to(HT, o_HTt, l, P_NFFN, tl, keep=(st if L == 2048 else None))
            K.mark('ff_norm')
            rows = Ring([(sb("ff_row", [128, T + 2], F32, st), Obj()) for _ in range(4)])
            outs = Ring([(sb("ff_out", [128, T], BF16, st), Obj()) for _ in range(2)])
            for (rw, o_rw) in rows.items:
                K.op("pool", lambda e: e.memset(rw[:, 0:1], 0.0), writes=[o_rw])
                K.op("pool", lambda e: e.memset(rw[:, T + 1:T + 2], 0.0), writes=[o_rw])
            flip = [0]

            def gemm_row(wb, o_wb, c0, dst, o_dst):
                for ti, (t0, tw) in enumerate(tl):
                    b = psr.next()
                    for kc in range(8):
                        K.op("pe", lambda e: e.matmul(ps[b][:, 0:tw], lhsT=wb[:, kc, c0:c0 + 128], rhs=HT[:, kc, t0:t0 + tw],
                                                      start=(kc == 0), stop=(kc == 7)), reads=[o_wb, o_HTt[ti]], writes=[o_ps[b]])
                    d = dst[:, 1 + t0:1 + t0 + tw]
                    flip[0] ^= 1
                    if flip[0]:
                        K.op("act", lambda e: e.activation(out=d, in_=ps[b][:, 0:tw], func=AF.Copy), reads=[o_ps[b]], writes=[o_dst])
                    else:
                        K.op("dve", lambda e: e.tensor_copy(out=d, in_=ps[b][:, 0:tw]), reads=[o_ps[b]], writes=[o_dst])

            for jp in range(11):
                wa, o_wa = ws.get(2 * jp)
                wv, o_wv = ws.get(2 * jp + 1)
                for jj in range(2):
                    j = jp * 2 + jj
                    ra, o_ra = rows.next(); ta, o_ta = rows.next(); rv, o_rv = rows.next(); tv, o_tv = rows.next()
                    gemm_row(wa, o_wa, jj * 128, ra, o_ra)
                    gemm_row(wv, o_wv, jj * 128, rv, o_rv)
                    conv3(ta[:, 1:T + 1], o_ta, ra, o_ra, l, P_FCW + 3 * j, T)
                    conv3(tv[:, 1:T + 1], o_tv, rv, o_rv, l, P_FCW + 3 * (22 + j), T)
                    K.op("act", lambda e: e.activation(out=ta[:, 1:T + 1], in_=ta[:, 1:T + 1], func=AF.Silu), reads=[o_ta], writes=[o_ta])
                    ob, o_ob = outs.next()
                    K.op("dve", lambda e: e.tensor_tensor(out=ob[:, 0:T], in0=ta[:, 1:T + 1], in1=tv[:, 1:T + 1], op=ALU.mult),
                         reads=[o_ta, o_tv], writes=[o_ob])
                    K.dma("sp", FA[j][:, 0:T], ob[:, 0:T], reads=[o_ob], writes=[o_FA[j]])
            K.barrier()
        with ExitStack() as st:
            K.mark('ff_up')
            wd = sb("ff_wd", [128, 22, 1024], BF16, st); o_wd_g = [Obj() for _ in range(4)]
            stgr = None
            for cg in range(4):
                for (k0, kn) in ((0, 8), (8, 8), (16, 6)):
                    load_w_into(wd[:, k0:k0 + kn, cg * 256:(cg + 1) * 256], o_wd_g[cg],
                                w_down[l, k0 * 128:(k0 + kn) * 128, cg * 256:(cg + 1) * 256], kn, 256, stgr)
            r_fa = Ring([(sb("ff_fa", [128, 22, 512], BF16, st), Obj()) for _ in range(2)])
            r_x = Ring([(sb("ff_x", [128, 8, 512], F32, st), Obj()) for _ in range(2)])
            for (t0, tw) in tl:
                fa, o_fa = r_fa.next()
                xr, o_xr = r_x.next()
                K.dma("sp", fa[:, :, 0:tw], FA[:, :, t0:t0 + tw].rearrange("c p t -> p c t"), reads=o_FA, writes=[o_fa])
                K.dma("sp", xr[:, :, 0:tw], XT[:, :, t0:t0 + tw].rearrange("c p t -> p c t"), reads=xto(t0), writes=[o_xr])
                for m in range(8):
                    b = psr.next()
                    for kc in range(22):
                        K.op("pe", lambda e: e.matmul(ps[b][:, 0:tw], lhsT=wd[:, kc, m * 128:(m + 1) * 128], rhs=fa[:, kc, 0:tw],
                                                      start=(kc == 0), stop=(kc == 21)), reads=[o_wd_g[m // 2], o_fa], writes=[o_ps[b]])
                    K.op("dve", lambda e: e.tensor_tensor(out=xr[:, m, 0:tw], in0=xr[:, m, 0:tw], in1=ps[b][:, 0:tw], op=ALU.add),
                         reads=[o_xr, o_ps[b]], writes=[o_xr])
                K.dma("pool", XT[:, :, t0:t0 + tw].rearrange("c p t -> p c t"), xr[:, :, 0:tw], reads=[o_xr], writes=xto(t0))
            K.barrier()

    seqs = [(xp, yp, i, 2048) for i in range(nP)] + [(xs, ys, i, 4096) for i in range(nS)]
    for (x_d, y_d, si, L) in seqs:
        phase_input(x_d, si, L)
        K.barrier()
        nch = 1 + L // 128
        with ExitStack() as sq:
            dtS = sb("dtS", [128, nch, 16], F32, sq)
            dtA = sb("dtA", [128, nch, 16], F32, sq)
            o_dt = Obj()
            for l in range(n_layers):
                phase1(l, L, dtS, dtA, o_dt)
                K.mark('phase1')
                if stop_after == "p1":
                    break
                if stop_after != "nossd":
                    phase_ssd(l, L, dtS, dtA, o_dt)
                    K.mark('phase_ssd')
                if stop_after == "ssd":
                    break
                phase_fnet(l, L)
                K.mark('phase_fnet')
                if (L, l) not in o_KF:
                    phase_hfilter(l, L)
                    K.mark('phase_hfilter')
                phase_hyena(l, L)
                K.mark('phase_hyena')
                if stop_after in ("hy", "nossd"):
                    break
                phase_merge(l, L)
                K.mark('phase_merge')
                if stop_after == "mix":
                    break
                phase_ffn(l, L)
                K.mark('phase_ffn')
            if debug:
                dbg = dscr("dbg_dt", [128, nch, 32], F32)
                K.dma("pool", dbg[:, :, 0:16], dtS[:], reads=[o_dt])
                K.dma("pool", dbg[:, :, 16:32], dtA[:], reads=[o_dt])
            K.barrier()
        phase_final(y_d, si, L)
        K.barrier()

    K.barrier()
    es.close()
    return nc, K


_BF = ml_dtypes.bfloat16


def _const_tables(L):
    T = L + 16
    N = 2 * T
    nch = 1 + L // 128
    a = np.arange(T + 1, dtype=np.int64)
    m = (a[:, None] * a[None, :]) % N
    ang = (2.0 * np.pi / N) * m.astype(np.float64)
    hc = np.cos(ang).astype(np.float32).astype(_BF)
    hs = np.sin(ang).astype(np.float32).astype(_BF)
    t = np.arange(T, dtype=np.int64)
    m2 = (t[:, None] * t[None, :]) % T
    ang2 = (2.0 * np.pi / T) * m2.astype(np.float64)
    sc = 1.0 / math.sqrt(T)
    fc = (np.cos(ang2) * sc).astype(np.float32).astype(_BF)
    fs = (-np.sin(ang2) * sc).astype(np.float32).astype(_BF)
    tt = np.linspace(0.0, 1.0, T, dtype=np.float32)[:, None]
    bands = 16
    w = (2.0 * math.pi / T) * np.arange(T, dtype=np.float32)[:, None]
    fr = np.linspace(1e-4, bands - 1, bands, dtype=np.float32)[None, :]
    z = np.concatenate([tt, np.cos(fr * w), -np.sin(fr * w)], axis=-1).astype(np.float32)
    zt = np.ascontiguousarray(z.T)
    ntt = np.zeros((128, nch), np.float32)
    wf = np.zeros((128, nch), np.float32)
    for ci, (t0, q) in enumerate(chunks_of(L)):
        ntt[0:q, ci] = -tt[t0:t0 + q, 0]
    wfull = np.full(T + 1, 2.0 / N, np.float32)
    wfull[0] = 1.0 / N
    wfull[T] = 1.0 / N
    for ci, (f0, q) in enumerate(fchunks_of(L)):
        wf[0:q, ci] = wfull[f0:f0 + q]
    tidx = np.full((nch, 128), T + 1, np.int64)
    fidx = np.full((nch, 128), T + 1, np.int64)
    for ci, (t0, q) in enumerate(chunks_of(L)):
        tidx[ci, 0:q] = np.arange(t0, t0 + q)
    for ci, (f0, q) in enumerate(fchunks_of(L)):
        fidx[ci, 0:q] = np.arange(f0, f0 + q)

    def blocked(tab):
        pad = np.zeros((T + 2, T + 2), tab.dtype)
        pad[0:T + 1, 0:T + 1] = tab
        out = np.empty((nch, 128, nch, 128), tab.dtype)
        for fi in range(nch):
            out[fi] = pad[tidx.T[:, :, None], fidx[fi][None, None, :]]
        return out
    return dict(fc=fc, fs=fs, hc=hc, hs=hs, hcb=blocked(hc), hsb=blocked(hs), zt=zt, ntt=ntt, wf=wf)


def _const_common():
    j = np.arange(128)
    cm = np.zeros((128, NCM), np.float32)
    cm[:, CM_ID:CM_ID + 128] = np.eye(128)
    cm[:, CM_LE:CM_LE + 128] = (j[:, None] <= j[None, :])
    cm[:, CM_GE:CM_GE + 128] = (j[:, None] >= j[None, :])
    cm[:, CM_GT:CM_GT + 128] = (j[:, None] > j[None, :])
    cm[:, CM_LT:CM_LT + 128] = (j[:, None] < j[None, :])
    cm[:, CM_ONE:CM_ONE + 128] = 1.0
    cmb = np.zeros((128, 512), np.float32)
    cmb[:, 0:128] = np.eye(128)
    cmb[:, 128:256] = 1.0
    ang = 2.0 * np.pi * ((j[:, None] * j[None, :]) % 128) / 128.0
    cmb[:, 256:384] = np.cos(ang) / math.sqrt(128.0)
    cmb[:, 384:512] = np.sin(ang) / math.sqrt(128.0)
    max_decay = math.log(1e-2) / 0.3
    min_decay = math.log(1e-2) / 1.5
    deltas = np.abs(np.linspace(min_decay, max_decay, BW, dtype=np.float32))
    dl = np.ascontiguousarray(np.broadcast_to(deltas[None, :], (128, BW))).astype(np.float32)
    return dict(cm=cm, cmb=cmb.astype(_BF), dl=dl)


def _pack_params(inp):
    pp = np.zeros((2, 128, NPP), np.float32)
    pb = np.zeros((2, 128, 32), np.float32)

    def cols(v):
        return np.ascontiguousarray(np.asarray(v, np.float32).reshape(-1, 128).T)

    for l in range(2):
        pp[l, :, P_NMIX:P_NMIX + 8] = cols(inp["norm_mix"][l])
        pp[l, :, P_NFFN:P_NFFN + 8] = cols(inp["norm_ffn"][l])
        pp[l, :, P_NFIN:P_NFIN + 8] = cols(inp["norm_final"])
        w = np.asarray(inp["ssm_conv_w"][l], np.float32)
        pp[l, :, P_SCW:P_SCW + 24] = w.reshape(3, 8, 128).transpose(2, 1, 0).reshape(128, 24)
        pp[l, :, P_SCB:P_SCB + 8] = cols(inp["ssm_conv_b"][l])
        pp[l, :, P_SD:P_SD + 4] = cols(np.repeat(np.asarray(inp["ssm_d"][l], np.float32), 64))
        pp[l, :, P_SNW:P_SNW + 4] = cols(inp["ssm_norm"][l])
        w = np.asarray(inp["hyena_conv_w"][l], np.float32)
        pp[l, :, P_HCW:P_HCW + 36] = w.reshape(3, 12, 128).transpose(2, 1, 0).reshape(128, 36)
        pp[l, :, P_HB:P_HB + 8] = cols(np.asarray(inp["hyena_bias"][l], np.float32).reshape(-1))
        w = np.asarray(inp["sc_conv_w"][l], np.float32)
        pp[l, :, P_CCW:P_CCW + 12] = w.reshape(3, 4, 128).transpose(2, 1, 0).reshape(128, 12)
        w = np.asarray(inp["ffn_conv_w"][l], np.float32)
        pp[l, :, P_FCW:P_FCW + 132] = w.reshape(3, 44, 128).transpose(2, 1, 0).reshape(128, 132)
        pp[l, 0:64, P_HB1] = np.asarray(inp["hyena_b1"][l], np.float32)
        pp[l, 0:64, P_HB2] = np.asarray(inp["hyena_b2"][l], np.float32)
        pp[l, 0:64, P_HFQ] = np.asarray(inp["hyena_freq"][l], np.float32)
        pb[l, :, 0:16] = np.broadcast_to(np.asarray(inp["ssm_dt_bias"][l], np.float32).reshape(1, 16), (128, 16))
        pb[l, :, 16:32] = np.broadcast_to(np.asarray(inp["ssm_a_log"][l], np.float32).reshape(1, 16), (128, 16))
    return pp, pb


def make_in_maps(inp, nP, nS, n_cores):
    common = _const_common()
    pp, pb = _pack_params(inp)
    base = dict(common)
    base["pp"] = pp
    base["pb"] = pb
    for k in ("meta_tokens", "w_in", "w_branch", "w_out", "w_up", "w_down", "hyena_w1", "hyena_w2", "hyena_w3"):
        base[k] = np.ascontiguousarray(np.asarray(inp[k], np.float32))
    for L in sorted(set(([2048] if nP else []) + ([4096] if nS else []))):
        for k, v in _const_tables(L).items():
            base["%s%d" % (k, L)] = v
    x_prompt = np.asarray(inp["x_prompt"], np.float32)
    x_sample = np.asarray(inp["x_sample"], np.float32)
    maps = []
    for c in range(n_cores):
        m = dict(base)
        m["xp"] = np.ascontiguousarray(x_prompt[c * nP:(c + 1) * nP]) if nP else np.zeros((1, 2048, D), np.float32)
        m["xs"] = np.ascontiguousarray(x_sample[c * nS:(c + 1) * nS]) if nS else np.zeros((1, 4096, D), np.float32)
        maps.append(m)
    return maps


def kernel(x_prompt, x_sample, meta_tokens, norm_mix, w_in, ssm_conv_w, ssm_conv_b, ssm_dt_bias,
           ssm_a_log, ssm_d, ssm_norm, hyena_conv_w, hyena_w1, hyena_b1, hyena_w2, hyena_b2, hyena_w3,
           hyena_freq, hyena_bias, sc_conv_w, w_branch, w_out, norm_ffn, ffn_conv_w, w_up, w_down,
           norm_final):
    inp = dict(x_prompt=x_prompt, x_sample=x_sample, meta_tokens=meta_tokens, norm_mix=norm_mix, w_in=w_in,
               ssm_conv_w=ssm_conv_w, ssm_conv_b=ssm_conv_b, ssm_dt_bias=ssm_dt_bias, ssm_a_log=ssm_a_log,
               ssm_d=ssm_d, ssm_norm=ssm_norm, hyena_conv_w=hyena_conv_w, hyena_w1=hyena_w1, hyena_b1=hyena_b1,
               hyena_w2=hyena_w2, hyena_b2=hyena_b2, hyena_w3=hyena_w3, hyena_freq=hyena_freq,
               hyena_bias=hyena_bias, sc_conv_w=sc_conv_w, w_branch=w_branch, w_out=w_out, norm_ffn=norm_ffn,
               ffn_conv_w=ffn_conv_w, w_up=w_up, w_down=w_down, norm_final=norm_final)
    nP, nS = 4, 1
    nc, _ = build_program(nP, nS)
    maps = make_in_maps(inp, nP, nS, 8)
    res = run_bass_kernel_spmd(nc, maps, core_ids=list(range(8)))
    y_p = np.concatenate([np.asarray(r["yp"], np.float32) for r in res.results], axis=0)
    y_s = np.concatenate([np.asarray(r["ys"], np.float32) for r in res.results], axis=0)
    return (y_p, y_s)
```
